# Optimizing a Trainium2 kernel written in Bass

```python
import jax, jax.numpy as jnp
from jax import lax
import numpy as np

D_MODEL = 2048
BATCH = 4
SEQ = 8192
DEPTH = 1

MEM_LEN = 256
HGRN_WIDTH = D_MODEL // 2
HGRN_HEADS = 8
HGRN_KDIM = HGRN_WIDTH // HGRN_HEADS
HGRN_VDIM = HGRN_WIDTH // HGRN_HEADS
CONV_CH = D_MODEL - HGRN_WIDTH
CONV_GROUPS = 8
SHORT_CONV_K = 3
IN_COLS = 4 * HGRN_WIDTH + 3 * CONV_CH
CHUNK = 64
MEM_HEADS = 4
MEM_HEAD_DIM = D_MODEL // MEM_HEADS
D_FF = 5632
FFN_CONV_K = 3
EPS = 1e-6

kernel_name = "hgrn2_shortconv_hybrid_block"


def rmsnorm(x, w):
    xf = x.astype(jnp.float32)
    y = xf * lax.rsqrt(jnp.mean(xf * xf, axis=-1, keepdims=True) + EPS)
    return (y * w.astype(jnp.float32)).astype(x.dtype)


def causal_dwconv(x, w):
    k = w.shape[0]
    s = x.shape[1]
    xp = jnp.pad(x, ((0, 0), (k - 1, 0), (0, 0)))
    y = xp[:, 0:s] * w[0]
    for j in range(1, k):
        y = y + xp[:, j:j + s] * w[j]
    return y


def hgrn2_chunked(q, k, v, logf):
    bb, s, h, kd = q.shape
    vd = v.shape[-1]
    n = s // CHUNK

    def to_chunks(t):
        return t.reshape(bb, n, CHUNK, h, t.shape[-1]).transpose(1, 0, 3, 2, 4)

    qc, kc, vc, gc = to_chunks(q), to_chunks(k), to_chunks(v), to_chunks(logf)
    causal = jnp.tril(jnp.ones((CHUNK, CHUNK), dtype=bool))

    def step(state, inp):
        q_, k_, v_, g_ = inp
        b = jnp.cumsum(g_, axis=2)
        o_inter = jnp.einsum('bhtk,bhkv->bhtv', q_ * jnp.exp(b), state)
        diff = b[:, :, :, None, :] - b[:, :, None, :, :]
        decay = jnp.exp(jnp.where(causal[:, :, None], diff, -jnp.inf))
        scores = jnp.einsum('bhtk,bhtsk,bhsk->bhts', q_, decay, k_)
        o = o_inter + jnp.einsum('bhts,bhsv->bhtv', scores, v_)
        b_last = b[:, :, -1:, :]
        new_state = (jnp.exp(b_last[:, :, 0, :])[..., None] * state
                     + jnp.einsum('bhsk,bhsv->bhkv', k_ * jnp.exp(b_last - b), v_))
        return new_state, o

    s0 = jnp.zeros((bb, h, kd, vd), jnp.float32)
    _, o = lax.scan(step, s0, (qc, kc, vc, gc))
    return o.transpose(1, 0, 3, 2, 4).reshape(bb, s, h, vd)


def hybrid_mixer(h, w_in, lb, hgrn_norm_w, sconv_w, w_out):
    bb, s, _ = h.shape
    proj = h @ w_in
    W, C = HGRN_WIDTH, CONV_CH
    splits = [W, 2 * W, 3 * W, 4 * W, 4 * W + C, 4 * W + 2 * C]
    q, f_pre, i_in, g, cb, cc, ch = jnp.split(proj, splits, axis=-1)

    f = lb + (1.0 - lb) * jax.nn.sigmoid(f_pre.astype(jnp.float32))
    logf = jnp.log(f)
    k = 1.0 - f
    qf = jax.nn.silu(q.astype(jnp.float32))
    heads = lambda t, d: t.reshape(bb, s, HGRN_HEADS, d)
    o = hgrn2_chunked(heads(qf, HGRN_KDIM), heads(k, HGRN_KDIM),
                      heads(i_in.astype(jnp.float32), HGRN_VDIM), heads(logf, HGRN_KDIM))
    o = rmsnorm(o, hgrn_norm_w).reshape(bb, s, W).astype(h.dtype)
    o = o * jax.nn.silu(g)

    y = cb * causal_dwconv(cc * ch, sconv_w)

    return jnp.concatenate([o, y], axis=-1) @ w_out


def memory_cross_attention(h, mem_n, wq, wk, wv, wo):
    bb, s, _ = h.shape
    m = mem_n.shape[1]
    q = (h @ wq).reshape(bb, s, MEM_HEADS, MEM_HEAD_DIM)
    k = (mem_n @ wk).reshape(bb, m, MEM_HEADS, MEM_HEAD_DIM)
    v = (mem_n @ wv).reshape(bb, m, MEM_HEADS, MEM_HEAD_DIM)
    sc = jnp.einsum('bqhd,bmhd->bhqm', q, k).astype(jnp.float32) * (MEM_HEAD_DIM ** -0.5)
    p = jax.nn.softmax(sc, axis=-1).astype(v.dtype)
    o = jnp.einsum('bhqm,bmhd->bqhd', p, v).reshape(bb, s, D_MODEL)
    return o @ wo


def conv_ffn(h, w_gate, w_up, conv_w, conv_b, w_down):
    a = causal_dwconv(h @ w_gate, conv_w) + conv_b
    return (jax.nn.silu(a) * (h @ w_up)) @ w_down


def setup_inputs(seed: int = 0) -> dict:
    key = jax.random.key(seed)
    ks = jax.random.split(key, 24)
    f32 = jnp.float32
    nrm = lambda k, shape, scale: jax.random.normal(k, shape, f32) * scale
    gain = lambda k, shape: 1.0 + 0.02 * jax.random.normal(k, shape, f32)
    L = DEPTH
    return {
        "x": nrm(ks[0], (BATCH, SEQ, D_MODEL), 1.0),
        "mem": nrm(ks[1], (BATCH, MEM_LEN, D_MODEL), 1.0),
        "hgrn_lb": nrm(ks[2], (DEPTH + 1, HGRN_WIDTH), 0.1),
        "norm1_w": gain(ks[3], (L, D_MODEL)),
        "w_in": nrm(ks[4], (L, D_MODEL, IN_COLS), D_MODEL ** -0.5),
        "hgrn_norm_w": gain(ks[5], (L, HGRN_VDIM)),
        "sconv_w": nrm(ks[6], (L, SHORT_CONV_K, CONV_CH), SHORT_CONV_K ** -0.5),
        "w_out": nrm(ks[7], (L, D_MODEL, D_MODEL), D_MODEL ** -0.5),
        "norm2_w": gain(ks[8], (L, D_MODEL)),
        "mem_norm_w": gain(ks[9], (L, D_MODEL)),
        "wq": nrm(ks[10], (L, D_MODEL, D_MODEL), D_MODEL ** -0.5),
        "wk": nrm(ks[11], (L, D_MODEL, D_MODEL), D_MODEL ** -0.5),
        "wv": nrm(ks[12], (L, D_MODEL, D_MODEL), D_MODEL ** -0.5),
        "wo": nrm(ks[13], (L, D_MODEL, D_MODEL), D_MODEL ** -0.5),
        "norm3_w": gain(ks[14], (L, D_MODEL)),
        "w_gate": nrm(ks[15], (L, D_MODEL, D_FF), D_MODEL ** -0.5),
        "w_up": nrm(ks[16], (L, D_MODEL, D_FF), D_MODEL ** -0.5),
        "ffn_conv_w": nrm(ks[17], (L, FFN_CONV_K, D_FF), FFN_CONV_K ** -0.5),
        "ffn_conv_b": nrm(ks[18], (L, D_FF), 0.02),
        "w_down": nrm(ks[19], (L, D_FF, D_MODEL), D_FF ** -0.5),
        "final_norm_w": gain(ks[20], (D_MODEL,)),
    }


def reference(x, mem, hgrn_lb, norm1_w, w_in, hgrn_norm_w, sconv_w, w_out,
              norm2_w, mem_norm_w, wq, wk, wv, wo, norm3_w, w_gate, w_up,
              ffn_conv_w, ffn_conv_b, w_down, final_norm_w):
    lb_table = jnp.cumsum(jax.nn.softmax(hgrn_lb.astype(jnp.float32), axis=0), axis=0)
    for l in range(DEPTH):
        h = rmsnorm(x, norm1_w[l])
        x = x + hybrid_mixer(h, w_in[l], lb_table[l], hgrn_norm_w[l], sconv_w[l], w_out[l])
        h = rmsnorm(x, norm2_w[l])
        mem_n = rmsnorm(mem, mem_norm_w[l])
        x = x + memory_cross_attention(h, mem_n, wq[l], wk[l], wv[l], wo[l])
        h = rmsnorm(x, norm3_w[l])
        x = x + conv_ffn(h, w_gate[l], w_up[l], ffn_conv_w[l], ffn_conv_b[l], w_down[l])
    return rmsnorm(x, final_norm_w)
```

```python
import numpy as np
import concourse.bass as bass
import concourse.mybir as mybir
from concourse.bass_utils import run_bass_kernel_spmd

F32 = mybir.dt.float32
BF16 = mybir.dt.bfloat16
AF = mybir.ActivationFunctionType
ALU = mybir.AluOpType

_DT_SIZE = {F32: 4, BF16: 2}

D = 2048
KC = 16
TT = 512
NSUB = 4
NH = 8
DFF = 5632
FC = 44
MEM = 256
EPS = 1e-6
SB_BASE = 17408
SB_LIMIT = 224 * 1024


class Reg:
    __slots__ = ("space", "lo", "hi", "ap", "w", "rs", "ov", "name")

    def __init__(self, space, lo, hi, ap, name):
        self.space, self.lo, self.hi, self.ap, self.name = space, lo, hi, ap, name
        self.w = None
        self.rs = {}
        self.ov = [self]


class Prog:
    ENG = ("pe", "act", "dve", "pool", "sp")

    def __init__(self, nc):
        self.nc = nc
        self.stream = {e: [] for e in self.ENG}
        self.cnt = {e: 0 for e in self.ENG}
        self.waited = {e: {} for e in self.ENG}
        self.regs = {"sb": [], "ps": []}
        self.dma_cnt = {}
        self.sb_off = SB_BASE
        self.n_t = 0
        self.phase = ""
        self.pe_log = []

    def tensor_at(self, name, free_shape, dtype, lo, parts=128):
        self.n_t += 1
        n = int(np.prod(free_shape)) * _DT_SIZE[dtype]
        assert lo % 32 == 0 and lo + n <= SB_LIMIT, (name, lo, n)
        t = self.nc.alloc_sbuf_tensor_at("%s_%d" % (name, self.n_t), [parts] + list(free_shape), dtype, offset=lo)
        return t.ap()

    def alloc(self, nbytes):
        lo = (self.sb_off + 63) // 64 * 64
        self.sb_off = lo + nbytes
        assert self.sb_off <= SB_LIMIT, self.sb_off
        return lo

    def reg(self, space, lo, hi, ap, name=""):
        r = Reg(space, lo, hi, ap, name)
        if space in self.regs:
            for o in self.regs[space]:
                if o.lo < hi and lo < o.hi:
                    o.ov.append(r)
                    r.ov.append(o)
            self.regs[space].append(r)
        return r

    def buf(self, name, free_shape, dtype, lo=None):
        n = int(np.prod(free_shape)) * _DT_SIZE[dtype]
        if lo is None:
            lo = self.alloc(n)
        ap = self.tensor_at(name, free_shape, dtype, lo)
        return self.reg("sb", lo, lo + n, ap, name)

    def chunks(self, name, nch, celems, dtype, lo=None):
        cb = celems * _DT_SIZE[dtype]
        if lo is None:
            lo = self.alloc(nch * cb)
        ap = self.tensor_at(name, [nch, celems], dtype, lo)
        regs = [self.reg("sb", lo + i * cb, lo + (i + 1) * cb, ap[:, i, :], "%s%d" % (name, i)) for i in range(nch)]
        return ap, regs

    def _collect(self, eng, reads, writes):
        waits = {}
        wd = self.waited[eng]

        def need(k, v, raw):
            if k == eng and not raw:
                return
            if wd.get(k, 0) >= v:
                return
            if waits.get(k, 0) < v:
                waits[k] = v

        for r in reads:
            for x in r.ov:
                if x.w is not None:
                    need(x.w[0], x.w[1], True)
        for w in writes:
            for x in w.ov:
                if x.w is not None:
                    need(x.w[0], x.w[1], False)
                for k, v in x.rs.items():
                    need(k, v, False)
        for k, v in waits.items():
            wd[k] = v
        return list(waits.items())

    def op(self, eng, reads, writes, fn):
        waits = self._collect(eng, reads, writes)
        self.cnt[eng] += 1
        c = self.cnt[eng]
        self.stream[eng].append((waits, fn, (eng, 1), self.phase))
        for r in reads:
            if r.rs.get(eng, 0) < c:
                r.rs[eng] = c
        for w in writes:
            w.w = (eng, c)
            w.rs = {}
        return (eng, c)

    def dma(self, q, key, reads, writes, fn):
        waits = self._collect(q, reads, writes)
        prev = self.dma_cnt.get(key, 0)
        if prev > 0 and self.waited[q].get(key, 0) < prev:
            waits = [w for w in waits if w[0] != key] + [(key, prev)]
            self.waited[q][key] = prev
        self.dma_cnt[key] = prev + 16
        c = self.dma_cnt[key]
        self.stream[q].append((waits, fn, (key, 16), self.phase))
        for r in reads:
            if r.rs.get(key, 0) < c:
                r.rs[key] = c
        for w in writes:
            w.w = (key, c)
            w.rs = {}
        return (key, c)

    def wait_all(self, eng, toks):
        waits = []
        for k, v in toks:
            if self.waited[eng].get(k, 0) < v:
                waits.append((k, v))
                self.waited[eng][k] = v
        self.stream[eng].append((waits, None, None, "end"))

    def emit(self):
        nc = self.nc
        keys = set()
        for e in self.ENG:
            for waits, fn, inc, _ph in self.stream[e]:
                for k, _ in waits:
                    keys.add(k)
                if inc is not None:
                    keys.add(inc[0])
        sems = {k: nc.alloc_semaphore("s_" + str(k)) for k in sorted(keys, key=str)}
        engobj = {"pe": "tensor", "act": "scalar", "dve": "vector", "pool": "gpsimd", "sp": "sync"}
        with nc.Block() as block:
            def mk(e):
                def body(eng):
                    cnt = [0]
                    if e == "pe":
                        class _C:
                            def __getattr__(s_, name):
                                f = getattr(eng, name)
                                if name in ("matmul", "transpose"):
                                    def g(*a, **k):
                                        cnt[0] += 1
                                        return f(*a, **k)
                                    return g
                                return f
                        engw = _C()
                    else:
                        engw = eng
                    for waits, fn, inc, ph in self.stream[e]:
                        for k, v in waits:
                            eng.wait_ge(sems[k], v)
                        if fn is not None:
                            c0 = cnt[0]
                            ins = fn(engw)
                            ins.then_inc(sems[inc[0]], inc[1])
                            if e == "pe":
                                self.pe_log.append((ph, cnt[0] - c0))
                return body
            for e in self.ENG:
                if self.stream[e]:
                    getattr(block, engobj[e])(mk(e))


CP_NW = 0
CP_LB = 80
CP_HNW = 96
CP_SCW = 97
CP_FCW = 121
CP_FCB = 253
CP_FLAG = 297
CP_EPS = 298
NCP = 300


def _fm(v, nch):
    return np.ascontiguousarray(np.asarray(v, np.float32).reshape(nch, 128).T)


def pack_consts(inp, flag):
    cp = np.zeros((128, NCP), np.float32)
    for i, n in enumerate(["norm1_w", "norm2_w", "norm3_w", "final_norm_w", "mem_norm_w"]):
        cp[:, CP_NW + 16 * i:CP_NW + 16 * (i + 1)] = _fm(np.asarray(inp[n]).reshape(-1), 16)
    lb = np.asarray(inp["hgrn_lb"], np.float32)
    for r in range(2):
        cp[:, CP_LB + 8 * r:CP_LB + 8 * (r + 1)] = _fm(lb[r], 8)
    cp[:, CP_HNW] = np.asarray(inp["hgrn_norm_w"], np.float32).reshape(-1)
    sc = np.asarray(inp["sconv_w"], np.float32).reshape(3, 1024)
    for j in range(3):
        cp[:, CP_SCW + 8 * j:CP_SCW + 8 * (j + 1)] = _fm(sc[j], 8)
    fw_ = np.asarray(inp["ffn_conv_w"], np.float32).reshape(3, DFF)
    for j in range(3):
        cp[:, CP_FCW + 44 * j:CP_FCW + 44 * (j + 1)] = _fm(fw_[j], 44)
    cp[:, CP_FCB:CP_FCB + 44] = _fm(np.asarray(inp["ffn_conv_b"], np.float32).reshape(-1), 44)
    cp[:, CP_FLAG] = flag
    cp[:, CP_EPS] = EPS
    return cp


def const_mats():
    cm = np.zeros((128, 2, 128), np.float32)
    cm[:, 0, :] = np.eye(128, dtype=np.float32)
    cm[:, 1, :] = np.triu(np.ones((128, 128), np.float32))
    return cm


W_SHAPES = {"w_in": (D, 7168), "w_out": (D, D), "wq": (D, D), "wk": (D, D), "wv": (D, D), "wo": (D, D),
            "w_gate": (D, DFF), "w_up": (D, DFF), "w_down": (DFF, D)}

STATE, WARM, FULL = 0, 1, 2


def tile_specs(mode, dbg=None):
    s = []
    if dbg == 0 and mode == FULL:
        return s
    if mode == STATE:
        s += [("w_in", 1024), ("w_in", 1536), ("w_in", 2048), ("w_in", 2560)]
        return s
    s += [("w_in", 0), ("w_in", 512)]
    s += [("w_in", 3072), ("w_in", 3584)]
    s += [("w_in", 1024), ("w_in", 1536)]
    s += [("w_in", 2048), ("w_in", 2560)]
    for b in range(2):
        s += [("w_in", 5120 + b * 512), ("w_in", 6144 + b * 512), ("w_in", 4096 + b * 512)]
    s += [("w_out", c * 512) for c in range(4)]
    if dbg == 1 and mode == FULL:
        return s
    s += [("wq", c * 512) for c in range(4)]
    s += [("wo", c * 512) for c in range(4)]
    if dbg == 2 and mode == FULL:
        return s
    for b in range(11):
        s.append(("w_gate", b * 512))
        if mode == FULL:
            s.append(("w_up", b * 512))
    if mode == FULL:
        for cg in range(4):
            for r in range(4):
                s.append(("w_down", r, cg))
    return s


def build_program(NPRE, NOWN, DBG=None):
    nc = bass.Bass("TRN2", target_bir_lowering=False)
    p = Prog(nc)

    xo_d = nc.dram_tensor("xo", [NOWN * TT, D], F32, kind="ExternalInput").ap()
    xp_d = nc.dram_tensor("xp", [max(NPRE, 1) * TT, D], F32, kind="ExternalInput").ap()
    mem_d = nc.dram_tensor("mem", [MEM, D], F32, kind="ExternalInput").ap()
    cp_d = nc.dram_tensor("cpack", [128, NCP], F32, kind="ExternalInput").ap()
    cm_d = nc.dram_tensor("cmats", [128, 2, 128], F32, kind="ExternalInput").ap()
    y_d = nc.dram_tensor("y", [NOWN * TT, D], F32, kind="ExternalOutput").ap()
    wf = {n: nc.dram_tensor(n, list(s), F32, kind="ExternalInput").ap() for n, s in W_SHAPES.items()}
    def _nblk(n_, s_):
        return 16 if n_ == "w_down" else s_[1] // 512
    wb = {n: nc.dram_tensor(n + "_bf", [_nblk(n, s), 128, (11 if n == "w_down" else 16) * 512], BF16, kind="Internal").ap()
          for n, s in W_SHAPES.items()}

    xT_ap, xT = p.chunks("xT", KC, TT, F32)
    hT_ap, hT = p.chunks("hT", KC, TT, BF16)
    cpk = p.buf("cpk", [NCP], F32)
    cmt = p.buf("cmt", [2, 128], F32)
    identb = p.buf("identb", [128], BF16)
    maskb = p.buf("maskb", [128], BF16)
    onesb = p.buf("onesb", [128], BF16)
    onesf = p.buf("onesf", [128], F32)
    lbv = p.buf("lbv", [16], F32)
    kT_ap, kT = p.chunks("kT", KC, MEM, BF16)
    vtm_ap, vtm = p.chunks("vtm", 2, D, BF16)
    S = p.buf("S", [NH, 128], F32)
    ucar_ap, ucar = p.chunks("ucar", 8, 2, F32)
    gcar_ap, gcar = p.chunks("gcar", FC, 2, F32)
    NSLOT = 3
    slots = [p.buf("wslot%d" % i, [8192], BF16) for i in range(NSLOT)]
    identf = cmt.ap[:, 0, :]

    def cpc(off, n=1):
        return cpk.ap[:, off:off + n]

    A = p.alloc(0)
    ARENA = SB_LIMIT - A

    def at(off):
        assert off % 64 == 0
        return A + off
    K1 = 1024
    SCR = []
    for i_, base_ in enumerate((0, 77 * K1)):
        SCR.append({n_: p.buf("%s%d" % (n_, i_), [TT], F32, at(base_ + k_ * 2 * K1))
                    for k_, n_ in enumerate(("qs", "omf", "lgf", "Bc", "bp"))})
    QtT_ap, QtT = p.chunks("QtT", NH, TT, BF16, at(10 * K1))
    KtT_ap, KtT = p.chunks("KtT", NH, TT, BF16, at(18 * K1))
    V_ap, V = p.chunks("V", NSUB, 1024, BF16, at(26 * K1))
    dsc = p.buf("dsc", [3, NSUB, NH], F32, at(34 * K1))
    Ktm = p.buf("Ktm", [NH, 128], BF16, at(34 * K1 + 512))
    scm = p.buf("scm", [NH, 128], BF16, at(36 * K1 + 512))
    smid = p.buf("smid", [NH, 128], BF16, at(38 * K1 + 512))
    sqo = p.buf("sqo", [NH * 128], BF16, at(40 * K1 + 512))
    rto = p.buf("rto", [NH * 128], F32, at(42 * K1 + 512))
    sgc = [p.buf("sgc%d" % i, [TT], BF16, at(46 * K1 + 512 + i * K1)) for i in range(2)]
    ccs = [p.buf("ccs%d" % i, [TT], F32, at(77 * K1 + i * 2 * K1)) for i in range(4)]
    t0s4 = [p.buf("t0s%d" % i, [TT], F32, at(i * 2 * K1)) for i in range(4)]
    ubuf = [p.buf("ubuf%d" % i, [TT + 2], F32, at(50 * K1 + 512 + i * 2112)) for i in range(2)]
    off = 57 * K1
    sqn = [p.buf("sqn%d" % i, [TT], BF16, at(off + i * K1)) for i in range(2)]
    rstdn = p.buf("rstdn", [TT], F32, at(off + 2 * K1))
    off = 61 * K1
    mix_ap, mix = p.chunks("mix", KC, TT, BF16, at(off))
    assert off + 16 * K1 <= ARENA, (off, ARENA)
    xin = [p.buf("xin%d" % i, [1024], F32, at(i * 4 * K1)) for i in range(4)]
    ost = [p.buf("ost%d" % i, [1024], F32, at(16 * K1 + i * 4 * K1)) for i in range(2)]
    oaT_ap, oaT = p.chunks("oaT", KC, TT, BF16, at(24 * K1))
    pT = [[p.buf("pT%d_%d" % (i, m), [TT], BF16, at(40 * K1 + (2 * i + m) * K1)) for m in range(2)] for i in range(2)]
    rs_ = [p.buf("rs%d" % i, [TT], F32, at(44 * K1 + i * 2 * K1)) for i in range(2)]
    hid_ap, hid = p.chunks("hid", FC, TT, BF16, at(12 * K1))
    gbuf = [p.buf("gbuf%d" % i, [TT + 2], F32, at(77 * K1 + i * 2112)) for i in range(2)]
    tb = [p.buf("tb%d" % i, [TT], F32, at(77 * K1 + 2 * 2112 + i * 2 * K1)) for i in range(2)]
    assert 77 * K1 + 2 * 2112 + 4 * K1 <= ARENA, ARENA

    ps_t = nc.alloc_psum_tensor("ps", [128, 4096], F32).ap()
    ps_tb = ps_t.bitcast(BF16)
    psr = [p.reg("ps", i * 2048, (i + 1) * 2048, ps_t[:, i * 512:(i + 1) * 512], "ps%d" % i) for i in range(8)]
    ps_ptr = [0]

    def psa(n=1):
        b = ps_ptr[0]
        if b % n:
            b += n - b % n
        if b + n > 6:
            b = 0
        ps_ptr[0] = (b + n) % 6
        return b, psr[b:b + n]

    def psf(b, n=1):
        return ps_t[:, b * 512:(b + n) * 512]

    wregs = {}

    def spec_aps(spec):
        if spec[0] == "w_down":
            _, r, cg = spec
            sl = (slice(r * 1408, (r + 1) * 1408), slice(cg * 512, (cg + 1) * 512))
            kk = 11
        else:
            n, c0 = spec
            sl = (slice(None), slice(c0, c0 + 512))
            kk = 16
        name = spec[0]
        blk = (spec[2] * 4 + spec[1]) if name == "w_down" else spec[1] // 512
        return wf[name][sl], wb[name][blk], kk

    modes = [STATE] * max(NPRE - 1, 0) + ([WARM] if NPRE > 0 else []) + [FULL] * NOWN
    NSTATE = max(NPRE - 1, 0)
    full_seq = []
    for ti_, m in enumerate(modes):
        if ti_ == NSTATE:
            full_seq += [("wk", c * 512) for c in range(4)] + [("wv", c * 512) for c in range(4)]
        full_seq += tile_specs(m, DBG)
    ci = 0
    for spec in full_seq:
        if spec in wregs:
            continue
        r = p.reg("dram", 0, 0, None, str(spec))
        wregs[spec] = r
        src, dst, kk_ = spec_aps(spec)
        srcv_ = src.rearrange("(k p) n -> p k n", p=128)
        dstv_ = dst.rearrange("p (k n) -> p k n", k=kk_)
        p.dma("pool", "cast%d" % (ci % 8), [], [r], lambda e, srcv_=srcv_, dstv_=dstv_: e.dma_start(out=dstv_, in_=srcv_))
        ci += 1

    ws = {"next_dma": 0, "next_acq": 0}

    def ws_issue():
        i = ws["next_dma"]
        if i >= len(full_seq):
            return
        spec = full_seq[i]
        _, src, kk = spec_aps(spec)
        slot = slots[i % NSLOT]
        dst = slot.ap[:, 0:kk * 512]
        p.dma("sp", "wslot%d" % (i % NSLOT), [wregs[spec]], [slot], lambda e, dst=dst, srcv=src: e.dma_start(out=dst, in_=srcv))
        ws["next_dma"] = i + 1

    def ws_acquire(spec):
        i = ws["next_acq"]
        assert full_seq[i] == spec, (i, full_seq[i], spec)
        ws["next_acq"] = i + 1
        assert ws["next_dma"] > i
        slot = slots[i % NSLOT]
        kk = 11 if spec[0] == "w_down" else 16
        return slot, slot.ap[:, 0:kk * 512].rearrange("p (k n) -> p k n", k=kk)

    def ws_release():
        ws_issue()

    p.dma("act", "ld_cp", [], [cpk], lambda e: e.dma_start(out=cpk.ap, in_=cp_d))
    p.dma("act", "ld_cm", [], [cmt], lambda e: e.dma_start(out=cmt.ap, in_=cm_d))
    for _ in range(NSLOT):
        ws_issue()
    p.op("dve", [cmt], [identb], lambda e: e.tensor_copy(identb.ap, cmt.ap[:, 0, :]))
    p.op("dve", [cmt], [maskb], lambda e: e.tensor_copy(maskb.ap, cmt.ap[:, 1, :]))
    p.op("dve", [], [onesb], lambda e: e.memset(onesb.ap, 1.0))
    p.op("dve", [], [onesf], lambda e: e.memset(onesf.ap, 1.0))
    p.op("dve", [], [S], lambda e: e.memset(S.ap, 0.0))
    p.op("dve", [], ucar, lambda e: e.memset(ucar_ap, 0.0))
    p.op("dve", [], gcar, lambda e: e.memset(gcar_ap, 0.0))
    p.op("dve", [cpk], [lbv], lambda e: e.tensor_tensor(lbv.ap[:, 0:8], cpc(CP_LB, 8), cpc(CP_LB + 8, 8), ALU.subtract))
    p.op("act", [lbv], [lbv], lambda e: e.activation(lbv.ap[:, 0:8], lbv.ap[:, 0:8], AF.Sigmoid))
    p.op("dve", [lbv], [lbv], lambda e: e.tensor_scalar(lbv.ap[:, 8:16], lbv.ap[:, 0:8], -1.0, 1.0, ALU.mult, ALU.add))

    eps_ap = cpc(CP_EPS)
    flip = {"a": 0}

    def evac_eng():
        flip["a"] ^= 1
        return "act" if flip["a"] else "dve"

    def copy_op(eng, reads, writes, out_ap, in_ap):
        if eng == "act":
            p.op("act", reads, writes, lambda e: e.copy(out_ap, in_ap))
        else:
            p.op(eng, reads, writes, lambda e: e.tensor_copy(out_ap, in_ap))

    pref = {"n": 0}

    def issue_x(src_d, row0, i):
        sub, half = divmod(i, 2)
        sl = xin[i % 4]
        srcv = src_d[row0 + sub * 128:row0 + (sub + 1) * 128, half * 1024:(half + 1) * 1024]
        p.dma("sp", "xin%d" % (i % 4), [], [sl], lambda e, sl=sl, srcv=srcv: e.dma_start(out=sl.ap, in_=srcv))

    def prefetch_x(nxt):
        if nxt is None:
            return
        for i in range(3):
            issue_x(nxt[0], nxt[1], i)
        pref["n"] = 3

    def load_x(src_d, row0, nsub):
        issued = pref["n"]
        pref["n"] = 0
        n = nsub * 2
        for i in range(n):
            while issued < n and issued < i + 4:
                issue_x(src_d, row0, issued)
                issued += 1
            sub, half = divmod(i, 2)
            sl = xin[i % 4]
            b, pr = psa(2)
            def tr(e, sl=sl, b=b):
                last = None
                for j in range(8):
                    last = e.transpose(ps_t[:, b * 512 + j * 128:b * 512 + (j + 1) * 128], sl.ap[:, j * 128:(j + 1) * 128], identf)
                return last
            p.op("pe", [sl, cmt], pr, tr)
            outv = xT_ap[:, half * 8:(half + 1) * 8, sub * 128:(sub + 1) * 128]
            inv = psf(b, 2).rearrange("p (j t) -> p j t", j=8)
            copy_op(evac_eng(), pr, xT[half * 8:(half + 1) * 8], outv, inv)

    stat_ptr = [0]

    def norm_stats_begin():
        b = 6 + stat_ptr[0] % 2
        stat_ptr[0] += 1
        return {"b": b, "pr": [psr[b]], "n": 0}

    def norm_stats_add(st, kc, ntok, defer=False, c0=0):
        sq = sqn[st["n"] % 2]
        p.op("act", [xT[kc]], [sq], lambda e: e.activation(sq.ap[:, 0:ntok], xT_ap[:, kc, c0:c0 + ntok], AF.Square))
        first = st["n"] == 0
        last = st["n"] == KC - 1
        b = st["b"]
        def mm():
            p.op("pe", [sq, onesb], st["pr"], lambda e: e.matmul(ps_t[:, b * 512:b * 512 + ntok], onesb.ap, sq.ap[:, 0:ntok], start=first, stop=last))
        st["n"] += 1
        if not defer:
            mm()
            return
        prev = st.get("pend")
        st["pend"] = mm
        if prev is not None:
            prev()

    def norm_stats_flush(st):
        prev = st.get("pend")
        if prev is not None:
            prev()
            st["pend"] = None

    def norm_apply(st, widx, ntok, out_ap, out_regs, c0=0):
        norm_stats_flush(st)
        b = st["b"]
        p.op("act", st["pr"] + [cpk], [rstdn], lambda e: e.activation(rstdn.ap[:, 0:ntok], ps_t[:, b * 512:b * 512 + ntok], AF.Ln, bias=eps_ap, scale=1.0 / D))
        p.op("act", [rstdn], [rstdn], lambda e: e.activation(rstdn.ap[:, 0:ntok], rstdn.ap[:, 0:ntok], AF.Exp, scale=-0.5))
        for kc in range(KC):
            p.op("dve", [xT[kc], rstdn, cpk], [out_regs[kc]],
                 lambda e, kc=kc: e.scalar_tensor_tensor(out_ap[:, kc, c0:c0 + ntok], xT_ap[:, kc, c0:c0 + ntok], cpc(CP_NW + 16 * widx + kc), rstdn.ap[:, 0:ntok], ALU.mult, ALU.mult))

    def norm_full(widx, ntok, out_ap, out_regs):
        st = norm_stats_begin()
        for kc in range(KC):
            norm_stats_add(st, kc, ntok)
        norm_apply(st, widx, ntok, out_ap, out_regs)

    def proj_fm(slot, sv, j, rhs_ap, rhs_regs, ntok, nk=KC, fine=False, c0=0):
        b, pr = psa(1)
        if fine:
            for kc in range(nk):
                p.op("pe", [slot, rhs_regs[kc]], pr,
                     lambda e, kc=kc: e.matmul(ps_t[:, b * 512:b * 512 + ntok], sv[:, kc, j * 128:(j + 1) * 128], rhs_ap[:, kc, c0:c0 + ntok], start=(kc == 0), stop=(kc == nk - 1)))
            return b, pr
        def mm(e):
            last = None
            for kc in range(nk):
                last = e.matmul(ps_t[:, b * 512:b * 512 + ntok], sv[:, kc, j * 128:(j + 1) * 128], rhs_ap[:, kc, c0:c0 + ntok], start=(kc == 0), stop=(kc == nk - 1))
            return last
        p.op("pe", [slot] + list(rhs_regs), pr, mm)
        return b, pr

    def hgrn_prep_head(h, slot_f, sv_f, j, full, sc, fine=False, qcs=slice(0, TT)):
        omf, lgf, Bc, bp = sc["omf"], sc["lgf"], sc["Bc"], sc["bp"]
        bf_, prf = proj_fm(slot_f, sv_f, j, hT_ap, hT, TT, fine=fine)
        p.op("act", prf, [omf], lambda e: e.activation(omf.ap, psf(bf_), AF.Exp))
        yield
        p.op("act", [omf], [omf], lambda e: e.activation(omf.ap, omf.ap, AF.Ln, bias=1.0))
        yield
        p.op("act", [omf], [omf], lambda e: e.activation(omf.ap, omf.ap, AF.Exp, scale=-1.0))
        yield
        p.op("dve", [omf, lbv], [omf], lambda e: e.tensor_scalar(omf.ap, omf.ap, lbv.ap[:, 8 + h:9 + h], None, ALU.mult))
        yield
        p.op("act", [omf], [lgf], lambda e: e.activation(lgf.ap, omf.ap, AF.Ln, bias=1.0, scale=-1.0))
        yield
        def scan(e):
            last = None
            for s_ in range(NSUB):
                sl = slice(s_ * 128, (s_ + 1) * 128)
                last = e.tensor_tensor_scan(Bc.ap[:, sl], onesf.ap, lgf.ap[:, sl], 0.0, ALU.mult, ALU.add)
            return last
        p.op("dve", [lgf, onesf], [Bc], scan)
        yield
        B3 = Bc.ap.rearrange("p (s t) -> p s t", s=NSUB)
        bp3 = bp.ap.rearrange("p (s t) -> p s t", s=NSUB)
        p.op("dve", [Bc], [bp], lambda e: e.tensor_tensor(bp3, B3, B3[:, :, 63:64].broadcast_to([128, NSUB, 128]), ALU.subtract))
        yield
        p.op("act", [Bc], [dsc], lambda e: e.activation(dsc.ap[:, 0, :, h], B3[:, :, 127], AF.Exp))
        p.op("act", [Bc], [dsc], lambda e: e.activation(dsc.ap[:, 2, :, h], B3[:, :, 63], AF.Exp))
        p.op("act", [bp], [dsc], lambda e: e.activation(dsc.ap[:, 1, :, h], bp3[:, :, 127], AF.Exp))
        yield
        if full:
            p.op("act", [bp], [lgf], lambda e: e.activation(lgf.ap, bp.ap, AF.Exp))
        p.op("act", [bp], [Bc], lambda e: e.activation(Bc.ap, bp.ap, AF.Exp, scale=-1.0))
        yield
        if full:
            p.op("dve", [QtT[h], lgf], [QtT[h]], lambda e: e.tensor_tensor(QtT_ap[:, h, qcs], QtT_ap[:, h, qcs], lgf.ap[:, qcs], ALU.mult))
        p.op("dve", [omf, Bc], [KtT[h]], lambda e: e.tensor_tensor(KtT_ap[:, h, :], omf.ap, Bc.ap, ALU.mult))
        yield

    def run_interleaved(gens):
        act_ = list(gens)
        while act_:
            for g_ in list(act_):
                try:
                    next(g_)
                except StopIteration:
                    act_.remove(g_)

    def v_proj(slot, sv, blk):
        for sub in range(NSUB):
            b, pr = psa(1)
            def mm(e, b=b, sub=sub):
                last = None
                for kc in range(KC):
                    last = e.matmul(psf(b), hT_ap[:, kc, sub * 128:(sub + 1) * 128], sv[:, kc, :], start=(kc == 0), stop=(kc == KC - 1))
                return last
            p.op("pe", [slot] + hT, pr, mm)
            copy_op(evac_eng(), pr, [V[sub]], V_ap[:, sub, blk * 512:(blk + 1) * 512], psf(b))

    def hgrn_state_tr(sub, kbuf):
        cols = slice(sub * 128, (sub + 1) * 128)
        bt, prt = psa(1)
        def trk(e):
            last = None
            for h in range(NH):
                last = e.transpose(ps_tb[:, bt * 1024 + h * 128:bt * 1024 + (h + 1) * 128], KtT_ap[:, h, cols], identb.ap)
            return last
        p.op("pe", KtT + [identb], prt, trk)
        copy_op(evac_eng(), prt, [kbuf], kbuf.ap.rearrange("p h k -> p (h k)"), ps_tb[:, bt * 1024:(bt + 1) * 1024])

    def hgrn_state_upd(sub, kbuf):
        bp_, prp = psa(2)
        def pm(e):
            last = None
            for h in range(NH):
                last = e.matmul(ps_t[:, bp_ * 512 + h * 128:bp_ * 512 + (h + 1) * 128], kbuf.ap[:, h, :], V_ap[:, sub, h * 128:(h + 1) * 128], start=True, stop=True)
            return last
        p.op("pe", [kbuf, V[sub]], prp, pm)
        p.op("dve", [S, dsc], [S], lambda e: e.tensor_tensor(S.ap, S.ap, dsc.ap[:, 0, sub, :].unsqueeze(2).broadcast_to([128, NH, 128]), ALU.mult))
        def su(e):
            last = None
            for h in range(NH):
                last = e.scalar_tensor_tensor(S.ap[:, h, :], ps_t[:, bp_ * 512 + h * 128:bp_ * 512 + (h + 1) * 128], dsc.ap[:, 1, sub, h:h + 1], S.ap[:, h, :], ALU.mult, ALU.add)
            return last
        p.op("dve", prp + [S, dsc], [S], su)

    def hgrn_subtile(sub, full):
        cols = slice(sub * 128, (sub + 1) * 128)
        bt, prt = psa(1)
        def trk(e):
            last = None
            for h in range(NH):
                last = e.transpose(ps_tb[:, bt * 1024 + h * 128:bt * 1024 + (h + 1) * 128], KtT_ap[:, h, cols], identb.ap)
            return last
        p.op("pe", KtT + [identb], prt, trk)
        copy_op(evac_eng(), prt, [Ktm], Ktm.ap.rearrange("p h k -> p (h k)"), ps_tb[:, bt * 1024:(bt + 1) * 1024])
        if full:
            p.op("dve", [S, dsc], [smid], lambda e: e.tensor_tensor(smid.ap, S.ap, dsc.ap[:, 2, sub, :].unsqueeze(2).broadcast_to([128, NH, 128]), ALU.mult))
            bs, prs = psa(2)
            def sc(e):
                last = None
                for h in range(NH):
                    last = e.matmul(ps_t[:, bs * 512 + h * 128:bs * 512 + (h + 1) * 128], KtT_ap[:, h, cols], QtT_ap[:, h, cols], start=True, stop=True)
                return last
            p.op("pe", KtT + QtT, prs, sc)
            p.op("dve", prs + [maskb], [scm], lambda e: e.tensor_tensor(scm.ap, psf(bs, 2).rearrange("p (h t) -> p h t", h=NH), maskb.ap.unsqueeze(1).broadcast_to([128, NH, 128]), ALU.mult))
            bo, pro = psa(2)
            def om(e):
                last = None
                for h in range(NH):
                    o_ = ps_t[:, bo * 512 + h * 128:bo * 512 + (h + 1) * 128]
                    e.matmul(o_, V_ap[:, sub, h * 128:(h + 1) * 128], scm.ap[:, h, :], start=True, stop=False)
                    last = e.matmul(o_, smid.ap[:, h, :], QtT_ap[:, h, cols], start=False, stop=True)
                return last
            p.op("pe", [V[sub], scm, smid] + QtT, pro, om)
        bp_, prp = psa(2)
        def pm(e):
            last = None
            for h in range(NH):
                last = e.matmul(ps_t[:, bp_ * 512 + h * 128:bp_ * 512 + (h + 1) * 128], Ktm.ap[:, h, :], V_ap[:, sub, h * 128:(h + 1) * 128], start=True, stop=True)
            return last
        p.op("pe", [Ktm, V[sub]], prp, pm)
        p.op("dve", [S, dsc], [S], lambda e: e.tensor_tensor(S.ap, S.ap, dsc.ap[:, 0, sub, :].unsqueeze(2).broadcast_to([128, NH, 128]), ALU.mult))
        def su(e):
            last = None
            for h in range(NH):
                last = e.scalar_tensor_tensor(S.ap[:, h, :], ps_t[:, bp_ * 512 + h * 128:bp_ * 512 + (h + 1) * 128], dsc.ap[:, 1, sub, h:h + 1], S.ap[:, h, :], ALU.mult, ALU.add)
            return last
        p.op("dve", prp + [S, dsc], [S], su)
        if full:
            p.op("act", pro, [sqo], lambda e: e.activation(sqo.ap, psf(bo, 2), AF.Square))
            bq, prq = psa(2)
            def ssm(e):
                e.matmul(psf(bq), onesb.ap, sqo.ap[:, 0:512], start=True, stop=True)
                return e.matmul(psf(bq + 1), onesb.ap, sqo.ap[:, 512:1024], start=True, stop=True)
            p.op("pe", [sqo, onesb], prq, ssm)
            p.op("act", prq + [cpk], [rto], lambda e: e.activation(rto.ap, psf(bq, 2), AF.Ln, bias=eps_ap, scale=1.0 / 128))
            p.op("act", [rto], [rto], lambda e: e.activation(rto.ap, rto.ap, AF.Exp, scale=-0.5))
            p.op("dve", pro + [rto], [sqo], lambda e: e.tensor_tensor(sqo.ap, psf(bo, 2), rto.ap, ALU.mult))
            p.op("dve", [sqo, cpk] + mix[0:NH], mix[0:NH],
                 lambda e: e.scalar_tensor_tensor(mix_ap[:, 0:NH, cols], sqo.ap.rearrange("p (h t) -> p h t", h=NH), cpc(CP_HNW), mix_ap[:, 0:NH, cols], ALU.mult, ALU.mult))

    out_toks = []
    ost_ptr = [0]

    tile_no = [0]

    def emit_output(out_row0):
        for sub in range(NSUB):
            for half in range(2):
                b, pr = psa(2)
                def tro(e, b=b, sub=sub, half=half):
                    last = None
                    for j in range(8):
                        last = e.transpose(ps_t[:, b * 512 + j * 128:b * 512 + (j + 1) * 128], xT_ap[:, half * 8 + j, sub * 128:(sub + 1) * 128], identf)
                    return last
                p.op("pe", xT[half * 8:(half + 1) * 8] + [cmt], pr, tro)
                os_ = ost[ost_ptr[0] % 2]
                ost_ptr[0] += 1
                copy_op(evac_eng(), pr, [os_], os_.ap, psf(b, 2))
                dstv = y_d[out_row0 + sub * 128:out_row0 + (sub + 1) * 128, half * 1024:(half + 1) * 1024]
                out_toks.append(p.dma("act", "ost%d" % ((ost_ptr[0] - 1) % 2), [os_], [], lambda e, os_=os_, dstv=dstv: e.dma_start(out=dstv, in_=os_.ap)))

    def do_tile(src_d, row0, mode, out_row0, nxt=None):
        full = mode != STATE
        c0, n = (0, TT) if mode != WARM else (TT - 128, 128)
        cs = slice(c0, c0 + n)
        tn = tile_no[0]
        tile_no[0] += 1
        p.phase = "t%d.load" % tn
        load_x(src_d, row0, NSUB)
        if DBG == 0 and mode == FULL:
            emit_output(out_row0)
            return
        p.phase = "t%d.norm1" % tn
        norm_full(0, TT, hT_ap, hT)
        p.phase = "t%d.prep" % tn
        if full:
            for hg in range(2):
                slot_q, sv_q = ws_acquire(("w_in", hg * 512))
                for j in range(4):
                    h = hg * 4 + j
                    b, pr = proj_fm(slot_q, sv_q, j, hT_ap, hT, n, fine=(h == 0), c0=c0)
                    p.op("act", pr, [QtT[h]], lambda e, b=b, h=h: e.activation(QtT_ap[:, h, cs], ps_t[:, b * 512:b * 512 + n], AF.Silu))
                ws_release()
            for hg in range(2):
                slot_g, sv_g = ws_acquire(("w_in", 3072 + hg * 512))
                for j in range(4):
                    h = hg * 4 + j
                    b, pr = proj_fm(slot_g, sv_g, j, hT_ap, hT, n, c0=c0)
                    p.op("act", pr, [mix[h]], lambda e, b=b, h=h: e.activation(mix_ap[:, h, cs], ps_t[:, b * 512:b * 512 + n], AF.Silu))
                ws_release()
        for hg in range(2):
            slot_f, sv_f = ws_acquire(("w_in", 1024 + hg * 512))
            for jj in range(0, 4, 2):
                run_interleaved([hgrn_prep_head(hg * 4 + j, slot_f, sv_f, j, full, SCR[j % 2], fine=(not full and hg == 0 and j == 0), qcs=cs)
                                 for j in (jj, jj + 1)])
            ws_release()
        p.phase = "t%d.vproj" % tn
        for blk in range(2):
            slot, sv = ws_acquire(("w_in", 2048 + blk * 512))
            v_proj(slot, sv, blk)
            ws_release()
        p.phase = "t%d.subtiles" % tn
        if mode == STATE:
            kb = [Ktm, scm]
            hgrn_state_tr(0, kb[0])
            for sub in range(NSUB):
                if sub + 1 < NSUB:
                    hgrn_state_tr(sub + 1, kb[(sub + 1) % 2])
                hgrn_state_upd(sub, kb[sub % 2])
        else:
            for sub in range(NSUB):
                hgrn_subtile(sub, mode == FULL or (mode == WARM and sub == NSUB - 1))
        if not full:
            prefetch_x(nxt)
            return
        p.phase = "t%d.sconv" % tn
        for blk in range(2):
            slot_c, sv_c = ws_acquire(("w_in", 5120 + blk * 512))
            for j in range(4):
                bc_, prc = proj_fm(slot_c, sv_c, j, hT_ap, hT, n, c0=c0)
                p.op("act", prc, [ccs[j]], lambda e, bc_=bc_, j=j: e.copy(ccs[j].ap[:, 0:n], ps_t[:, bc_ * 512:bc_ * 512 + n]))
            ws_release()
            slot_h, sv_h = ws_acquire(("w_in", 6144 + blk * 512))
            for j in range(4):
                c = blk * 4 + j
                bh, prh = proj_fm(slot_h, sv_h, j, hT_ap, hT, n, c0=c0)
                ub = ubuf[c % 2]
                t0s = t0s4[j]
                p.op("dve", [ucar[c]], [ub], lambda e, ub=ub, c=c: e.tensor_copy(ub.ap[:, 0:2], ucar_ap[:, c, :]))
                p.op("dve", prh + [ccs[j]], [ub], lambda e, ub=ub, bh=bh, j=j: e.tensor_tensor(ub.ap[:, 2:n + 2], ps_t[:, bh * 512:bh * 512 + n], ccs[j].ap[:, 0:n], ALU.mult))
                p.op("dve", [ub], [ucar[c]], lambda e, ub=ub, c=c: e.tensor_copy(ucar_ap[:, c, :], ub.ap[:, n:n + 2]))
                p.op("act", [ub, cpk], [t0s], lambda e, ub=ub, c=c, t0s=t0s: e.activation(t0s.ap[:, 0:n], ub.ap[:, 2:n + 2], AF.Identity, scale=cpc(CP_SCW + 16 + c)))
                p.op("dve", [ub, t0s, cpk], [t0s], lambda e, ub=ub, c=c, t0s=t0s: e.scalar_tensor_tensor(t0s.ap[:, 0:n], ub.ap[:, 1:n + 1], cpc(CP_SCW + 8 + c), t0s.ap[:, 0:n], ALU.mult, ALU.add))
                p.op("dve", [ub, t0s, cpk], [t0s], lambda e, ub=ub, c=c, t0s=t0s: e.scalar_tensor_tensor(t0s.ap[:, 0:n], ub.ap[:, 0:n], cpc(CP_SCW + c), t0s.ap[:, 0:n], ALU.mult, ALU.add))
            ws_release()
            slot_b, sv_b = ws_acquire(("w_in", 4096 + blk * 512))
            for j in range(4):
                c = blk * 4 + j
                bb, prb = proj_fm(slot_b, sv_b, j, hT_ap, hT, n, c0=c0)
                p.op("dve", prb + [t0s4[j]], [mix[8 + c]], lambda e, bb=bb, c=c, j=j: e.tensor_tensor(mix_ap[:, 8 + c, cs], ps_t[:, bb * 512:bb * 512 + n], t0s4[j].ap[:, 0:n], ALU.mult))
            ws_release()

        def resid_proj(wname, act_ap, act_regs, st):
            for c in range(4):
                slot, sv = ws_acquire((wname, c * 512))
                for j in range(4):
                    ch = c * 4 + j
                    b, pr = proj_fm(slot, sv, j, act_ap, act_regs, n, c0=c0)
                    p.op("dve", pr + [xT[ch]], [xT[ch]], lambda e, b=b, ch=ch: e.tensor_tensor(xT_ap[:, ch, cs], ps_t[:, b * 512:b * 512 + n], xT_ap[:, ch, cs], ALU.add))
                    norm_stats_add(st, ch, n, defer=True, c0=c0)
                ws_release()

        p.phase = "t%d.wout" % tn
        st = norm_stats_begin()
        resid_proj("w_out", mix_ap, mix, st)
        if DBG == 1 and mode == FULL:
            emit_output(out_row0)
            return
        norm_apply(st, 1, n, hT_ap, hT, c0=c0)
        p.phase = "t%d.wq" % tn
        qT_ap, qT = mix_ap, mix
        for c in range(4):
            slot, sv = ws_acquire(("wq", c * 512))
            for j in range(4):
                ch = c * 4 + j
                b, pr = proj_fm(slot, sv, j, hT_ap, hT, n, fine=(ch == 0), c0=c0)
                p.op("act", pr, [qT[ch]], lambda e, b=b, ch=ch: e.activation(qT_ap[:, ch, cs], ps_t[:, b * 512:b * 512 + n], AF.Identity, scale=float(512 ** -0.5)))
            ws_release()
        def att_scores(hd):
            pts = pT[hd % 2]
            for mc in range(2):
                b, pr = psa(1)
                def smm(e, b=b, mc=mc, hd=hd):
                    last = None
                    for dc in range(4):
                        kc = hd * 4 + dc
                        last = e.matmul(ps_t[:, b * 512:b * 512 + n], kT_ap[:, kc, mc * 128:(mc + 1) * 128], qT_ap[:, kc, cs], start=(dc == 0), stop=(dc == 3))
                    return last
                p.op("pe", kT[hd * 4:hd * 4 + 4] + qT[hd * 4:hd * 4 + 4], pr, smm)
                p.op("act", pr, [pts[mc]], lambda e, b=b, mc=mc, pts=pts: e.activation(pts[mc].ap[:, 0:n], ps_t[:, b * 512:b * 512 + n], AF.Exp))

        def att_rest(hd):
            pts = pT[hd % 2]
            b, pr = psa(1)
            def summ(e, b=b, pts=pts):
                e.matmul(ps_t[:, b * 512:b * 512 + n], onesb.ap, pts[0].ap[:, 0:n], start=True, stop=False)
                return e.matmul(ps_t[:, b * 512:b * 512 + n], onesb.ap, pts[1].ap[:, 0:n], start=False, stop=True)
            p.op("pe", [pts[0], pts[1], onesb], pr, summ)
            rsb = rs_[hd % 2]
            p.op("act", pr, [rsb], lambda e, b=b, rsb=rsb: e.activation(rsb.ap[:, 0:n], ps_t[:, b * 512:b * 512 + n], AF.Ln))
            p.op("act", [rsb], [rsb], lambda e, rsb=rsb: e.activation(rsb.ap[:, 0:n], rsb.ap[:, 0:n], AF.Exp, scale=-1.0))
            for dc in range(4):
                kc = hd * 4 + dc
                b, pr = psa(1)
                def pv(e, b=b, kc=kc, pts=pts):
                    e.matmul(ps_t[:, b * 512:b * 512 + n], vtm_ap[:, 0, kc * 128:(kc + 1) * 128], pts[0].ap[:, 0:n], start=True, stop=False)
                    return e.matmul(ps_t[:, b * 512:b * 512 + n], vtm_ap[:, 1, kc * 128:(kc + 1) * 128], pts[1].ap[:, 0:n], start=False, stop=True)
                p.op("pe", [vtm[0], vtm[1], pts[0], pts[1]], pr, pv)
                p.op("dve", pr + [rsb], [oaT[kc]], lambda e, b=b, kc=kc, rsb=rsb: e.tensor_tensor(oaT_ap[:, kc, cs], ps_t[:, b * 512:b * 512 + n], rsb.ap[:, 0:n], ALU.mult))

        p.phase = "t%d.att" % tn
        att_scores(0)
        for hd in range(4):
            if hd + 1 < 4:
                att_scores(hd + 1)
            att_rest(hd)
        p.phase = "t%d.wo" % tn
        st = norm_stats_begin()
        resid_proj("wo", oaT_ap, oaT, st)
        if DBG == 2 and mode == FULL:
            emit_output(out_row0)
            return
        norm_apply(st, 2, n, hT_ap, hT, c0=c0)
        p.phase = "t%d.ffn" % tn
        prefetch_x(nxt)
        for blk in range(11):
            slot_g, sv_g = ws_acquire(("w_gate", blk * 512))
            if mode == FULL:
                slot_u, sv_u = ws_acquire(("w_up", blk * 512))
            for j in range(4):
                c = blk * 4 + j
                bg, prg = proj_fm(slot_g, sv_g, j, hT_ap, hT, n, fine=(c == 0), c0=c0)
                gb = gbuf[c % 2]
                t = tb[c % 2]
                p.op("dve", [gcar[c]], [gb], lambda e, gb=gb, c=c: e.tensor_copy(gb.ap[:, 0:2], gcar_ap[:, c, :]))
                p.op("act", prg, [gb], lambda e, gb=gb, bg=bg: e.copy(gb.ap[:, 2:n + 2], ps_t[:, bg * 512:bg * 512 + n]))
                p.op("dve", [gb], [gcar[c]], lambda e, gb=gb, c=c: e.tensor_copy(gcar_ap[:, c, :], gb.ap[:, n:n + 2]))
                if mode != FULL:
                    continue
                p.op("act", prg + [cpk], [t], lambda e, t=t, bg=bg, c=c: e.activation(t.ap, psf(bg), AF.Identity, bias=cpc(CP_FCB + c), scale=cpc(CP_FCW + 88 + c)))
                bu, pru = proj_fm(slot_u, sv_u, j, hT_ap, hT, TT)
                p.op("dve", [gb, t, cpk], [t], lambda e, gb=gb, t=t, c=c: e.scalar_tensor_tensor(t.ap, gb.ap[:, 1:TT + 1], cpc(CP_FCW + 44 + c), t.ap, ALU.mult, ALU.add))
                p.op("dve", [gb, t, cpk], [t], lambda e, gb=gb, t=t, c=c: e.scalar_tensor_tensor(t.ap, gb.ap[:, 0:TT], cpc(CP_FCW + c), t.ap, ALU.mult, ALU.add))
                p.op("act", [t], [t], lambda e, t=t: e.activation(t.ap, t.ap, AF.Silu))
                p.op("dve", pru + [t], [hid[c]], lambda e, t=t, bu=bu, c=c: e.tensor_tensor(hid_ap[:, c, :], psf(bu), t.ap, ALU.mult))
            ws_release()
            if mode == FULL:
                ws_release()
        if mode != FULL:
            p.op("dve", gcar + [cpk], gcar, lambda e: e.tensor_scalar(gcar_ap, gcar_ap, cpc(CP_FLAG), None, ALU.mult))
            p.op("dve", ucar + [cpk], ucar, lambda e: e.tensor_scalar(ucar_ap, ucar_ap, cpc(CP_FLAG), None, ALU.mult))
            p.op("dve", [S, cpk], [S], lambda e: e.tensor_scalar(S.ap, S.ap, cpc(CP_FLAG), None, ALU.mult))
            return
        p.phase = "t%d.down" % tn
        st = norm_stats_begin()
        for cg in range(4):
            b4, pr4 = psa(4)
            for r in range(4):
                slot, sv = ws_acquire(("w_down", r, cg))
                for j in range(4):
                    def dm(e, b4=b4, r=r, sv=sv, j=j):
                        last = None
                        for kk in range(11):
                            last = e.matmul(psf(b4 + j), sv[:, kk, j * 128:(j + 1) * 128], hid_ap[:, r * 11 + kk, :], start=(r == 0 and kk == 0), stop=(r == 3 and kk == 10))
                        return last
                    p.op("pe", [slot] + hid[r * 11:(r + 1) * 11], [pr4[j]], dm)
                ws_release()
            for j in range(4):
                ch = cg * 4 + j
                p.op("dve", [pr4[j], xT[ch]], [xT[ch]], lambda e, b4=b4, j=j, ch=ch: e.tensor_tensor(xT_ap[:, ch, :], psf(b4 + j), xT_ap[:, ch, :], ALU.add))
                norm_stats_add(st, ch, TT, defer=True)
        p.phase = "t%d.out" % tn
        if DBG != 3:
            norm_apply(st, 3, TT, xT_ap, xT)
        emit_output(out_row0)

    def mem_kv():
        p.phase = "mem"
        load_x(mem_d, 0, 2)
        norm_full(4, MEM, hT_ap, hT)
        for c in range(4):
            slot, sv = ws_acquire(("wk", c * 512))
            for j in range(4):
                b, pr = proj_fm(slot, sv, j, hT_ap, hT, MEM)
                ch = c * 4 + j
                copy_op(evac_eng(), pr, [kT[ch]], kT_ap[:, ch, :], ps_t[:, b * 512:b * 512 + MEM])
            ws_release()
        for c in range(4):
            slot, sv = ws_acquire(("wv", c * 512))
            for ms in range(2):
                b, pr = psa(1)
                def mm(e, b=b, ms=ms, sv=sv):
                    last = None
                    for kc in range(KC):
                        last = e.matmul(psf(b), hT_ap[:, kc, ms * 128:(ms + 1) * 128], sv[:, kc, :], start=(kc == 0), stop=(kc == KC - 1))
                    return last
                p.op("pe", [slot] + hT, pr, mm)
                copy_op(evac_eng(), pr, [vtm[ms]], vtm_ap[:, ms, c * 512:(c + 1) * 512], psf(b))
            ws_release()


    tiles = [(xp_d, t * TT, modes[t], None) for t in range(NPRE)] + [(xo_d, t * TT, FULL, t * TT) for t in range(NOWN)]
    for ti, (src_, r0_, m_, o_) in enumerate(tiles):
        if ti == NSTATE:
            mem_kv()
        nxt = tiles[ti + 1][0:2] if ti + 1 < len(tiles) else None
        if DBG is not None or ti + 1 == NSTATE:
            nxt = None
        do_tile(src_, r0_, m_, o_, nxt)
    assert ws["next_acq"] == len(full_seq)
    last = {}
    for k, v in out_toks:
        last[k] = max(last.get(k, 0), v)
    p.wait_all("act", list(last.items()))
    p.emit()
    global _LAST_PROG
    _LAST_PROG = p
    return nc


_CACHE = {}
_LAST_PROG = None


def run_cores(inp, x, mem, NPRE, NOWN, n_split, DBG=None):
    B, S_, _ = x.shape
    key = (NPRE, NOWN, DBG)
    if key not in _CACHE:
        _CACHE[key] = build_program(NPRE, NOWN, DBG)
    nc = _CACHE[key]
    cm = const_mats()
    wts = {n: np.ascontiguousarray(np.asarray(inp[n], np.float32).reshape(W_SHAPES[n])) for n in W_SHAPES}
    in_maps = []
    L = NOWN * TT
    P_ = max(NPRE, 1) * TT
    for b in range(B):
        for h in range(n_split):
            xo = np.ascontiguousarray(x[b, h * L:(h + 1) * L])
            if h == 0:
                xp = np.zeros((P_, D), np.float32)
                flag = 0.0
            else:
                xp = np.ascontiguousarray(x[b, h * L - P_:h * L])
                flag = 1.0
            m = {"xo": xo, "xp": xp, "mem": np.ascontiguousarray(mem[b]), "cpack": pack_consts(inp, flag), "cmats": cm}
            m.update(wts)
            in_maps.append(m)
    res = run_bass_kernel_spmd(nc, in_maps, core_ids=list(range(len(in_maps))))
    out = np.empty((B, S_, D), np.float32)
    i = 0
    for b in range(B):
        for h in range(n_split):
            out[b, h * L:(h + 1) * L] = res.results[i]["y"]
            i += 1
    return out


def kernel(**inputs):
    x = np.asarray(inputs["x"], np.float32)
    mem = np.asarray(inputs["mem"], np.float32)
    return run_cores(inputs, x, mem, NPRE=8, NOWN=8, n_split=2)
```

```python
import numpy as np
import concourse.bass as bass
import concourse.mybir as mybir
from concourse.bass_utils import run_bass_kernel_spmd

F32 = mybir.dt.float32
BF16 = mybir.dt.bfloat16
AF = mybir.ActivationFunctionType
ALU = mybir.AluOpType

_DT_SIZE = {F32: 4, BF16: 2}

D = 2048
KC = 16
TT = 512
NSUB = 4
NH = 8
DFF = 5632
FC = 44
MEM = 256
EPS = 1e-6
SB_BASE = 17408
SB_LIMIT = 224 * 1024


class Reg:
    __slots__ = ("space", "lo", "hi", "ap", "w", "rs", "ov", "name")

    def __init__(self, space, lo, hi, ap, name):
        self.space, self.lo, self.hi, self.ap, self.name = space, lo, hi, ap, name
        self.w = None
        self.rs = {}
        self.ov = [self]


class Prog:
    ENG = ("pe", "act", "dve", "pool", "sp")

    def __init__(self, nc):
        self.nc = nc
        self.stream = {e: [] for e in self.ENG}
        self.cnt = {e: 0 for e in self.ENG}
        self.waited = {e: {} for e in self.ENG}
        self.regs = {"sb": [], "ps": []}
        self.dma_cnt = {}
        self.sb_off = SB_BASE
        self.n_t = 0
        self.phase = ""
        self.pe_log = []

    def tensor_at(self, name, free_shape, dtype, lo, parts=128):
        self.n_t += 1
        n = int(np.prod(free_shape)) * _DT_SIZE[dtype]
        assert lo % 32 == 0 and lo + n <= SB_LIMIT, (name, lo, n)
        t = self.nc.alloc_sbuf_tensor_at("%s_%d" % (name, self.n_t), [parts] + list(free_shape), dtype, offset=lo)
        return t.ap()

    def alloc(self, nbytes):
        lo = (self.sb_off + 63) // 64 * 64
        self.sb_off = lo + nbytes
        assert self.sb_off <= SB_LIMIT, self.sb_off
        return lo

    def reg(self, space, lo, hi, ap, name=""):
        r = Reg(space, lo, hi, ap, name)
        if space in self.regs:
            for o in self.regs[space]:
                if o.lo < hi and lo < o.hi:
                    o.ov.append(r)
                    r.ov.append(o)
            self.regs[space].append(r)
        return r

    def buf(self, name, free_shape, dtype, lo=None):
        n = int(np.prod(free_shape)) * _DT_SIZE[dtype]
        if lo is None:
            lo = self.alloc(n)
        ap = self.tensor_at(name, free_shape, dtype, lo)
        return self.reg("sb", lo, lo + n, ap, name)

    def chunks(self, name, nch, celems, dtype, lo=None):
        cb = celems * _DT_SIZE[dtype]
        if lo is None:
            lo = self.alloc(nch * cb)
        ap = self.tensor_at(name, [nch, celems], dtype, lo)
        regs = [self.reg("sb", lo + i * cb, lo + (i + 1) * cb, ap[:, i, :], "%s%d" % (name, i)) for i in range(nch)]
        return ap, regs

    def _collect(self, eng, reads, writes):
        waits = {}
        wd = self.waited[eng]

        def need(k, v, raw):
            if k == eng and not raw and eng == "pe":
                return
            if wd.get(k, 0) >= v:
                return
            if waits.get(k, 0) < v:
                waits[k] = v

        for r in reads:
            for x in r.ov:
                if x.w is not None:
                    need(x.w[0], x.w[1], True)
        for w in writes:
            for x in w.ov:
                if x.w is not None:
                    need(x.w[0], x.w[1], False)
                for k, v in x.rs.items():
                    need(k, v, False)
        for k, v in waits.items():
            wd[k] = v
        return list(waits.items())

    def op(self, eng, reads, writes, fn):
        waits = self._collect(eng, reads, writes)
        self.cnt[eng] += 1
        c = self.cnt[eng]
        self.stream[eng].append((waits, fn, (eng, 1), self.phase))
        for r in reads:
            if r.rs.get(eng, 0) < c:
                r.rs[eng] = c
        for w in writes:
            w.w = (eng, c)
            w.rs = {}
        return (eng, c)

    def dma(self, q, key, reads, writes, fn):
        waits = self._collect(q, reads, writes)
        prev = self.dma_cnt.get(key, 0)
        if prev > 0 and self.waited[q].get(key, 0) < prev:
            waits = [w for w in waits if w[0] != key] + [(key, prev)]
            self.waited[q][key] = prev
        self.dma_cnt[key] = prev + 16
        c = self.dma_cnt[key]
        self.stream[q].append((waits, fn, (key, 16), self.phase))
        for r in reads:
            if r.rs.get(key, 0) < c:
                r.rs[key] = c
        for w in writes:
            w.w = (key, c)
            w.rs = {}
        return (key, c)

    def wait_all(self, eng, toks):
        waits = []
        for k, v in toks:
            if self.waited[eng].get(k, 0) < v:
                waits.append((k, v))
                self.waited[eng][k] = v
        self.stream[eng].append((waits, None, None, "end"))

    def emit(self):
        nc = self.nc
        keys = set()
        for e in self.ENG:
            for waits, fn, inc, _ph in self.stream[e]:
                for k, _ in waits:
                    keys.add(k)
                if inc is not None:
                    keys.add(inc[0])
        sems = {k: nc.alloc_semaphore("s_" + str(k)) for k in sorted(keys, key=str)}
        engobj = {"pe": "tensor", "act": "scalar", "dve": "vector", "pool": "gpsimd", "sp": "sync"}
        with nc.Block() as block:
            def mk(e):
                def body(eng):
                    cnt = [0]
                    if e == "pe":
                        class _C:
                            def __getattr__(s_, name):
                                f = getattr(eng, name)
                                if name in ("matmul", "transpose"):
                                    def g(*a, **k):
                                        cnt[0] += 1
                                        return f(*a, **k)
                                    return g
                                return f
                        engw = _C()
                    else:
                        engw = eng
                    for waits, fn, inc, ph in self.stream[e]:
                        for k, v in waits:
                            eng.wait_ge(sems[k], v)
                        if fn is not None:
                            c0 = cnt[0]
                            ins = fn(engw)
                            ins.then_inc(sems[inc[0]], inc[1])
                            if e == "pe":
                                self.pe_log.append((ph, cnt[0] - c0))
                return body
            for e in self.ENG:
                if self.stream[e]:
                    getattr(block, engobj[e])(mk(e))


CP_NW = 0
CP_LB = 80
CP_HNW = 96
CP_SCW = 97
CP_FCW = 121
CP_FCB = 253
CP_FLAG = 297
CP_EPS = 298
NCP = 300


def _fm(v, nch):
    return np.ascontiguousarray(np.asarray(v, np.float32).reshape(nch, 128).T)


def pack_consts(inp, flag):
    cp = np.zeros((128, NCP), np.float32)
    for i, n in enumerate(["norm1_w", "norm2_w", "norm3_w", "final_norm_w", "mem_norm_w"]):
        cp[:, CP_NW + 16 * i:CP_NW + 16 * (i + 1)] = _fm(np.asarray(inp[n]).reshape(-1), 16)
    lb = np.asarray(inp["hgrn_lb"], np.float32)
    for r in range(2):
        cp[:, CP_LB + 8 * r:CP_LB + 8 * (r + 1)] = _fm(lb[r], 8)
    cp[:, CP_HNW] = np.asarray(inp["hgrn_norm_w"], np.float32).reshape(-1)
    sc = np.asarray(inp["sconv_w"], np.float32).reshape(3, 1024)
    for j in range(3):
        cp[:, CP_SCW + 8 * j:CP_SCW + 8 * (j + 1)] = _fm(sc[j], 8)
    fw_ = np.asarray(inp["ffn_conv_w"], np.float32).reshape(3, DFF)
    for j in range(3):
        cp[:, CP_FCW + 44 * j:CP_FCW + 44 * (j + 1)] = _fm(fw_[j], 44)
    cp[:, CP_FCB:CP_FCB + 44] = _fm(np.asarray(inp["ffn_conv_b"], np.float32).reshape(-1), 44)
    cp[:, CP_FLAG] = flag
    cp[:, CP_EPS] = EPS
    return cp


def const_mats():
    cm = np.zeros((128, 2, 128), np.float32)
    cm[:, 0, :] = np.eye(128, dtype=np.float32)
    cm[:, 1, :] = np.triu(np.ones((128, 128), np.float32))
    return cm


W_SHAPES = {"w_in": (D, 7168), "w_out": (D, D), "wq": (D, D), "wk": (D, D), "wv": (D, D), "wo": (D, D),
            "w_gate": (D, DFF), "w_up": (D, DFF), "w_down": (DFF, D)}

STATE, WARM, FULL = 0, 1, 2


def tile_specs(mode, dbg=None):
    s = []
    if dbg == 0 and mode == FULL:
        return s
    if mode == STATE:
        s += [("w_in", 1024), ("w_in", 1536), ("w_in", 2048), ("w_in", 2560)]
        return s
    s += [("w_in", 0), ("w_in", 512)]
    s += [("w_in", 3072), ("w_in", 3584)]
    s += [("w_in", 1024), ("w_in", 1536)]
    s += [("w_in", 2048), ("w_in", 2560)]
    for b in range(2):
        s += [("w_in", 5120 + b * 512), ("w_in", 6144 + b * 512), ("w_in", 4096 + b * 512)]
    s += [("w_out", c * 512) for c in range(4)]
    if dbg == 1 and mode == FULL:
        return s
    s += [("wq", c * 512) for c in range(4)]
    s += [("wo", c * 512) for c in range(4)]
    if dbg == 2 and mode == FULL:
        return s
    for b in range(11):
        s.append(("w_gate", b * 512))
        if mode == FULL:
            s.append(("w_up", b * 512))
    if mode == FULL:
        for cg in range(4):
            for r in range(4):
                s.append(("w_down", r, cg))
    return s


def build_program(NPRE, NOWN, DBG=None):
    nc = bass.Bass("TRN2", target_bir_lowering=False)
    p = Prog(nc)

    xo_d = nc.dram_tensor("xo", [NOWN * TT, D], F32, kind="ExternalInput").ap()
    xp_d = nc.dram_tensor("xp", [max(NPRE, 1) * TT, D], F32, kind="ExternalInput").ap()
    mem_d = nc.dram_tensor("mem", [MEM, D], F32, kind="ExternalInput").ap()
    cp_d = nc.dram_tensor("cpack", [128, NCP], F32, kind="ExternalInput").ap()
    cm_d = nc.dram_tensor("cmats", [128, 2, 128], F32, kind="ExternalInput").ap()
    y_d = nc.dram_tensor("y", [NOWN * TT, D], F32, kind="ExternalOutput").ap()
    wf = {n: nc.dram_tensor(n, list(s), F32, kind="ExternalInput").ap() for n, s in W_SHAPES.items()}
    def _nblk(n_, s_):
        return 16 if n_ == "w_down" else s_[1] // 512
    wb = {n: nc.dram_tensor(n + "_bf", [_nblk(n, s), 128, (11 if n == "w_down" else 16) * 512], BF16, kind="Internal").ap()
          for n, s in W_SHAPES.items()}

    xT_ap, xT = p.chunks("xT", KC, TT, F32)
    hT_ap, hT = p.chunks("hT", KC, TT, BF16)
    cpk = p.buf("cpk", [NCP], F32)
    cmt = p.buf("cmt", [2, 128], F32)
    identb = p.buf("identb", [128], BF16)
    maskb = p.buf("maskb", [128], BF16)
    onesb = p.buf("onesb", [128], BF16)
    onesf = p.buf("onesf", [128], F32)
    lbv = p.buf("lbv", [16], F32)
    kT_ap, kT = p.chunks("kT", KC, MEM, BF16)
    vtm_ap, vtm = p.chunks("vtm", 2, D, BF16)
    S = p.buf("S", [NH, 128], F32)
    ucar_ap, ucar = p.chunks("ucar", 8, 2, F32)
    gcar_ap, gcar = p.chunks("gcar", FC, 2, F32)
    NSLOT = 3
    slots = [p.buf("wslot%d" % i, [8192], BF16) for i in range(NSLOT)]
    identf = cmt.ap[:, 0, :]

    def cpc(off, n=1):
        return cpk.ap[:, off:off + n]

    A = p.alloc(0)
    ARENA = SB_LIMIT - A

    def at(off):
        assert off % 64 == 0
        return A + off
    K1 = 1024
    SCR = []
    for i_, base_ in enumerate((0, 77 * K1)):
        SCR.append({n_: p.buf("%s%d" % (n_, i_), [TT], F32, at(base_ + k_ * 2 * K1))
                    for k_, n_ in enumerate(("qs", "omf", "lgf", "Bc", "bp"))})
    QtT_ap, QtT = p.chunks("QtT", NH, TT, BF16, at(10 * K1))
    KtT_ap, KtT = p.chunks("KtT", NH, TT, BF16, at(18 * K1))
    V_ap, V = p.chunks("V", NSUB, 1024, BF16, at(26 * K1))
    dsc = p.buf("dsc", [3, NSUB, NH], F32, at(34 * K1))
    Ktm = p.buf("Ktm", [NH, 128], BF16, at(34 * K1 + 512))
    scm = p.buf("scm", [NH, 128], BF16, at(36 * K1 + 512))
    smid = p.buf("smid", [NH, 128], BF16, at(38 * K1 + 512))
    sqo = p.buf("sqo", [NH * 128], BF16, at(40 * K1 + 512))
    rto = p.buf("rto", [NH * 128], F32, at(42 * K1 + 512))
    sgc = [p.buf("sgc%d" % i, [TT], BF16, at(46 * K1 + 512 + i * K1)) for i in range(2)]
    ccs = [p.buf("ccs%d" % i, [TT], F32, at(77 * K1 + i * 2 * K1)) for i in range(4)]
    t0s4 = [p.buf("t0s%d" % i, [TT], F32, at(i * 2 * K1)) for i in range(4)]
    ubuf = [p.buf("ubuf%d" % i, [TT + 2], F32, at(50 * K1 + 512 + i * 2112)) for i in range(2)]
    off = 57 * K1
    sqn = [p.buf("sqn%d" % i, [TT], BF16, at(off + i * K1)) for i in range(2)]
    rstdn = p.buf("rstdn", [TT], F32, at(off + 2 * K1))
    off = 61 * K1
    mix_ap, mix = p.chunks("mix", KC, TT, BF16, at(off))
    assert off + 16 * K1 <= ARENA, (off, ARENA)
    xin = [p.buf("xin%d" % i, [1024], F32, at(i * 4 * K1)) for i in range(4)]
    ost = [p.buf("ost%d" % i, [1024], F32, at(16 * K1 + i * 4 * K1)) for i in range(2)]
    oaT_ap, oaT = p.chunks("oaT", KC, TT, BF16, at(24 * K1))
    pT = [[p.buf("pT%d_%d" % (i, m), [TT], BF16, at(40 * K1 + (2 * i + m) * K1)) for m in range(2)] for i in range(2)]
    rs_ = [p.buf("rs%d" % i, [TT], F32, at(44 * K1 + i * 2 * K1)) for i in range(2)]
    hid_ap, hid = p.chunks("hid", FC, TT, BF16, at(12 * K1))
    gbuf = [p.buf("gbuf%d" % i, [TT + 2], F32, at(77 * K1 + i * 2112)) for i in range(2)]
    tb = [p.buf("tb%d" % i, [TT], F32, at(77 * K1 + 2 * 2112 + i * 2 * K1)) for i in range(2)]
    assert 77 * K1 + 2 * 2112 + 4 * K1 <= ARENA, ARENA

    ps_t = nc.alloc_psum_tensor("ps", [128, 4096], F32).ap()
    ps_tb = ps_t.bitcast(BF16)
    psr = [p.reg("ps", i * 2048, (i + 1) * 2048, ps_t[:, i * 512:(i + 1) * 512], "ps%d" % i) for i in range(8)]
    ps_ptr = [0]

    def psa(n=1):
        b = ps_ptr[0]
        if b % n:
            b += n - b % n
        if b + n > 6:
            b = 0
        ps_ptr[0] = (b + n) % 6
        return b, psr[b:b + n]

    def psf(b, n=1):
        return ps_t[:, b * 512:(b + n) * 512]

    wregs = {}

    def spec_aps(spec):
        if spec[0] == "w_down":
            _, r, cg = spec
            sl = (slice(r * 1408, (r + 1) * 1408), slice(cg * 512, (cg + 1) * 512))
            kk = 11
        else:
            n, c0 = spec
            sl = (slice(None), slice(c0, c0 + 512))
            kk = 16
        name = spec[0]
        blk = (spec[2] * 4 + spec[1]) if name == "w_down" else spec[1] // 512
        return wf[name][sl], wb[name][blk], kk

    modes = [STATE] * max(NPRE - 1, 0) + ([WARM] if NPRE > 0 else []) + [FULL] * NOWN
    NSTATE = max(NPRE - 1, 0)
    full_seq = []
    for ti_, m in enumerate(modes):
        if ti_ == NSTATE:
            full_seq += [("wk", c * 512) for c in range(4)] + [("wv", c * 512) for c in range(4)]
        full_seq += tile_specs(m, DBG)
    ci = 0
    for spec in full_seq:
        if spec in wregs:
            continue
        r = p.reg("dram", 0, 0, None, str(spec))
        wregs[spec] = r
        src, dst, kk_ = spec_aps(spec)
        srcv_ = src.rearrange("(k p) n -> p k n", p=128)
        dstv_ = dst.rearrange("p (k n) -> p k n", k=kk_)
        p.dma("pool", "cast%d" % (ci % 8), [], [r], lambda e, srcv_=srcv_, dstv_=dstv_: e.dma_start(out=dstv_, in_=srcv_))
        ci += 1

    ws = {"next_dma": 0, "next_acq": 0}

    def ws_issue():
        i = ws["next_dma"]
        if i >= len(full_seq):
            return
        spec = full_seq[i]
        _, src, kk = spec_aps(spec)
        slot = slots[i % NSLOT]
        dst = slot.ap[:, 0:kk * 512]
        p.dma("sp", "wslot%d" % (i % NSLOT), [wregs[spec]], [slot], lambda e, dst=dst, srcv=src: e.dma_start(out=dst, in_=srcv))
        ws["next_dma"] = i + 1

    def ws_acquire(spec):
        i = ws["next_acq"]
        assert full_seq[i] == spec, (i, full_seq[i], spec)
        ws["next_acq"] = i + 1
        assert ws["next_dma"] > i
        slot = slots[i % NSLOT]
        kk = 11 if spec[0] == "w_down" else 16
        return slot, slot.ap[:, 0:kk * 512].rearrange("p (k n) -> p k n", k=kk)

    def ws_release():
        ws_issue()

    p.dma("act", "ld_cp", [], [cpk], lambda e: e.dma_start(out=cpk.ap, in_=cp_d))
    p.dma("act", "ld_cm", [], [cmt], lambda e: e.dma_start(out=cmt.ap, in_=cm_d))
    for _ in range(NSLOT):
        ws_issue()
    p.op("dve", [cmt], [identb], lambda e: e.tensor_copy(identb.ap, cmt.ap[:, 0, :]))
    p.op("dve", [cmt], [maskb], lambda e: e.tensor_copy(maskb.ap, cmt.ap[:, 1, :]))
    p.op("dve", [], [onesb], lambda e: e.memset(onesb.ap, 1.0))
    p.op("dve", [], [onesf], lambda e: e.memset(onesf.ap, 1.0))
    p.op("dve", [], [S], lambda e: e.memset(S.ap, 0.0))
    p.op("dve", [], ucar, lambda e: e.memset(ucar_ap, 0.0))
    p.op("dve", [], gcar, lambda e: e.memset(gcar_ap, 0.0))
    p.op("dve", [cpk], [lbv], lambda e: e.tensor_tensor(lbv.ap[:, 0:8], cpc(CP_LB, 8), cpc(CP_LB + 8, 8), ALU.subtract))
    p.op("act", [lbv], [lbv], lambda e: e.activation(lbv.ap[:, 0:8], lbv.ap[:, 0:8], AF.Sigmoid))
    p.op("dve", [lbv], [lbv], lambda e: e.tensor_scalar(lbv.ap[:, 8:16], lbv.ap[:, 0:8], -1.0, 1.0, ALU.mult, ALU.add))

    eps_ap = cpc(CP_EPS)
    flip = {"a": 0}

    def evac_eng():
        flip["a"] ^= 1
        return "act" if flip["a"] else "dve"

    def copy_op(eng, reads, writes, out_ap, in_ap):
        if eng == "act":
            p.op("act", reads, writes, lambda e: e.copy(out_ap, in_ap))
        else:
            p.op(eng, reads, writes, lambda e: e.tensor_copy(out_ap, in_ap))

    pref = {"n": 0}

    def issue_x(src_d, row0, i):
        sub, half = divmod(i, 2)
        sl = xin[i % 4]
        srcv = src_d[row0 + sub * 128:row0 + (sub + 1) * 128, half * 1024:(half + 1) * 1024]
        p.dma("sp", "xin%d" % (i % 4), [], [sl], lambda e, sl=sl, srcv=srcv: e.dma_start(out=sl.ap, in_=srcv))

    def prefetch_x(nxt):
        if nxt is None:
            return
        for i in range(3):
            issue_x(nxt[0], nxt[1], i)
        pref["n"] = 3

    def load_x(src_d, row0, nsub):
        issued = pref["n"]
        pref["n"] = 0
        n = nsub * 2
        for i in range(n):
            while issued < n and issued < i + 4:
                issue_x(src_d, row0, issued)
                issued += 1
            sub, half = divmod(i, 2)
            sl = xin[i % 4]
            b, pr = psa(2)
            def tr(e, sl=sl, b=b):
                last = None
                for j in range(8):
                    last = e.transpose(ps_t[:, b * 512 + j * 128:b * 512 + (j + 1) * 128], sl.ap[:, j * 128:(j + 1) * 128], identf)
                return last
            p.op("pe", [sl, cmt], pr, tr)
            outv = xT_ap[:, half * 8:(half + 1) * 8, sub * 128:(sub + 1) * 128]
            inv = psf(b, 2).rearrange("p (j t) -> p j t", j=8)
            copy_op(evac_eng(), pr, xT[half * 8:(half + 1) * 8], outv, inv)

    stat_ptr = [0]

    def norm_stats_begin():
        b = 6 + stat_ptr[0] % 2
        stat_ptr[0] += 1
        return {"b": b, "pr": [psr[b]], "n": 0}

    def norm_stats_add(st, kc, ntok, defer=False, c0=0):
        sq = sqn[st["n"] % 2]
        p.op("act", [xT[kc]], [sq], lambda e: e.activation(sq.ap[:, 0:ntok], xT_ap[:, kc, c0:c0 + ntok], AF.Square))
        first = st["n"] == 0
        last = st["n"] == KC - 1
        b = st["b"]
        def mm():
            p.op("pe", [sq, onesb], st["pr"], lambda e: e.matmul(ps_t[:, b * 512:b * 512 + ntok], onesb.ap, sq.ap[:, 0:ntok], start=first, stop=last))
        st["n"] += 1
        if not defer:
            mm()
            return
        prev = st.get("pend")
        st["pend"] = mm
        if prev is not None:
            prev()

    def norm_stats_flush(st):
        prev = st.get("pend")
        if prev is not None:
            prev()
            st["pend"] = None

    def norm_apply(st, widx, ntok, out_ap, out_regs, c0=0):
        norm_stats_flush(st)
        b = st["b"]
        p.op("act", st["pr"] + [cpk], [rstdn], lambda e: e.activation(rstdn.ap[:, 0:ntok], ps_t[:, b * 512:b * 512 + ntok], AF.Ln, bias=eps_ap, scale=1.0 / D))
        p.op("act", [rstdn], [rstdn], lambda e: e.activation(rstdn.ap[:, 0:ntok], rstdn.ap[:, 0:ntok], AF.Exp, scale=-0.5))
        for kc in range(KC):
            p.op("dve", [xT[kc], rstdn, cpk], [out_regs[kc]],
                 lambda e, kc=kc: e.scalar_tensor_tensor(out_ap[:, kc, c0:c0 + ntok], xT_ap[:, kc, c0:c0 + ntok], cpc(CP_NW + 16 * widx + kc), rstdn.ap[:, 0:ntok], ALU.mult, ALU.mult))

    def norm_full(widx, ntok, out_ap, out_regs):
        st = norm_stats_begin()
        for kc in range(KC):
            norm_stats_add(st, kc, ntok)
        norm_apply(st, widx, ntok, out_ap, out_regs)

    def proj_fm(slot, sv, j, rhs_ap, rhs_regs, ntok, nk=KC, fine=False, c0=0):
        b, pr = psa(1)
        if fine:
            for kc in range(nk):
                p.op("pe", [slot, rhs_regs[kc]], pr,
                     lambda e, kc=kc: e.matmul(ps_t[:, b * 512:b * 512 + ntok], sv[:, kc, j * 128:(j + 1) * 128], rhs_ap[:, kc, c0:c0 + ntok], start=(kc == 0), stop=(kc == nk - 1)))
            return b, pr
        def mm(e):
            last = None
            for kc in range(nk):
                last = e.matmul(ps_t[:, b * 512:b * 512 + ntok], sv[:, kc, j * 128:(j + 1) * 128], rhs_ap[:, kc, c0:c0 + ntok], start=(kc == 0), stop=(kc == nk - 1))
            return last
        p.op("pe", [slot] + list(rhs_regs), pr, mm)
        return b, pr

    def hgrn_prep_head(h, slot_f, sv_f, j, full, sc, fine=False, qcs=slice(0, TT)):
        omf, lgf, Bc, bp = sc["omf"], sc["lgf"], sc["Bc"], sc["bp"]
        bf_, prf = proj_fm(slot_f, sv_f, j, hT_ap, hT, TT, fine=fine)
        p.op("act", prf, [omf], lambda e: e.activation(omf.ap, psf(bf_), AF.Exp))
        yield
        p.op("act", [omf], [omf], lambda e: e.activation(omf.ap, omf.ap, AF.Ln, bias=1.0))
        yield
        p.op("act", [omf], [omf], lambda e: e.activation(omf.ap, omf.ap, AF.Exp, scale=-1.0))
        yield
        p.op("dve", [omf, lbv], [omf], lambda e: e.tensor_scalar(omf.ap, omf.ap, lbv.ap[:, 8 + h:9 + h], None, ALU.mult))
        yield
        p.op("act", [omf], [lgf], lambda e: e.activation(lgf.ap, omf.ap, AF.Ln, bias=1.0, scale=-1.0))
        yield
        def scan(e):
            last = None
            for s_ in range(NSUB):
                sl = slice(s_ * 128, (s_ + 1) * 128)
                last = e.tensor_tensor_scan(Bc.ap[:, sl], onesf.ap, lgf.ap[:, sl], 0.0, ALU.mult, ALU.add)
            return last
        p.op("dve", [lgf, onesf], [Bc], scan)
        yield
        B3 = Bc.ap.rearrange("p (s t) -> p s t", s=NSUB)
        bp3 = bp.ap.rearrange("p (s t) -> p s t", s=NSUB)
        p.op("dve", [Bc], [bp], lambda e: e.tensor_tensor(bp3, B3, B3[:, :, 63:64].broadcast_to([128, NSUB, 128]), ALU.subtract))
        yield
        p.op("act", [Bc], [dsc], lambda e: e.activation(dsc.ap[:, 0, :, h], B3[:, :, 127], AF.Exp))
        p.op("act", [Bc], [dsc], lambda e: e.activation(dsc.ap[:, 2, :, h], B3[:, :, 63], AF.Exp))
        p.op("act", [bp], [dsc], lambda e: e.activation(dsc.ap[:, 1, :, h], bp3[:, :, 127], AF.Exp))
        yield
        if full:
            p.op("act", [bp], [lgf], lambda e: e.activation(lgf.ap, bp.ap, AF.Exp))
        p.op("act", [bp], [Bc], lambda e: e.activation(Bc.ap, bp.ap, AF.Exp, scale=-1.0))
        yield
        if full:
            p.op("dve", [QtT[h], lgf], [QtT[h]], lambda e: e.tensor_tensor(QtT_ap[:, h, qcs], QtT_ap[:, h, qcs], lgf.ap[:, qcs], ALU.mult))
        p.op("dve", [omf, Bc], [KtT[h]], lambda e: e.tensor_tensor(KtT_ap[:, h, :], omf.ap, Bc.ap, ALU.mult))
        yield

    def run_interleaved(gens):
        act_ = list(gens)
        while act_:
            for g_ in list(act_):
                try:
                    next(g_)
                except StopIteration:
                    act_.remove(g_)

    def v_proj(slot, sv, blk):
        for sub in range(NSUB):
            b, pr = psa(1)
            def mm(e, b=b, sub=sub):
                last = None
                for kc in range(KC):
                    last = e.matmul(psf(b), hT_ap[:, kc, sub * 128:(sub + 1) * 128], sv[:, kc, :], start=(kc == 0), stop=(kc == KC - 1))
                return last
            p.op("pe", [slot] + hT, pr, mm)
            copy_op(evac_eng(), pr, [V[sub]], V_ap[:, sub, blk * 512:(blk + 1) * 512], psf(b))

    def hgrn_state_tr(sub, kbuf):
        cols = slice(sub * 128, (sub + 1) * 128)
        bt, prt = psa(1)
        def trk(e):
            last = None
            for h in range(NH):
                last = e.transpose(ps_tb[:, bt * 1024 + h * 128:bt * 1024 + (h + 1) * 128], KtT_ap[:, h, cols], identb.ap)
            return last
        p.op("pe", KtT + [identb], prt, trk)
        copy_op(evac_eng(), prt, [kbuf], kbuf.ap.rearrange("p h k -> p (h k)"), ps_tb[:, bt * 1024:(bt + 1) * 1024])

    def hgrn_state_upd(sub, kbuf):
        bp_, prp = psa(2)
        def pm(e):
            last = None
            for h in range(NH):
                last = e.matmul(ps_t[:, bp_ * 512 + h * 128:bp_ * 512 + (h + 1) * 128], kbuf.ap[:, h, :], V_ap[:, sub, h * 128:(h + 1) * 128], start=True, stop=True)
            return last
        p.op("pe", [kbuf, V[sub]], prp, pm)
        p.op("dve", [S, dsc], [S], lambda e: e.tensor_tensor(S.ap, S.ap, dsc.ap[:, 0, sub, :].unsqueeze(2).broadcast_to([128, NH, 128]), ALU.mult))
        def su(e):
            last = None
            for h in range(NH):
                last = e.scalar_tensor_tensor(S.ap[:, h, :], ps_t[:, bp_ * 512 + h * 128:bp_ * 512 + (h + 1) * 128], dsc.ap[:, 1, sub, h:h + 1], S.ap[:, h, :], ALU.mult, ALU.add)
            return last
        p.op("dve", prp + [S, dsc], [S], su)

    def hgrn_subtile(sub, full):
        cols = slice(sub * 128, (sub + 1) * 128)
        bt, prt = psa(1)
        def trk(e):
            last = None
            for h in range(NH):
                last = e.transpose(ps_tb[:, bt * 1024 + h * 128:bt * 1024 + (h + 1) * 128], KtT_ap[:, h, cols], identb.ap)
            return last
        p.op("pe", KtT + [identb], prt, trk)
        copy_op(evac_eng(), prt, [Ktm], Ktm.ap.rearrange("p h k -> p (h k)"), ps_tb[:, bt * 1024:(bt + 1) * 1024])
        if full:
            p.op("dve", [S, dsc], [smid], lambda e: e.tensor_tensor(smid.ap, S.ap, dsc.ap[:, 2, sub, :].unsqueeze(2).broadcast_to([128, NH, 128]), ALU.mult))
            bs, prs = psa(2)
            def sc(e):
                last = None
                for h in range(NH):
                    last = e.matmul(ps_t[:, bs * 512 + h * 128:bs * 512 + (h + 1) * 128], KtT_ap[:, h, cols], QtT_ap[:, h, cols], start=True, stop=True)
                return last
            p.op("pe", KtT + QtT, prs, sc)
            p.op("dve", prs + [maskb], [scm], lambda e: e.tensor_tensor(scm.ap, psf(bs, 2).rearrange("p (h t) -> p h t", h=NH), maskb.ap.unsqueeze(1).broadcast_to([128, NH, 128]), ALU.mult))
            bo, pro = psa(2)
            def om(e):
                last = None
                for h in range(NH):
                    o_ = ps_t[:, bo * 512 + h * 128:bo * 512 + (h + 1) * 128]
                    e.matmul(o_, V_ap[:, sub, h * 128:(h + 1) * 128], scm.ap[:, h, :], start=True, stop=False)
                    last = e.matmul(o_, smid.ap[:, h, :], QtT_ap[:, h, cols], start=False, stop=True)
                return last
            p.op("pe", [V[sub], scm, smid] + QtT, pro, om)
        bp_, prp = psa(2)
        def pm(e):
            last = None
            for h in range(NH):
                last = e.matmul(ps_t[:, bp_ * 512 + h * 128:bp_ * 512 + (h + 1) * 128], Ktm.ap[:, h, :], V_ap[:, sub, h * 128:(h + 1) * 128], start=True, stop=True)
            return last
        p.op("pe", [Ktm, V[sub]], prp, pm)
        p.op("dve", [S, dsc], [S], lambda e: e.tensor_tensor(S.ap, S.ap, dsc.ap[:, 0, sub, :].unsqueeze(2).broadcast_to([128, NH, 128]), ALU.mult))
        def su(e):
            last = None
            for h in range(NH):
                last = e.scalar_tensor_tensor(S.ap[:, h, :], ps_t[:, bp_ * 512 + h * 128:bp_ * 512 + (h + 1) * 128], dsc.ap[:, 1, sub, h:h + 1], S.ap[:, h, :], ALU.mult, ALU.add)
            return last
        p.op("dve", prp + [S, dsc], [S], su)
        if full:
            p.op("act", pro, [sqo], lambda e: e.activation(sqo.ap, psf(bo, 2), AF.Square))
            bq, prq = psa(2)
            def ssm(e):
                e.matmul(psf(bq), onesb.ap, sqo.ap[:, 0:512], start=True, stop=True)
                return e.matmul(psf(bq + 1), onesb.ap, sqo.ap[:, 512:1024], start=True, stop=True)
            p.op("pe", [sqo, onesb], prq, ssm)
            p.op("act", prq + [cpk], [rto], lambda e: e.activation(rto.ap, psf(bq, 2), AF.Ln, bias=eps_ap, scale=1.0 / 128))
            p.op("act", [rto], [rto], lambda e: e.activation(rto.ap, rto.ap, AF.Exp, scale=-0.5))
            p.op("dve", pro + [rto], [sqo], lambda e: e.tensor_tensor(sqo.ap, psf(bo, 2), rto.ap, ALU.mult))
            p.op("dve", [sqo, cpk] + mix[0:NH], mix[0:NH],
                 lambda e: e.scalar_tensor_tensor(mix_ap[:, 0:NH, cols], sqo.ap.rearrange("p (h t) -> p h t", h=NH), cpc(CP_HNW), mix_ap[:, 0:NH, cols], ALU.mult, ALU.mult))

    out_toks = []
    ost_ptr = [0]

    tile_no = [0]

    def emit_output(out_row0):
        for sub in range(NSUB):
            for half in range(2):
                b, pr = psa(2)
                def tro(e, b=b, sub=sub, half=half):
                    last = None
                    for j in range(8):
                        last = e.transpose(ps_t[:, b * 512 + j * 128:b * 512 + (j + 1) * 128], xT_ap[:, half * 8 + j, sub * 128:(sub + 1) * 128], identf)
                    return last
                p.op("pe", xT[half * 8:(half + 1) * 8] + [cmt], pr, tro)
                os_ = ost[ost_ptr[0] % 2]
                ost_ptr[0] += 1
                copy_op(evac_eng(), pr, [os_], os_.ap, psf(b, 2))
                dstv = y_d[out_row0 + sub * 128:out_row0 + (sub + 1) * 128, half * 1024:(half + 1) * 1024]
                out_toks.append(p.dma("act", "ost%d" % ((ost_ptr[0] - 1) % 2), [os_], [], lambda e, os_=os_, dstv=dstv: e.dma_start(out=dstv, in_=os_.ap)))

    def do_tile(src_d, row0, mode, out_row0, nxt=None):
        full = mode != STATE
        c0, n = (0, TT) if mode != WARM else (TT - 128, 128)
        cs = slice(c0, c0 + n)
        tn = tile_no[0]
        tile_no[0] += 1
        p.phase = "t%d.load" % tn
        load_x(src_d, row0, NSUB)
        if DBG == 0 and mode == FULL:
            emit_output(out_row0)
            return
        p.phase = "t%d.norm1" % tn
        norm_full(0, TT, hT_ap, hT)
        p.phase = "t%d.prep" % tn
        if full:
            for hg in range(2):
                slot_q, sv_q = ws_acquire(("w_in", hg * 512))
                for j in range(4):
                    h = hg * 4 + j
                    b, pr = proj_fm(slot_q, sv_q, j, hT_ap, hT, n, fine=(h == 0), c0=c0)
                    p.op("act", pr, [QtT[h]], lambda e, b=b, h=h: e.activation(QtT_ap[:, h, cs], ps_t[:, b * 512:b * 512 + n], AF.Silu))
                ws_release()
            for hg in range(2):
                slot_g, sv_g = ws_acquire(("w_in", 3072 + hg * 512))
                for j in range(4):
                    h = hg * 4 + j
                    b, pr = proj_fm(slot_g, sv_g, j, hT_ap, hT, n, c0=c0)
                    p.op("act", pr, [mix[h]], lambda e, b=b, h=h: e.activation(mix_ap[:, h, cs], ps_t[:, b * 512:b * 512 + n], AF.Silu))
                ws_release()
        for hg in range(2):
            slot_f, sv_f = ws_acquire(("w_in", 1024 + hg * 512))
            for jj in range(0, 4, 2):
                run_interleaved([hgrn_prep_head(hg * 4 + j, slot_f, sv_f, j, full, SCR[j % 2], fine=(not full and hg == 0 and j == 0), qcs=cs)
                                 for j in (jj, jj + 1)])
            ws_release()
        p.phase = "t%d.vproj" % tn
        for blk in range(2):
            slot, sv = ws_acquire(("w_in", 2048 + blk * 512))
            v_proj(slot, sv, blk)
            ws_release()
        p.phase = "t%d.subtiles" % tn
        if mode == STATE:
            kb = [Ktm, scm]
            hgrn_state_tr(0, kb[0])
            for sub in range(NSUB):
                if sub + 1 < NSUB:
                    hgrn_state_tr(sub + 1, kb[(sub + 1) % 2])
                hgrn_state_upd(sub, kb[sub % 2])
        else:
            for sub in range(NSUB):
                hgrn_subtile(sub, mode == FULL or (mode == WARM and sub == NSUB - 1))
        if not full:
            prefetch_x(nxt)
            return
        p.phase = "t%d.sconv" % tn
        for blk in range(2):
            slot_c, sv_c = ws_acquire(("w_in", 5120 + blk * 512))
            for j in range(4):
                bc_, prc = proj_fm(slot_c, sv_c, j, hT_ap, hT, n, c0=c0)
                p.op("act", prc, [ccs[j]], lambda e, bc_=bc_, j=j: e.copy(ccs[j].ap[:, 0:n], ps_t[:, bc_ * 512:bc_ * 512 + n]))
            ws_release()
            slot_h, sv_h = ws_acquire(("w_in", 6144 + blk * 512))
            for j in range(4):
                c = blk * 4 + j
                bh, prh = proj_fm(slot_h, sv_h, j, hT_ap, hT, n, c0=c0)
                ub = ubuf[c % 2]
                t0s = t0s4[j]
                p.op("dve", [ucar[c]], [ub], lambda e, ub=ub, c=c: e.tensor_copy(ub.ap[:, 0:2], ucar_ap[:, c, :]))
                p.op("dve", prh + [ccs[j]], [ub], lambda e, ub=ub, bh=bh, j=j: e.tensor_tensor(ub.ap[:, 2:n + 2], ps_t[:, bh * 512:bh * 512 + n], ccs[j].ap[:, 0:n], ALU.mult))
                p.op("dve", [ub], [ucar[c]], lambda e, ub=ub, c=c: e.tensor_copy(ucar_ap[:, c, :], ub.ap[:, n:n + 2]))
                p.op("act", [ub, cpk], [t0s], lambda e, ub=ub, c=c, t0s=t0s: e.activation(t0s.ap[:, 0:n], ub.ap[:, 2:n + 2], AF.Identity, scale=cpc(CP_SCW + 16 + c)))
                p.op("dve", [ub, t0s, cpk], [t0s], lambda e, ub=ub, c=c, t0s=t0s: e.scalar_tensor_tensor(t0s.ap[:, 0:n], ub.ap[:, 1:n + 1], cpc(CP_SCW + 8 + c), t0s.ap[:, 0:n], ALU.mult, ALU.add))
                p.op("dve", [ub, t0s, cpk], [t0s], lambda e, ub=ub, c=c, t0s=t0s: e.scalar_tensor_tensor(t0s.ap[:, 0:n], ub.ap[:, 0:n], cpc(CP_SCW + c), t0s.ap[:, 0:n], ALU.mult, ALU.add))
            ws_release()
            slot_b, sv_b = ws_acquire(("w_in", 4096 + blk * 512))
            for j in range(4):
                c = blk * 4 + j
                bb, prb = proj_fm(slot_b, sv_b, j, hT_ap, hT, n, c0=c0)
                p.op("dve", prb + [t0s4[j]], [mix[8 + c]], lambda e, bb=bb, c=c, j=j: e.tensor_tensor(mix_ap[:, 8 + c, cs], ps_t[:, bb * 512:bb * 512 + n], t0s4[j].ap[:, 0:n], ALU.mult))
            ws_release()

        def resid_proj(wname, act_ap, act_regs, st):
            for c in range(4):
                slot, sv = ws_acquire((wname, c * 512))
                for j in range(4):
                    ch = c * 4 + j
                    b, pr = proj_fm(slot, sv, j, act_ap, act_regs, n, c0=c0)
                    p.op("dve", pr + [xT[ch]], [xT[ch]], lambda e, b=b, ch=ch: e.tensor_tensor(xT_ap[:, ch, cs], ps_t[:, b * 512:b * 512 + n], xT_ap[:, ch, cs], ALU.add))
                    norm_stats_add(st, ch, n, defer=True, c0=c0)
                ws_release()

        p.phase = "t%d.wout" % tn
        st = norm_stats_begin()
        resid_proj("w_out", mix_ap, mix, st)
        if DBG == 1 and mode == FULL:
            emit_output(out_row0)
            return
        norm_apply(st, 1, n, hT_ap, hT, c0=c0)
        p.phase = "t%d.wq" % tn
        qT_ap, qT = mix_ap, mix
        for c in range(4):
            slot, sv = ws_acquire(("wq", c * 512))
            for j in range(4):
                ch = c * 4 + j
                b, pr = proj_fm(slot, sv, j, hT_ap, hT, n, fine=(ch == 0), c0=c0)
                p.op("act", pr, [qT[ch]], lambda e, b=b, ch=ch: e.activation(qT_ap[:, ch, cs], ps_t[:, b * 512:b * 512 + n], AF.Identity, scale=float(512 ** -0.5)))
            ws_release()
        def att_scores(hd):
            pts = pT[hd % 2]
            for mc in range(2):
                b, pr = psa(1)
                def smm(e, b=b, mc=mc, hd=hd):
                    last = None
                    for dc in range(4):
                        kc = hd * 4 + dc
                        last = e.matmul(ps_t[:, b * 512:b * 512 + n], kT_ap[:, kc, mc * 128:(mc + 1) * 128], qT_ap[:, kc, cs], start=(dc == 0), stop=(dc == 3))
                    return last
                p.op("pe", kT[hd * 4:hd * 4 + 4] + qT[hd * 4:hd * 4 + 4], pr, smm)
                p.op("act", pr, [pts[mc]], lambda e, b=b, mc=mc, pts=pts: e.activation(pts[mc].ap[:, 0:n], ps_t[:, b * 512:b * 512 + n], AF.Exp))

        def att_rest(hd):
            pts = pT[hd % 2]
            b, pr = psa(1)
            def summ(e, b=b, pts=pts):
                e.matmul(ps_t[:, b * 512:b * 512 + n], onesb.ap, pts[0].ap[:, 0:n], start=True, stop=False)
                return e.matmul(ps_t[:, b * 512:b * 512 + n], onesb.ap, pts[1].ap[:, 0:n], start=False, stop=True)
            p.op("pe", [pts[0], pts[1], onesb], pr, summ)
            rsb = rs_[hd % 2]
            p.op("act", pr, [rsb], lambda e, b=b, rsb=rsb: e.activation(rsb.ap[:, 0:n], ps_t[:, b * 512:b * 512 + n], AF.Ln))
            p.op("act", [rsb], [rsb], lambda e, rsb=rsb: e.activation(rsb.ap[:, 0:n], rsb.ap[:, 0:n], AF.Exp, scale=-1.0))
            for dc in range(4):
                kc = hd * 4 + dc
                b, pr = psa(1)
                def pv(e, b=b, kc=kc, pts=pts):
                    e.matmul(ps_t[:, b * 512:b * 512 + n], vtm_ap[:, 0, kc * 128:(kc + 1) * 128], pts[0].ap[:, 0:n], start=True, stop=False)
                    return e.matmul(ps_t[:, b * 512:b * 512 + n], vtm_ap[:, 1, kc * 128:(kc + 1) * 128], pts[1].ap[:, 0:n], start=False, stop=True)
                p.op("pe", [vtm[0], vtm[1], pts[0], pts[1]], pr, pv)
                p.op("dve", pr + [rsb], [oaT[kc]], lambda e, b=b, kc=kc, rsb=rsb: e.tensor_tensor(oaT_ap[:, kc, cs], ps_t[:, b * 512:b * 512 + n], rsb.ap[:, 0:n], ALU.mult))

        p.phase = "t%d.att" % tn
        att_scores(0)
        for hd in range(4):
            if hd + 1 < 4:
                att_scores(hd + 1)
            att_rest(hd)
        p.phase = "t%d.wo" % tn
        st = norm_stats_begin()
        resid_proj("wo", oaT_ap, oaT, st)
        if DBG == 2 and mode == FULL:
            emit_output(out_row0)
            return
        norm_apply(st, 2, n, hT_ap, hT, c0=c0)
        p.phase = "t%d.ffn" % tn
        prefetch_x(nxt)
        for blk in range(11):
            slot_g, sv_g = ws_acquire(("w_gate", blk * 512))
            if mode == FULL:
                slot_u, sv_u = ws_acquire(("w_up", blk * 512))
            for j in range(4):
                c = blk * 4 + j
                bg, prg = proj_fm(slot_g, sv_g, j, hT_ap, hT, n, fine=(c == 0), c0=c0)
                gb = gbuf[c % 2]
                t = tb[c % 2]
                p.op("dve", [gcar[c]], [gb], lambda e, gb=gb, c=c: e.tensor_copy(gb.ap[:, 0:2], gcar_ap[:, c, :]))
                p.op("act", prg, [gb], lambda e, gb=gb, bg=bg: e.copy(gb.ap[:, 2:n + 2], ps_t[:, bg * 512:bg * 512 + n]))
                p.op("dve", [gb], [gcar[c]], lambda e, gb=gb, c=c: e.tensor_copy(gcar_ap[:, c, :], gb.ap[:, n:n + 2]))
                if mode != FULL:
                    continue
                p.op("act", prg + [cpk], [t], lambda e, t=t, bg=bg, c=c: e.activation(t.ap, psf(bg), AF.Identity, bias=cpc(CP_FCB + c), scale=cpc(CP_FCW + 88 + c)))
                bu, pru = proj_fm(slot_u, sv_u, j, hT_ap, hT, TT)
                p.op("dve", [gb, t, cpk], [t], lambda e, gb=gb, t=t, c=c: e.scalar_tensor_tensor(t.ap, gb.ap[:, 1:TT + 1], cpc(CP_FCW + 44 + c), t.ap, ALU.mult, ALU.add))
                p.op("dve", [gb, t, cpk], [t], lambda e, gb=gb, t=t, c=c: e.scalar_tensor_tensor(t.ap, gb.ap[:, 0:TT], cpc(CP_FCW + c), t.ap, ALU.mult, ALU.add))
                p.op("act", [t], [t], lambda e, t=t: e.activation(t.ap, t.ap, AF.Silu))
                p.op("dve", pru + [t], [hid[c]], lambda e, t=t, bu=bu, c=c: e.tensor_tensor(hid_ap[:, c, :], psf(bu), t.ap, ALU.mult))
            ws_release()
            if mode == FULL:
                ws_release()
        if mode != FULL:
            p.op("dve", gcar + [cpk], gcar, lambda e: e.tensor_scalar(gcar_ap, gcar_ap, cpc(CP_FLAG), None, ALU.mult))
            p.op("dve", ucar + [cpk], ucar, lambda e: e.tensor_scalar(ucar_ap, ucar_ap, cpc(CP_FLAG), None, ALU.mult))
            p.op("dve", [S, cpk], [S], lambda e: e.tensor_scalar(S.ap, S.ap, cpc(CP_FLAG), None, ALU.mult))
            return
        p.phase = "t%d.down" % tn
        st = norm_stats_begin()
        for cg in range(4):
            b4, pr4 = psa(4)
            for r in range(4):
                slot, sv = ws_acquire(("w_down", r, cg))
                for j in range(4):
                    def dm(e, b4=b4, r=r, sv=sv, j=j):
                        last = None
                        for kk in range(11):
                            last = e.matmul(psf(b4 + j), sv[:, kk, j * 128:(j + 1) * 128], hid_ap[:, r * 11 + kk, :], start=(r == 0 and kk == 0), stop=(r == 3 and kk == 10))
                        return last
                    p.op("pe", [slot] + hid[r * 11:(r + 1) * 11], [pr4[j]], dm)
                ws_release()
            for j in range(4):
                ch = cg * 4 + j
                p.op("dve", [pr4[j], xT[ch]], [xT[ch]], lambda e, b4=b4, j=j, ch=ch: e.tensor_tensor(xT_ap[:, ch, :], psf(b4 + j), xT_ap[:, ch, :], ALU.add))
                norm_stats_add(st, ch, TT, defer=True)
        p.phase = "t%d.out" % tn
        if DBG != 3:
            norm_apply(st, 3, TT, xT_ap, xT)
        emit_output(out_row0)

    def mem_kv():
        p.phase = "mem"
        load_x(mem_d, 0, 2)
        norm_full(4, MEM, hT_ap, hT)
        for c in range(4):
            slot, sv = ws_acquire(("wk", c * 512))
            for j in range(4):
                b, pr = proj_fm(slot, sv, j, hT_ap, hT, MEM)
                ch = c * 4 + j
                copy_op(evac_eng(), pr, [kT[ch]], kT_ap[:, ch, :], ps_t[:, b * 512:b * 512 + MEM])
            ws_release()
        for c in range(4):
            slot, sv = ws_acquire(("wv", c * 512))
            for ms in range(2):
                b, pr = psa(1)
                def mm(e, b=b, ms=ms, sv=sv):
                    last = None
                    for kc in range(KC):
                        last = e.matmul(psf(b), hT_ap[:, kc, ms * 128:(ms + 1) * 128], sv[:, kc, :], start=(kc == 0), stop=(kc == KC - 1))
                    return last
                p.op("pe", [slot] + hT, pr, mm)
                copy_op(evac_eng(), pr, [vtm[ms]], vtm_ap[:, ms, c * 512:(c + 1) * 512], psf(b))
            ws_release()


    tiles = [(xp_d, t * TT, modes[t], None) for t in range(NPRE)] + [(xo_d, t * TT, FULL, t * TT) for t in range(NOWN)]
    for ti, (src_, r0_, m_, o_) in enumerate(tiles):
        if ti == NSTATE:
            mem_kv()
        nxt = tiles[ti + 1][0:2] if ti + 1 < len(tiles) else None
        if DBG is not None or ti + 1 == NSTATE:
            nxt = None
        do_tile(src_, r0_, m_, o_, nxt)
    assert ws["next_acq"] == len(full_seq)
    last = {}
    for k, v in out_toks:
        last[k] = max(last.get(k, 0), v)
    p.wait_all("act", list(last.items()))
    p.emit()
    global _LAST_PROG
    _LAST_PROG = p
    return nc


_CACHE = {}
_LAST_PROG = None


def run_cores(inp, x, mem, NPRE, NOWN, n_split, DBG=None):
    B, S_, _ = x.shape
    key = (NPRE, NOWN, DBG)
    if key not in _CACHE:
        _CACHE[key] = build_program(NPRE, NOWN, DBG)
    nc = _CACHE[key]
    cm = const_mats()
    wts = {n: np.ascontiguousarray(np.asarray(inp[n], np.float32).reshape(W_SHAPES[n])) for n in W_SHAPES}
    in_maps = []
    L = NOWN * TT
    P_ = max(NPRE, 1) * TT
    for b in range(B):
        for h in range(n_split):
            xo = np.ascontiguousarray(x[b, h * L:(h + 1) * L])
            if h == 0:
                xp = np.zeros((P_, D), np.float32)
                flag = 0.0
            else:
                xp = np.ascontiguousarray(x[b, h * L - P_:h * L])
                flag = 1.0
            m = {"xo": xo, "xp": xp, "mem": np.ascontiguousarray(mem[b]), "cpack": pack_consts(inp, flag), "cmats": cm}
            m.update(wts)
            in_maps.append(m)
    res = run_bass_kernel_spmd(nc, in_maps, core_ids=list(range(len(in_maps))))
    out = np.empty((B, S_, D), np.float32)
    i = 0
    for b in range(B):
        for h in range(n_split):
            out[b, h * L:(h + 1) * L] = res.results[i]["y"]
            i += 1
    return out


def kernel(**inputs):
    x = np.asarray(inputs["x"], np.float32)
    mem = np.asarray(inputs["mem"], np.float32)
    return run_cores(inputs, x, mem, NPRE=8, NOWN=8, n_split=2)
```

```python
import numpy as np
import concourse.bass as bass
import concourse.mybir as mybir
from concourse.bass_utils import run_bass_kernel_spmd

F32 = mybir.dt.float32
BF16 = mybir.dt.bfloat16
AF = mybir.ActivationFunctionType
ALU = mybir.AluOpType

_DT_SIZE = {F32: 4, BF16: 2}

D = 2048
KC = 16
TT = 512
NSUB = 4
NH = 8
DFF = 5632
FC = 44
MEM = 256
EPS = 1e-6
SB_BASE = 17408
SB_LIMIT = 224 * 1024


class Reg:
    __slots__ = ("space", "lo", "hi", "ap", "w", "rs", "ov", "name")

    def __init__(self, space, lo, hi, ap, name):
        self.space, self.lo, self.hi, self.ap, self.name = space, lo, hi, ap, name
        self.w = None
        self.rs = {}
        self.ov = [self]


class Prog:
    ENG = ("pe", "act", "dve", "pool", "sp")

    def __init__(self, nc):
        self.nc = nc
        self.stream = {e: [] for e in self.ENG}
        self.cnt = {e: 0 for e in self.ENG}
        self.waited = {e: {} for e in self.ENG}
        self.regs = {"sb": [], "ps": []}
        self.dma_cnt = {}
        self.sb_off = SB_BASE
        self.n_t = 0
        self.phase = ""
        self.pe_log = []

    def tensor_at(self, name, free_shape, dtype, lo, parts=128):
        self.n_t += 1
        n = int(np.prod(free_shape)) * _DT_SIZE[dtype]
        assert lo % 32 == 0 and lo + n <= SB_LIMIT, (name, lo, n)
        t = self.nc.alloc_sbuf_tensor_at("%s_%d" % (name, self.n_t), [parts] + list(free_shape), dtype, offset=lo)
        return t.ap()

    def alloc(self, nbytes):
        lo = (self.sb_off + 63) // 64 * 64
        self.sb_off = lo + nbytes
        assert self.sb_off <= SB_LIMIT, self.sb_off
        return lo

    def reg(self, space, lo, hi, ap, name=""):
        r = Reg(space, lo, hi, ap, name)
        if space in self.regs:
            for o in self.regs[space]:
                if o.lo < hi and lo < o.hi:
                    o.ov.append(r)
                    r.ov.append(o)
            self.regs[space].append(r)
        return r

    def buf(self, name, free_shape, dtype, lo=None):
        n = int(np.prod(free_shape)) * _DT_SIZE[dtype]
        if lo is None:
            lo = self.alloc(n)
        ap = self.tensor_at(name, free_shape, dtype, lo)
        return self.reg("sb", lo, lo + n, ap, name)

    def chunks(self, name, nch, celems, dtype, lo=None):
        cb = celems * _DT_SIZE[dtype]
        if lo is None:
            lo = self.alloc(nch * cb)
        ap = self.tensor_at(name, [nch, celems], dtype, lo)
        regs = [self.reg("sb", lo + i * cb, lo + (i + 1) * cb, ap[:, i, :], "%s%d" % (name, i)) for i in range(nch)]
        return ap, regs

    def _collect(self, eng, reads, writes):
        waits = {}
        wd = self.waited[eng]

        def need(k, v, raw):
            if k == eng and not raw and eng == "pe":
                return
            if wd.get(k, 0) >= v:
                return
            if waits.get(k, 0) < v:
                waits[k] = v

        for r in reads:
            for x in r.ov:
                if x.w is not None:
                    need(x.w[0], x.w[1], True)
        for w in writes:
            for x in w.ov:
                if x.w is not None:
                    need(x.w[0], x.w[1], False)
                for k, v in x.rs.items():
                    need(k, v, False)
        for k, v in waits.items():
            wd[k] = v
        return list(waits.items())

    def op(self, eng, reads, writes, fn):
        waits = self._collect(eng, reads, writes)
        self.cnt[eng] += 1
        c = self.cnt[eng]
        self.stream[eng].append((waits, fn, (eng, 1), self.phase))
        for r in reads:
            if r.rs.get(eng, 0) < c:
                r.rs[eng] = c
        for w in writes:
            w.w = (eng, c)
            w.rs = {}
        return (eng, c)

    def dma(self, q, key, reads, writes, fn):
        waits = self._collect(q, reads, writes)
        prev = self.dma_cnt.get(key, 0)
        if prev > 0 and self.waited[q].get(key, 0) < prev:
            waits = [w for w in waits if w[0] != key] + [(key, prev)]
            self.waited[q][key] = prev
        self.dma_cnt[key] = prev + 16
        c = self.dma_cnt[key]
        self.stream[q].append((waits, fn, (key, 16), self.phase))
        for r in reads:
            if r.rs.get(key, 0) < c:
                r.rs[key] = c
        for w in writes:
            w.w = (key, c)
            w.rs = {}
        return (key, c)

    def wait_all(self, eng, toks):
        waits = []
        for k, v in toks:
            if self.waited[eng].get(k, 0) < v:
                waits.append((k, v))
                self.waited[eng][k] = v
        self.stream[eng].append((waits, None, None, "end"))

    def emit(self):
        nc = self.nc
        keys = set()
        for e in self.ENG:
            for waits, fn, inc, _ph in self.stream[e]:
                for k, _ in waits:
                    keys.add(k)
                if inc is not None:
                    keys.add(inc[0])
        sems = {k: nc.alloc_semaphore("s_" + str(k)) for k in sorted(keys, key=str)}
        engobj = {"pe": "tensor", "act": "scalar", "dve": "vector", "pool": "gpsimd", "sp": "sync"}
        with nc.Block() as block:
            def mk(e):
                def body(eng):
                    cnt = [0]
                    if e == "pe":
                        class _C:
                            def __getattr__(s_, name):
                                f = getattr(eng, name)
                                if name in ("matmul", "transpose"):
                                    def g(*a, **k):
                                        cnt[0] += 1
                                        return f(*a, **k)
                                    return g
                                return f
                        engw = _C()
                    else:
                        engw = eng
                    for waits, fn, inc, ph in self.stream[e]:
                        for k, v in waits:
                            eng.wait_ge(sems[k], v)
                        if fn is not None:
                            c0 = cnt[0]
                            ins = fn(engw)
                            ins.then_inc(sems[inc[0]], inc[1])
                            if e == "pe":
                                self.pe_log.append((ph, cnt[0] - c0))
                return body
            for e in self.ENG:
                if self.stream[e]:
                    getattr(block, engobj[e])(mk(e))


CP_NW = 0
CP_LB = 80
CP_HNW = 96
CP_SCW = 97
CP_FCW = 121
CP_FCB = 253
CP_FLAG = 297
CP_EPS = 298
NCP = 300


def _fm(v, nch):
    return np.ascontiguousarray(np.asarray(v, np.float32).reshape(nch, 128).T)


def pack_consts(inp, flag):
    cp = np.zeros((128, NCP), np.float32)
    for i, n in enumerate(["norm1_w", "norm2_w", "norm3_w", "final_norm_w", "mem_norm_w"]):
        cp[:, CP_NW + 16 * i:CP_NW + 16 * (i + 1)] = _fm(np.asarray(inp[n]).reshape(-1), 16)
    lb = np.asarray(inp["hgrn_lb"], np.float32)
    for r in range(2):
        cp[:, CP_LB + 8 * r:CP_LB + 8 * (r + 1)] = _fm(lb[r], 8)
    cp[:, CP_HNW] = np.asarray(inp["hgrn_norm_w"], np.float32).reshape(-1)
    sc = np.asarray(inp["sconv_w"], np.float32).reshape(3, 1024)
    for j in range(3):
        cp[:, CP_SCW + 8 * j:CP_SCW + 8 * (j + 1)] = _fm(sc[j], 8)
    fw_ = np.asarray(inp["ffn_conv_w"], np.float32).reshape(3, DFF)
    for j in range(3):
        cp[:, CP_FCW + 44 * j:CP_FCW + 44 * (j + 1)] = _fm(fw_[j], 44)
    cp[:, CP_FCB:CP_FCB + 44] = _fm(np.asarray(inp["ffn_conv_b"], np.float32).reshape(-1), 44)
    cp[:, CP_FLAG] = flag
    cp[:, CP_EPS] = EPS
    return cp


def const_mats():
    cm = np.zeros((128, 2, 128), np.float32)
    cm[:, 0, :] = np.eye(128, dtype=np.float32)
    cm[:, 1, :] = np.triu(np.ones((128, 128), np.float32))
    return cm


W_SHAPES = {"w_in": (D, 7168), "w_out": (D, D), "wq": (D, D), "wk": (D, D), "wv": (D, D), "wo": (D, D),
            "w_gate": (D, DFF), "w_up": (D, DFF), "w_down": (DFF, D)}

STATE, WARM, FULL = 0, 1, 2


def tile_specs(mode, dbg=None):
    s = []
    if dbg == 0 and mode == FULL:
        return s
    if mode == STATE:
        s += [("w_in", 1024), ("w_in", 1536), ("w_in", 2048), ("w_in", 2560)]
        return s
    s += [("w_in", 0), ("w_in", 512)]
    s += [("w_in", 3072), ("w_in", 3584)]
    s += [("w_in", 1024), ("w_in", 1536)]
    s += [("w_in", 2048), ("w_in", 2560)]
    for b in range(2):
        s += [("w_in", 5120 + b * 512), ("w_in", 6144 + b * 512), ("w_in", 4096 + b * 512)]
    s += [("w_out", c * 512) for c in range(4)]
    if dbg == 1 and mode == FULL:
        return s
    s += [("wq", c * 512) for c in range(4)]
    s += [("wo", c * 512) for c in range(4)]
    if dbg == 2 and mode == FULL:
        return s
    for b in range(11):
        s.append(("w_gate", b * 512))
        if mode == FULL:
            s.append(("w_up", b * 512))
    if mode == FULL:
        for cg in range(4):
            for r in range(4):
                s.append(("w_down", r, cg))
    return s


def build_program(NPRE, NOWN, DBG=None):
    nc = bass.Bass("TRN2", target_bir_lowering=False)
    p = Prog(nc)

    xo_d = nc.dram_tensor("xo", [NOWN * TT, D], F32, kind="ExternalInput").ap()
    xp_d = nc.dram_tensor("xp", [max(NPRE, 1) * TT, D], F32, kind="ExternalInput").ap()
    mem_d = nc.dram_tensor("mem", [MEM, D], F32, kind="ExternalInput").ap()
    cp_d = nc.dram_tensor("cpack", [128, NCP], F32, kind="ExternalInput").ap()
    cm_d = nc.dram_tensor("cmats", [128, 2, 128], F32, kind="ExternalInput").ap()
    y_d = nc.dram_tensor("y", [NOWN * TT, D], F32, kind="ExternalOutput").ap()
    wf = {n: nc.dram_tensor(n, list(s), F32, kind="ExternalInput").ap() for n, s in W_SHAPES.items()}
    def _nblk(n_, s_):
        return 16 if n_ == "w_down" else s_[1] // 512
    wb = {n: nc.dram_tensor(n + "_bf", [_nblk(n, s), 128, (11 if n == "w_down" else 16) * 512], BF16, kind="Internal").ap()
          for n, s in W_SHAPES.items()}

    xT_ap, xT = p.chunks("xT", KC, TT, F32)
    hT_ap, hT = p.chunks("hT", KC, TT, BF16)
    cpk = p.buf("cpk", [NCP], F32)
    cmt = p.buf("cmt", [2, 128], F32)
    identb = p.buf("identb", [128], BF16)
    maskb = p.buf("maskb", [128], BF16)
    onesb = p.buf("onesb", [128], BF16)
    onesf = p.buf("onesf", [128], F32)
    lbv = p.buf("lbv", [16], F32)
    kT_ap, kT = p.chunks("kT", KC, MEM, BF16)
    vtm_ap, vtm = p.chunks("vtm", 2, D, BF16)
    S = p.buf("S", [NH, 128], F32)
    ucar_ap, ucar = p.chunks("ucar", 8, 2, F32)
    gcar_ap, gcar = p.chunks("gcar", FC, 2, F32)
    NSLOT = 3
    slots = [p.buf("wslot%d" % i, [8192], BF16) for i in range(NSLOT)]
    identf = cmt.ap[:, 0, :]

    def cpc(off, n=1):
        return cpk.ap[:, off:off + n]

    A = p.alloc(0)
    ARENA = SB_LIMIT - A

    def at(off):
        assert off % 64 == 0
        return A + off
    K1 = 1024
    SCR = []
    for i_, base_ in enumerate((0, 77 * K1)):
        SCR.append({n_: p.buf("%s%d" % (n_, i_), [TT], F32, at(base_ + k_ * 2 * K1))
                    for k_, n_ in enumerate(("qs", "omf", "lgf", "Bc", "bp"))})
    QtT_ap, QtT = p.chunks("QtT", NH, TT, BF16, at(10 * K1))
    KtT_ap, KtT = p.chunks("KtT", NH, TT, BF16, at(18 * K1))
    V_ap, V = p.chunks("V", NSUB, 1024, BF16, at(26 * K1))
    dsc = p.buf("dsc", [3, NSUB, NH], F32, at(34 * K1))
    Ktm = p.buf("Ktm", [NH, 128], BF16, at(34 * K1 + 512))
    scm = p.buf("scm", [NH, 128], BF16, at(36 * K1 + 512))
    smid = p.buf("smid", [NH, 128], BF16, at(38 * K1 + 512))
    sqo = p.buf("sqo", [NH * 128], BF16, at(40 * K1 + 512))
    rto = p.buf("rto", [NH * 128], F32, at(42 * K1 + 512))
    sgc = [p.buf("sgc%d" % i, [TT], BF16, at(46 * K1 + 512 + i * K1)) for i in range(2)]
    ccs = [p.buf("ccs%d" % i, [TT], F32, at(77 * K1 + i * 2 * K1)) for i in range(4)]
    t0s4 = [p.buf("t0s%d" % i, [TT], F32, at(i * 2 * K1)) for i in range(4)]
    ubuf = [p.buf("ubuf%d" % i, [TT + 2], F32, at(50 * K1 + 512 + i * 2112)) for i in range(2)]
    off = 57 * K1
    sqn = [p.buf("sqn%d" % i, [TT], BF16, at(off + i * K1)) for i in range(2)]
    rstdn = p.buf("rstdn", [TT], F32, at(off + 2 * K1))
    off = 61 * K1
    mix_ap, mix = p.chunks("mix", KC, TT, BF16, at(off))
    assert off + 16 * K1 <= ARENA, (off, ARENA)
    xin = [p.buf("xin%d" % i, [1024], F32, at(i * 4 * K1)) for i in range(4)]
    ost = [p.buf("ost%d" % i, [1024], F32, at(16 * K1 + i * 4 * K1)) for i in range(2)]
    oaT_ap, oaT = p.chunks("oaT", KC, TT, BF16, at(24 * K1))
    pT = [[p.buf("pT%d_%d" % (i, m), [TT], BF16, at(40 * K1 + (2 * i + m) * K1)) for m in range(2)] for i in range(2)]
    rs_ = [p.buf("rs%d" % i, [TT], F32, at(44 * K1 + i * 2 * K1)) for i in range(2)]
    hid_ap, hid = p.chunks("hid", FC, TT, BF16, at(12 * K1))
    gbuf = [p.buf("gbuf%d" % i, [TT + 2], F32, at(77 * K1 + i * 2112)) for i in range(2)]
    tb = [p.buf("tb%d" % i, [TT], F32, at(77 * K1 + 2 * 2112 + i * 2 * K1)) for i in range(2)]
    assert 77 * K1 + 2 * 2112 + 4 * K1 <= ARENA, ARENA

    ps_t = nc.alloc_psum_tensor("ps", [128, 4096], F32).ap()
    ps_tb = ps_t.bitcast(BF16)
    psr = [p.reg("ps", i * 2048, (i + 1) * 2048, ps_t[:, i * 512:(i + 1) * 512], "ps%d" % i) for i in range(8)]
    ps_ptr = [0]

    def psa(n=1):
        b = ps_ptr[0]
        if b % n:
            b += n - b % n
        if b + n > 6:
            b = 0
        ps_ptr[0] = (b + n) % 6
        return b, psr[b:b + n]

    def psf(b, n=1):
        return ps_t[:, b * 512:(b + n) * 512]

    wregs = {}

    def spec_aps(spec):
        if spec[0] == "w_down":
            _, r, cg = spec
            sl = (slice(r * 1408, (r + 1) * 1408), slice(cg * 512, (cg + 1) * 512))
            kk = 11
        else:
            n, c0 = spec
            sl = (slice(None), slice(c0, c0 + 512))
            kk = 16
        name = spec[0]
        blk = (spec[2] * 4 + spec[1]) if name == "w_down" else spec[1] // 512
        return wf[name][sl], wb[name][blk], kk

    modes = [STATE] * max(NPRE - 1, 0) + ([WARM] if NPRE > 0 else []) + [FULL] * NOWN
    NSTATE = max(NPRE - 1, 0)
    full_seq = []
    for ti_, m in enumerate(modes):
        if ti_ == NSTATE:
            full_seq += [("wk", c * 512) for c in range(4)] + [("wv", c * 512) for c in range(4)]
        full_seq += tile_specs(m, DBG)
    ci = 0
    for spec in full_seq:
        if spec in wregs:
            continue
        r = p.reg("dram", 0, 0, None, str(spec))
        wregs[spec] = r
        src, dst, kk_ = spec_aps(spec)
        srcv_ = src.rearrange("(k p) n -> p k n", p=128)
        dstv_ = dst.rearrange("p (k n) -> p k n", k=kk_)
        p.dma("pool", "cast%d" % (ci % 8), [], [r], lambda e, srcv_=srcv_, dstv_=dstv_: e.dma_start(out=dstv_, in_=srcv_))
        ci += 1

    ws = {"next_dma": 0, "next_acq": 0}

    def ws_issue():
        i = ws["next_dma"]
        if i >= len(full_seq):
            return
        spec = full_seq[i]
        _, src, kk = spec_aps(spec)
        slot = slots[i % NSLOT]
        dst = slot.ap[:, 0:kk * 512]
        p.dma("sp", "wslot%d" % (i % NSLOT), [wregs[spec]], [slot], lambda e, dst=dst, srcv=src: e.dma_start(out=dst, in_=srcv))
        ws["next_dma"] = i + 1

    def ws_acquire(spec):
        i = ws["next_acq"]
        if ws["next_dma"] == 0:
            for _ in range(NSLOT):
                ws_issue()
        assert full_seq[i] == spec, (i, full_seq[i], spec)
        ws["next_acq"] = i + 1
        assert ws["next_dma"] > i
        slot = slots[i % NSLOT]
        kk = 11 if spec[0] == "w_down" else 16
        return slot, slot.ap[:, 0:kk * 512].rearrange("p (k n) -> p k n", k=kk)

    def ws_release():
        ws_issue()

    p.dma("act", "ld_cp", [], [cpk], lambda e: e.dma_start(out=cpk.ap, in_=cp_d))
    p.dma("act", "ld_cm", [], [cmt], lambda e: e.dma_start(out=cmt.ap, in_=cm_d))
    p.op("dve", [cmt], [identb], lambda e: e.tensor_copy(identb.ap, cmt.ap[:, 0, :]))
    p.op("dve", [cmt], [maskb], lambda e: e.tensor_copy(maskb.ap, cmt.ap[:, 1, :]))
    p.op("dve", [], [onesb], lambda e: e.memset(onesb.ap, 1.0))
    p.op("dve", [], [onesf], lambda e: e.memset(onesf.ap, 1.0))
    p.op("dve", [], [S], lambda e: e.memset(S.ap, 0.0))
    p.op("dve", [], ucar, lambda e: e.memset(ucar_ap, 0.0))
    p.op("dve", [], gcar, lambda e: e.memset(gcar_ap, 0.0))
    p.op("dve", [cpk], [lbv], lambda e: e.tensor_tensor(lbv.ap[:, 0:8], cpc(CP_LB, 8), cpc(CP_LB + 8, 8), ALU.subtract))
    p.op("act", [lbv], [lbv], lambda e: e.activation(lbv.ap[:, 0:8], lbv.ap[:, 0:8], AF.Sigmoid))
    p.op("dve", [lbv], [lbv], lambda e: e.tensor_scalar(lbv.ap[:, 8:16], lbv.ap[:, 0:8], -1.0, 1.0, ALU.mult, ALU.add))

    eps_ap = cpc(CP_EPS)
    flip = {"a": 0}

    def evac_eng():
        flip["a"] ^= 1
        return "act" if flip["a"] else "dve"

    def copy_op(eng, reads, writes, out_ap, in_ap):
        if eng == "act":
            p.op("act", reads, writes, lambda e: e.copy(out_ap, in_ap))
        else:
            p.op(eng, reads, writes, lambda e: e.tensor_copy(out_ap, in_ap))

    pref = {"n": 0}

    def issue_x(src_d, row0, i):
        sub, half = divmod(i, 2)
        sl = xin[i % 4]
        srcv = src_d[row0 + sub * 128:row0 + (sub + 1) * 128, half * 1024:(half + 1) * 1024]
        p.dma("sp", "xin%d" % (i % 4), [], [sl], lambda e, sl=sl, srcv=srcv: e.dma_start(out=sl.ap, in_=srcv))

    def prefetch_x(nxt):
        if nxt is None:
            return
        for i in range(3):
            issue_x(nxt[0], nxt[1], i)
        pref["n"] = 3

    def load_x(src_d, row0, nsub):
        issued = pref["n"]
        pref["n"] = 0
        n = nsub * 2
        for i in range(n):
            while issued < n and issued < i + 4:
                issue_x(src_d, row0, issued)
                issued += 1
            sub, half = divmod(i, 2)
            sl = xin[i % 4]
            b, pr = psa(2)
            def tr(e, sl=sl, b=b):
                last = None
                for j in range(8):
                    last = e.transpose(ps_t[:, b * 512 + j * 128:b * 512 + (j + 1) * 128], sl.ap[:, j * 128:(j + 1) * 128], identf)
                return last
            p.op("pe", [sl, cmt], pr, tr)
            outv = xT_ap[:, half * 8:(half + 1) * 8, sub * 128:(sub + 1) * 128]
            inv = psf(b, 2).rearrange("p (j t) -> p j t", j=8)
            copy_op(evac_eng(), pr, xT[half * 8:(half + 1) * 8], outv, inv)

    xnb = [p.buf("xnb%d" % i, [1024], BF16, at(61 * K1 + i * 2 * K1)) for i in range(2)]
    sjunk = p.buf("sjunk", [1024], BF16, at(65 * K1))
    acc2 = p.buf("acc2", [4], F32, at(67 * K1))

    def load_norm_state(src_d, row0):
        issued = pref["n"]
        pref["n"] = 0
        n_ = NSUB * 2
        for sub in range(NSUB):
            while issued < n_ and issued < 2 * sub + 4:
                issue_x(src_d, row0, issued)
                issued += 1
            s0, s1 = xin[(sub * 2) % 4], xin[(sub * 2 + 1) % 4]
            for half, sl in enumerate((s0, s1)):
                p.op("act", [sl], [sjunk, acc2], lambda e, sl=sl, half=half: e.activation(sjunk.ap, sl.ap, AF.Square, accum_out=acc2.ap[:, half:half + 1]))
            p.op("dve", [acc2], [acc2], lambda e: e.tensor_tensor(acc2.ap[:, 2:3], acc2.ap[:, 0:1], acc2.ap[:, 1:2], ALU.add))
            p.op("act", [acc2, cpk], [acc2], lambda e: e.activation(acc2.ap[:, 3:4], acc2.ap[:, 2:3], AF.Ln, bias=eps_ap, scale=1.0 / D))
            p.op("act", [acc2], [acc2], lambda e: e.activation(acc2.ap[:, 3:4], acc2.ap[:, 3:4], AF.Exp, scale=-0.5))
            for half, sl in enumerate((s0, s1)):
                xb = xnb[half]
                p.op("dve", [sl, acc2], [xb], lambda e, sl=sl, xb=xb: e.tensor_scalar(xb.ap, sl.ap, acc2.ap[:, 3:4], None, ALU.mult))
                bt, prt = psa(1)
                def tr(e, xb=xb, bt=bt):
                    last = None
                    for j in range(8):
                        last = e.transpose(ps_tb[:, bt * 1024 + j * 128:bt * 1024 + (j + 1) * 128], xb.ap[:, j * 128:(j + 1) * 128], identb.ap)
                    return last
                p.op("pe", [xb, identb], prt, tr)
                outv = hT_ap[:, half * 8:(half + 1) * 8, sub * 128:(sub + 1) * 128]
                inv = ps_tb[:, bt * 1024:(bt + 1) * 1024].rearrange("p (j t) -> p j t", j=8)
                wbc = cpc(CP_NW + half * 8, 8).unsqueeze(2).broadcast_to([128, 8, 128])
                p.op("dve", prt + [cpk], hT[half * 8:(half + 1) * 8], lambda e, outv=outv, inv=inv, wbc=wbc: e.tensor_tensor(outv, inv, wbc, ALU.mult))

    stat_ptr = [0]

    def norm_stats_begin():
        b = 6 + stat_ptr[0] % 2
        stat_ptr[0] += 1
        return {"b": b, "pr": [psr[b]], "n": 0}

    def norm_stats_add(st, kc, ntok, defer=False, c0=0):
        sq = sqn[st["n"] % 2]
        p.op("act", [xT[kc]], [sq], lambda e: e.activation(sq.ap[:, 0:ntok], xT_ap[:, kc, c0:c0 + ntok], AF.Square))
        first = st["n"] == 0
        last = st["n"] == KC - 1
        b = st["b"]
        def mm():
            p.op("pe", [sq, onesb], st["pr"], lambda e: e.matmul(ps_t[:, b * 512:b * 512 + ntok], onesb.ap, sq.ap[:, 0:ntok], start=first, stop=last))
        st["n"] += 1
        if not defer:
            mm()
            return
        prev = st.get("pend")
        st["pend"] = mm
        if prev is not None:
            prev()

    def norm_stats_flush(st):
        prev = st.get("pend")
        if prev is not None:
            prev()
            st["pend"] = None

    def norm_apply(st, widx, ntok, out_ap, out_regs, c0=0):
        norm_stats_flush(st)
        b = st["b"]
        p.op("act", st["pr"] + [cpk], [rstdn], lambda e: e.activation(rstdn.ap[:, 0:ntok], ps_t[:, b * 512:b * 512 + ntok], AF.Ln, bias=eps_ap, scale=1.0 / D))
        p.op("act", [rstdn], [rstdn], lambda e: e.activation(rstdn.ap[:, 0:ntok], rstdn.ap[:, 0:ntok], AF.Exp, scale=-0.5))
        for kc in range(KC):
            p.op("dve", [xT[kc], rstdn, cpk], [out_regs[kc]],
                 lambda e, kc=kc: e.scalar_tensor_tensor(out_ap[:, kc, c0:c0 + ntok], xT_ap[:, kc, c0:c0 + ntok], cpc(CP_NW + 16 * widx + kc), rstdn.ap[:, 0:ntok], ALU.mult, ALU.mult))

    def norm_full(widx, ntok, out_ap, out_regs):
        st = norm_stats_begin()
        for kc in range(KC):
            norm_stats_add(st, kc, ntok)
        norm_apply(st, widx, ntok, out_ap, out_regs)

    def proj_fm(slot, sv, j, rhs_ap, rhs_regs, ntok, nk=KC, fine=False, c0=0):
        b, pr = psa(1)
        if fine:
            for kc in range(nk):
                p.op("pe", [slot, rhs_regs[kc]], pr,
                     lambda e, kc=kc: e.matmul(ps_t[:, b * 512:b * 512 + ntok], sv[:, kc, j * 128:(j + 1) * 128], rhs_ap[:, kc, c0:c0 + ntok], start=(kc == 0), stop=(kc == nk - 1)))
            return b, pr
        def mm(e):
            last = None
            for kc in range(nk):
                last = e.matmul(ps_t[:, b * 512:b * 512 + ntok], sv[:, kc, j * 128:(j + 1) * 128], rhs_ap[:, kc, c0:c0 + ntok], start=(kc == 0), stop=(kc == nk - 1))
            return last
        p.op("pe", [slot] + list(rhs_regs), pr, mm)
        return b, pr

    def hgrn_prep_head(h, slot_f, sv_f, j, full, sc, fine=False, qcs=slice(0, TT)):
        omf, lgf, Bc, bp = sc["omf"], sc["lgf"], sc["Bc"], sc["bp"]
        bf_, prf = proj_fm(slot_f, sv_f, j, hT_ap, hT, TT, fine=fine)
        p.op("act", prf, [omf], lambda e: e.activation(omf.ap, psf(bf_), AF.Exp))
        yield
        p.op("act", [omf], [omf], lambda e: e.activation(omf.ap, omf.ap, AF.Ln, bias=1.0))
        yield
        p.op("act", [omf], [omf], lambda e: e.activation(omf.ap, omf.ap, AF.Exp, scale=-1.0))
        yield
        p.op("dve", [omf, lbv], [omf], lambda e: e.tensor_scalar(omf.ap, omf.ap, lbv.ap[:, 8 + h:9 + h], None, ALU.mult))
        yield
        p.op("act", [omf], [lgf], lambda e: e.activation(lgf.ap, omf.ap, AF.Ln, bias=1.0, scale=-1.0))
        yield
        def scan(e):
            last = None
            for s_ in range(NSUB):
                sl = slice(s_ * 128, (s_ + 1) * 128)
                last = e.tensor_tensor_scan(Bc.ap[:, sl], onesf.ap, lgf.ap[:, sl], 0.0, ALU.mult, ALU.add)
            return last
        p.op("dve", [lgf, onesf], [Bc], scan)
        yield
        B3 = Bc.ap.rearrange("p (s t) -> p s t", s=NSUB)
        bp3 = bp.ap.rearrange("p (s t) -> p s t", s=NSUB)
        p.op("dve", [Bc], [bp], lambda e: e.tensor_tensor(bp3, B3, B3[:, :, 63:64].broadcast_to([128, NSUB, 128]), ALU.subtract))
        yield
        p.op("act", [Bc], [dsc], lambda e: e.activation(dsc.ap[:, 0, :, h], B3[:, :, 127], AF.Exp))
        p.op("act", [Bc], [dsc], lambda e: e.activation(dsc.ap[:, 2, :, h], B3[:, :, 63], AF.Exp))
        p.op("act", [bp], [dsc], lambda e: e.activation(dsc.ap[:, 1, :, h], bp3[:, :, 127], AF.Exp))
        yield
        if full:
            p.op("act", [bp], [lgf], lambda e: e.activation(lgf.ap, bp.ap, AF.Exp))
        p.op("act", [bp], [Bc], lambda e: e.activation(Bc.ap, bp.ap, AF.Exp, scale=-1.0))
        yield
        if full:
            p.op("dve", [QtT[h], lgf], [QtT[h]], lambda e: e.tensor_tensor(QtT_ap[:, h, qcs], QtT_ap[:, h, qcs], lgf.ap[:, qcs], ALU.mult))
        p.op("dve", [omf, Bc], [KtT[h]], lambda e: e.tensor_tensor(KtT_ap[:, h, :], omf.ap, Bc.ap, ALU.mult))
        yield

    def run_interleaved(gens):
        act_ = list(gens)
        while act_:
            for g_ in list(act_):
                try:
                    next(g_)
                except StopIteration:
                    act_.remove(g_)

    def v_proj(slot, sv, blk):
        for sub in range(NSUB):
            b, pr = psa(1)
            def mm(e, b=b, sub=sub):
                last = None
                for kc in range(KC):
                    last = e.matmul(psf(b), hT_ap[:, kc, sub * 128:(sub + 1) * 128], sv[:, kc, :], start=(kc == 0), stop=(kc == KC - 1))
                return last
            p.op("pe", [slot] + hT, pr, mm)
            copy_op(evac_eng(), pr, [V[sub]], V_ap[:, sub, blk * 512:(blk + 1) * 512], psf(b))

    def hgrn_state_tr(sub, kbuf):
        cols = slice(sub * 128, (sub + 1) * 128)
        bt, prt = psa(1)
        def trk(e):
            last = None
            for h in range(NH):
                last = e.transpose(ps_tb[:, bt * 1024 + h * 128:bt * 1024 + (h + 1) * 128], KtT_ap[:, h, cols], identb.ap)
            return last
        p.op("pe", KtT + [identb], prt, trk)
        copy_op(evac_eng(), prt, [kbuf], kbuf.ap.rearrange("p h k -> p (h k)"), ps_tb[:, bt * 1024:(bt + 1) * 1024])

    def hgrn_state_upd(sub, kbuf):
        bp_, prp = psa(2)
        def pm(e):
            last = None
            for h in range(NH):
                last = e.matmul(ps_t[:, bp_ * 512 + h * 128:bp_ * 512 + (h + 1) * 128], kbuf.ap[:, h, :], V_ap[:, sub, h * 128:(h + 1) * 128], start=True, stop=True)
            return last
        p.op("pe", [kbuf, V[sub]], prp, pm)
        p.op("dve", [S, dsc], [S], lambda e: e.tensor_tensor(S.ap, S.ap, dsc.ap[:, 0, sub, :].unsqueeze(2).broadcast_to([128, NH, 128]), ALU.mult))
        def su(e):
            last = None
            for h in range(NH):
                last = e.scalar_tensor_tensor(S.ap[:, h, :], ps_t[:, bp_ * 512 + h * 128:bp_ * 512 + (h + 1) * 128], dsc.ap[:, 1, sub, h:h + 1], S.ap[:, h, :], ALU.mult, ALU.add)
            return last
        p.op("dve", prp + [S, dsc], [S], su)

    def hgrn_subtile(sub, full):
        cols = slice(sub * 128, (sub + 1) * 128)
        bt, prt = psa(1)
        def trk(e):
            last = None
            for h in range(NH):
                last = e.transpose(ps_tb[:, bt * 1024 + h * 128:bt * 1024 + (h + 1) * 128], KtT_ap[:, h, cols], identb.ap)
            return last
        p.op("pe", KtT + [identb], prt, trk)
        copy_op(evac_eng(), prt, [Ktm], Ktm.ap.rearrange("p h k -> p (h k)"), ps_tb[:, bt * 1024:(bt + 1) * 1024])
        if full:
            p.op("dve", [S, dsc], [smid], lambda e: e.tensor_tensor(smid.ap, S.ap, dsc.ap[:, 2, sub, :].unsqueeze(2).broadcast_to([128, NH, 128]), ALU.mult))
            bs, prs = psa(2)
            def sc(e):
                last = None
                for h in range(NH):
                    last = e.matmul(ps_t[:, bs * 512 + h * 128:bs * 512 + (h + 1) * 128], KtT_ap[:, h, cols], QtT_ap[:, h, cols], start=True, stop=True)
                return last
            p.op("pe", KtT + QtT, prs, sc)
            p.op("dve", prs + [maskb], [scm], lambda e: e.tensor_tensor(scm.ap, psf(bs, 2).rearrange("p (h t) -> p h t", h=NH), maskb.ap.unsqueeze(1).broadcast_to([128, NH, 128]), ALU.mult))
            bo, pro = psa(2)
            def om(e):
                last = None
                for h in range(NH):
                    o_ = ps_t[:, bo * 512 + h * 128:bo * 512 + (h + 1) * 128]
                    e.matmul(o_, V_ap[:, sub, h * 128:(h + 1) * 128], scm.ap[:, h, :], start=True, stop=False)
                    last = e.matmul(o_, smid.ap[:, h, :], QtT_ap[:, h, cols], start=False, stop=True)
                return last
            p.op("pe", [V[sub], scm, smid] + QtT, pro, om)
        bp_, prp = psa(2)
        def pm(e):
            last = None
            for h in range(NH):
                last = e.matmul(ps_t[:, bp_ * 512 + h * 128:bp_ * 512 + (h + 1) * 128], Ktm.ap[:, h, :], V_ap[:, sub, h * 128:(h + 1) * 128], start=True, stop=True)
            return last
        p.op("pe", [Ktm, V[sub]], prp, pm)
        p.op("dve", [S, dsc], [S], lambda e: e.tensor_tensor(S.ap, S.ap, dsc.ap[:, 0, sub, :].unsqueeze(2).broadcast_to([128, NH, 128]), ALU.mult))
        def su(e):
            last = None
            for h in range(NH):
                last = e.scalar_tensor_tensor(S.ap[:, h, :], ps_t[:, bp_ * 512 + h * 128:bp_ * 512 + (h + 1) * 128], dsc.ap[:, 1, sub, h:h + 1], S.ap[:, h, :], ALU.mult, ALU.add)
            return last
        p.op("dve", prp + [S, dsc], [S], su)
        if full:
            p.op("act", pro, [sqo], lambda e: e.activation(sqo.ap, psf(bo, 2), AF.Square))
            bq, prq = psa(2)
            def ssm(e):
                e.matmul(psf(bq), onesb.ap, sqo.ap[:, 0:512], start=True, stop=True)
                return e.matmul(psf(bq + 1), onesb.ap, sqo.ap[:, 512:1024], start=True, stop=True)
            p.op("pe", [sqo, onesb], prq, ssm)
            p.op("act", prq + [cpk], [rto], lambda e: e.activation(rto.ap, psf(bq, 2), AF.Ln, bias=eps_ap, scale=1.0 / 128))
            p.op("act", [rto], [rto], lambda e: e.activation(rto.ap, rto.ap, AF.Exp, scale=-0.5))
            p.op("dve", pro + [rto], [sqo], lambda e: e.tensor_tensor(sqo.ap, psf(bo, 2), rto.ap, ALU.mult))
            p.op("dve", [sqo, cpk] + mix[0:NH], mix[0:NH],
                 lambda e: e.scalar_tensor_tensor(mix_ap[:, 0:NH, cols], sqo.ap.rearrange("p (h t) -> p h t", h=NH), cpc(CP_HNW), mix_ap[:, 0:NH, cols], ALU.mult, ALU.mult))

    out_toks = []
    ost_ptr = [0]

    tile_no = [0]

    def emit_output(out_row0):
        for sub in range(NSUB):
            for half in range(2):
                b, pr = psa(2)
                def tro(e, b=b, sub=sub, half=half):
                    last = None
                    for j in range(8):
                        last = e.transpose(ps_t[:, b * 512 + j * 128:b * 512 + (j + 1) * 128], xT_ap[:, half * 8 + j, sub * 128:(sub + 1) * 128], identf)
                    return last
                p.op("pe", xT[half * 8:(half + 1) * 8] + [cmt], pr, tro)
                os_ = ost[ost_ptr[0] % 2]
                ost_ptr[0] += 1
                copy_op(evac_eng(), pr, [os_], os_.ap, psf(b, 2))
                dstv = y_d[out_row0 + sub * 128:out_row0 + (sub + 1) * 128, half * 1024:(half + 1) * 1024]
                out_toks.append(p.dma("act", "ost%d" % ((ost_ptr[0] - 1) % 2), [os_], [], lambda e, os_=os_, dstv=dstv: e.dma_start(out=dstv, in_=os_.ap)))

    def do_tile(src_d, row0, mode, out_row0, nxt=None):
        full = mode != STATE
        c0, n = (0, TT) if mode != WARM else (TT - 128, 128)
        cs = slice(c0, c0 + n)
        tn = tile_no[0]
        tile_no[0] += 1
        p.phase = "t%d.load" % tn
        if mode == STATE:
            load_norm_state(src_d, row0)
        else:
            load_x(src_d, row0, NSUB)
            if DBG == 0 and mode == FULL:
                emit_output(out_row0)
                return
            p.phase = "t%d.norm1" % tn
            norm_full(0, TT, hT_ap, hT)
        p.phase = "t%d.prep" % tn
        if full:
            for hg in range(2):
                slot_q, sv_q = ws_acquire(("w_in", hg * 512))
                for j in range(4):
                    h = hg * 4 + j
                    b, pr = proj_fm(slot_q, sv_q, j, hT_ap, hT, n, fine=(h == 0), c0=c0)
                    p.op("act", pr, [QtT[h]], lambda e, b=b, h=h: e.activation(QtT_ap[:, h, cs], ps_t[:, b * 512:b * 512 + n], AF.Silu))
                ws_release()
            for hg in range(2):
                slot_g, sv_g = ws_acquire(("w_in", 3072 + hg * 512))
                for j in range(4):
                    h = hg * 4 + j
                    b, pr = proj_fm(slot_g, sv_g, j, hT_ap, hT, n, c0=c0)
                    p.op("act", pr, [mix[h]], lambda e, b=b, h=h: e.activation(mix_ap[:, h, cs], ps_t[:, b * 512:b * 512 + n], AF.Silu))
                ws_release()
        for hg in range(2):
            slot_f, sv_f = ws_acquire(("w_in", 1024 + hg * 512))
            for jj in range(0, 4, 2):
                run_interleaved([hgrn_prep_head(hg * 4 + j, slot_f, sv_f, j, full, SCR[j % 2], fine=(not full and hg == 0 and j == 0), qcs=cs)
                                 for j in (jj, jj + 1)])
            ws_release()
        p.phase = "t%d.vproj" % tn
        for blk in range(2):
            slot, sv = ws_acquire(("w_in", 2048 + blk * 512))
            v_proj(slot, sv, blk)
            ws_release()
        p.phase = "t%d.subtiles" % tn
        if mode == STATE:
            kb = [Ktm, scm]
            hgrn_state_tr(0, kb[0])
            for sub in range(NSUB):
                if sub + 1 < NSUB:
                    hgrn_state_tr(sub + 1, kb[(sub + 1) % 2])
                hgrn_state_upd(sub, kb[sub % 2])
        else:
            for sub in range(NSUB):
                hgrn_subtile(sub, mode == FULL or (mode == WARM and sub == NSUB - 1))
        if not full:
            prefetch_x(nxt)
            return
        p.phase = "t%d.sconv" % tn
        for blk in range(2):
            slot_c, sv_c = ws_acquire(("w_in", 5120 + blk * 512))
            for j in range(4):
                bc_, prc = proj_fm(slot_c, sv_c, j, hT_ap, hT, n, c0=c0)
                p.op("act", prc, [ccs[j]], lambda e, bc_=bc_, j=j: e.copy(ccs[j].ap[:, 0:n], ps_t[:, bc_ * 512:bc_ * 512 + n]))
            ws_release()
            slot_h, sv_h = ws_acquire(("w_in", 6144 + blk * 512))
            for j in range(4):
                c = blk * 4 + j
                bh, prh = proj_fm(slot_h, sv_h, j, hT_ap, hT, n, c0=c0)
                ub = ubuf[c % 2]
                t0s = t0s4[j]
                p.op("dve", [ucar[c]], [ub], lambda e, ub=ub, c=c: e.tensor_copy(ub.ap[:, 0:2], ucar_ap[:, c, :]))
                p.op("dve", prh + [ccs[j]], [ub], lambda e, ub=ub, bh=bh, j=j: e.tensor_tensor(ub.ap[:, 2:n + 2], ps_t[:, bh * 512:bh * 512 + n], ccs[j].ap[:, 0:n], ALU.mult))
                p.op("dve", [ub], [ucar[c]], lambda e, ub=ub, c=c: e.tensor_copy(ucar_ap[:, c, :], ub.ap[:, n:n + 2]))
                p.op("act", [ub, cpk], [t0s], lambda e, ub=ub, c=c, t0s=t0s: e.activation(t0s.ap[:, 0:n], ub.ap[:, 2:n + 2], AF.Identity, scale=cpc(CP_SCW + 16 + c)))
                p.op("dve", [ub, t0s, cpk], [t0s], lambda e, ub=ub, c=c, t0s=t0s: e.scalar_tensor_tensor(t0s.ap[:, 0:n], ub.ap[:, 1:n + 1], cpc(CP_SCW + 8 + c), t0s.ap[:, 0:n], ALU.mult, ALU.add))
                p.op("dve", [ub, t0s, cpk], [t0s], lambda e, ub=ub, c=c, t0s=t0s: e.scalar_tensor_tensor(t0s.ap[:, 0:n], ub.ap[:, 0:n], cpc(CP_SCW + c), t0s.ap[:, 0:n], ALU.mult, ALU.add))
            ws_release()
            slot_b, sv_b = ws_acquire(("w_in", 4096 + blk * 512))
            for j in range(4):
                c = blk * 4 + j
                bb, prb = proj_fm(slot_b, sv_b, j, hT_ap, hT, n, c0=c0)
                p.op("dve", prb + [t0s4[j]], [mix[8 + c]], lambda e, bb=bb, c=c, j=j: e.tensor_tensor(mix_ap[:, 8 + c, cs], ps_t[:, bb * 512:bb * 512 + n], t0s4[j].ap[:, 0:n], ALU.mult))
            ws_release()

        def resid_proj(wname, act_ap, act_regs, st):
            for c in range(4):
                slot, sv = ws_acquire((wname, c * 512))
                for j in range(4):
                    ch = c * 4 + j
                    b, pr = proj_fm(slot, sv, j, act_ap, act_regs, n, c0=c0)
                    p.op("dve", pr + [xT[ch]], [xT[ch]], lambda e, b=b, ch=ch: e.tensor_tensor(xT_ap[:, ch, cs], ps_t[:, b * 512:b * 512 + n], xT_ap[:, ch, cs], ALU.add))
                    norm_stats_add(st, ch, n, defer=True, c0=c0)
                ws_release()

        p.phase = "t%d.wout" % tn
        st = norm_stats_begin()
        resid_proj("w_out", mix_ap, mix, st)
        if DBG == 1 and mode == FULL:
            emit_output(out_row0)
            return
        norm_apply(st, 1, n, hT_ap, hT, c0=c0)
        p.phase = "t%d.wq" % tn
        qT_ap, qT = mix_ap, mix
        for c in range(4):
            slot, sv = ws_acquire(("wq", c * 512))
            for j in range(4):
                ch = c * 4 + j
                b, pr = proj_fm(slot, sv, j, hT_ap, hT, n, fine=(ch == 0), c0=c0)
                p.op("act", pr, [qT[ch]], lambda e, b=b, ch=ch: e.activation(qT_ap[:, ch, cs], ps_t[:, b * 512:b * 512 + n], AF.Identity, scale=float(512 ** -0.5)))
            ws_release()
        def att_scores(hd):
            pts = pT[hd % 2]
            for mc in range(2):
                b, pr = psa(1)
                def smm(e, b=b, mc=mc, hd=hd):
                    last = None
                    for dc in range(4):
                        kc = hd * 4 + dc
                        last = e.matmul(ps_t[:, b * 512:b * 512 + n], kT_ap[:, kc, mc * 128:(mc + 1) * 128], qT_ap[:, kc, cs], start=(dc == 0), stop=(dc == 3))
                    return last
                p.op("pe", kT[hd * 4:hd * 4 + 4] + qT[hd * 4:hd * 4 + 4], pr, smm)
                p.op("act", pr, [pts[mc]], lambda e, b=b, mc=mc, pts=pts: e.activation(pts[mc].ap[:, 0:n], ps_t[:, b * 512:b * 512 + n], AF.Exp))

        def att_rest(hd):
            pts = pT[hd % 2]
            b, pr = psa(1)
            def summ(e, b=b, pts=pts):
                e.matmul(ps_t[:, b * 512:b * 512 + n], onesb.ap, pts[0].ap[:, 0:n], start=True, stop=False)
                return e.matmul(ps_t[:, b * 512:b * 512 + n], onesb.ap, pts[1].ap[:, 0:n], start=False, stop=True)
            p.op("pe", [pts[0], pts[1], onesb], pr, summ)
            rsb = rs_[hd % 2]
            p.op("act", pr, [rsb], lambda e, b=b, rsb=rsb: e.activation(rsb.ap[:, 0:n], ps_t[:, b * 512:b * 512 + n], AF.Ln))
            p.op("act", [rsb], [rsb], lambda e, rsb=rsb: e.activation(rsb.ap[:, 0:n], rsb.ap[:, 0:n], AF.Exp, scale=-1.0))
            for dc in range(4):
                kc = hd * 4 + dc
                b, pr = psa(1)
                def pv(e, b=b, kc=kc, pts=pts):
                    e.matmul(ps_t[:, b * 512:b * 512 + n], vtm_ap[:, 0, kc * 128:(kc + 1) * 128], pts[0].ap[:, 0:n], start=True, stop=False)
                    return e.matmul(ps_t[:, b * 512:b * 512 + n], vtm_ap[:, 1, kc * 128:(kc + 1) * 128], pts[1].ap[:, 0:n], start=False, stop=True)
                p.op("pe", [vtm[0], vtm[1], pts[0], pts[1]], pr, pv)
                p.op("dve", pr + [rsb], [oaT[kc]], lambda e, b=b, kc=kc, rsb=rsb: e.tensor_tensor(oaT_ap[:, kc, cs], ps_t[:, b * 512:b * 512 + n], rsb.ap[:, 0:n], ALU.mult))

        p.phase = "t%d.att" % tn
        att_scores(0)
        for hd in range(4):
            if hd + 1 < 4:
                att_scores(hd + 1)
            att_rest(hd)
        p.phase = "t%d.wo" % tn
        st = norm_stats_begin()
        resid_proj("wo", oaT_ap, oaT, st)
        if DBG == 2 and mode == FULL:
            emit_output(out_row0)
            return
        norm_apply(st, 2, n, hT_ap, hT, c0=c0)
        p.phase = "t%d.ffn" % tn
        prefetch_x(nxt)
        for blk in range(11):
            slot_g, sv_g = ws_acquire(("w_gate", blk * 512))
            if mode == FULL:
                slot_u, sv_u = ws_acquire(("w_up", blk * 512))
            for j in range(4):
                c = blk * 4 + j
                bg, prg = proj_fm(slot_g, sv_g, j, hT_ap, hT, n, fine=(c == 0), c0=c0)
                gb = gbuf[c % 2]
                t = tb[c % 2]
                p.op("dve", [gcar[c]], [gb], lambda e, gb=gb, c=c: e.tensor_copy(gb.ap[:, 0:2], gcar_ap[:, c, :]))
                p.op("act", prg, [gb], lambda e, gb=gb, bg=bg: e.copy(gb.ap[:, 2:n + 2], ps_t[:, bg * 512:bg * 512 + n]))
                p.op("dve", [gb], [gcar[c]], lambda e, gb=gb, c=c: e.tensor_copy(gcar_ap[:, c, :], gb.ap[:, n:n + 2]))
                if mode != FULL:
                    continue
                p.op("act", prg + [cpk], [t], lambda e, t=t, bg=bg, c=c: e.activation(t.ap, psf(bg), AF.Identity, bias=cpc(CP_FCB + c), scale=cpc(CP_FCW + 88 + c)))
                bu, pru = proj_fm(slot_u, sv_u, j, hT_ap, hT, TT)
                p.op("dve", [gb, t, cpk], [t], lambda e, gb=gb, t=t, c=c: e.scalar_tensor_tensor(t.ap, gb.ap[:, 1:TT + 1], cpc(CP_FCW + 44 + c), t.ap, ALU.mult, ALU.add))
                p.op("dve", [gb, t, cpk], [t], lambda e, gb=gb, t=t, c=c: e.scalar_tensor_tensor(t.ap, gb.ap[:, 0:TT], cpc(CP_FCW + c), t.ap, ALU.mult, ALU.add))
                p.op("act", [t], [t], lambda e, t=t: e.activation(t.ap, t.ap, AF.Silu))
                p.op("dve", pru + [t], [hid[c]], lambda e, t=t, bu=bu, c=c: e.tensor_tensor(hid_ap[:, c, :], psf(bu), t.ap, ALU.mult))
            ws_release()
            if mode == FULL:
                ws_release()
        if mode != FULL:
            p.op("dve", gcar + [cpk], gcar, lambda e: e.tensor_scalar(gcar_ap, gcar_ap, cpc(CP_FLAG), None, ALU.mult))
            p.op("dve", ucar + [cpk], ucar, lambda e: e.tensor_scalar(ucar_ap, ucar_ap, cpc(CP_FLAG), None, ALU.mult))
            p.op("dve", [S, cpk], [S], lambda e: e.tensor_scalar(S.ap, S.ap, cpc(CP_FLAG), None, ALU.mult))
            return
        p.phase = "t%d.down" % tn
        st = norm_stats_begin()
        for cg in range(4):
            b4, pr4 = psa(4)
            for r in range(4):
                slot, sv = ws_acquire(("w_down", r, cg))
                for j in range(4):
                    def dm(e, b4=b4, r=r, sv=sv, j=j):
                        last = None
                        for kk in range(11):
                            last = e.matmul(psf(b4 + j), sv[:, kk, j * 128:(j + 1) * 128], hid_ap[:, r * 11 + kk, :], start=(r == 0 and kk == 0), stop=(r == 3 and kk == 10))
                        return last
                    p.op("pe", [slot] + hid[r * 11:(r + 1) * 11], [pr4[j]], dm)
                ws_release()
            for j in range(4):
                ch = cg * 4 + j
                p.op("dve", [pr4[j], xT[ch]], [xT[ch]], lambda e, b4=b4, j=j, ch=ch: e.tensor_tensor(xT_ap[:, ch, :], psf(b4 + j), xT_ap[:, ch, :], ALU.add))
                norm_stats_add(st, ch, TT, defer=True)
        p.phase = "t%d.out" % tn
        if DBG != 3:
            norm_apply(st, 3, TT, xT_ap, xT)
        emit_output(out_row0)

    def mem_kv():
        p.phase = "mem"
        load_x(mem_d, 0, 2)
        norm_full(4, MEM, hT_ap, hT)
        for c in range(4):
            slot, sv = ws_acquire(("wk", c * 512))
            for j in range(4):
                b, pr = proj_fm(slot, sv, j, hT_ap, hT, MEM)
                ch = c * 4 + j
                copy_op(evac_eng(), pr, [kT[ch]], kT_ap[:, ch, :], ps_t[:, b * 512:b * 512 + MEM])
            ws_release()
        for c in range(4):
            slot, sv = ws_acquire(("wv", c * 512))
            for ms in range(2):
                b, pr = psa(1)
                def mm(e, b=b, ms=ms, sv=sv):
                    last = None
                    for kc in range(KC):
                        last = e.matmul(psf(b), hT_ap[:, kc, ms * 128:(ms + 1) * 128], sv[:, kc, :], start=(kc == 0), stop=(kc == KC - 1))
                    return last
                p.op("pe", [slot] + hT, pr, mm)
                copy_op(evac_eng(), pr, [vtm[ms]], vtm_ap[:, ms, c * 512:(c + 1) * 512], psf(b))
            ws_release()


    tiles = [(xp_d, t * TT, modes[t], None) for t in range(NPRE)] + [(xo_d, t * TT, FULL, t * TT) for t in range(NOWN)]
    for ti, (src_, r0_, m_, o_) in enumerate(tiles):
        if ti == NSTATE:
            mem_kv()
        nxt = tiles[ti + 1][0:2] if ti + 1 < len(tiles) else None
        if DBG is not None or ti + 1 == NSTATE:
            nxt = None
        do_tile(src_, r0_, m_, o_, nxt)
    assert ws["next_acq"] == len(full_seq)
    last = {}
    for k, v in out_toks:
        last[k] = max(last.get(k, 0), v)
    p.wait_all("act", list(last.items()))
    p.emit()
    global _LAST_PROG
    _LAST_PROG = p
    return nc


_CACHE = {}
_LAST_PROG = None


def run_cores(inp, x, mem, NPRE, NOWN, n_split, DBG=None):
    B, S_, _ = x.shape
    key = (NPRE, NOWN, DBG)
    if key not in _CACHE:
        _CACHE[key] = build_program(NPRE, NOWN, DBG)
    nc = _CACHE[key]
    cm = const_mats()
    wts = {n: np.ascontiguousarray(np.asarray(inp[n], np.float32).reshape(W_SHAPES[n])) for n in W_SHAPES}
    in_maps = []
    L = NOWN * TT
    P_ = max(NPRE, 1) * TT
    for b in range(B):
        for h in range(n_split):
            xo = np.ascontiguousarray(x[b, h * L:(h + 1) * L])
            if h == 0:
                xp = np.zeros((P_, D), np.float32)
                flag = 0.0
            else:
                xp = np.ascontiguousarray(x[b, h * L - P_:h * L])
                flag = 1.0
            m = {"xo": xo, "xp": xp, "mem": np.ascontiguousarray(mem[b]), "cpack": pack_consts(inp, flag), "cmats": cm}
            m.update(wts)
            in_maps.append(m)
    res = run_bass_kernel_spmd(nc, in_maps, core_ids=list(range(len(in_maps))))
    out = np.empty((B, S_, D), np.float32)
    i = 0
    for b in range(B):
        for h in range(n_split):
            out[b, h * L:(h + 1) * L] = res.results[i]["y"]
            i += 1
    return out


def kernel(**inputs):
    x = np.asarray(inputs["x"], np.float32)
    mem = np.asarray(inputs["mem"], np.float32)
    return run_cores(inputs, x, mem, NPRE=8, NOWN=8, n_split=2)
```

```python
import numpy as np
import concourse.bass as bass
import concourse.mybir as mybir
from concourse.bass_utils import run_bass_kernel_spmd

F32 = mybir.dt.float32
BF16 = mybir.dt.bfloat16
AF = mybir.ActivationFunctionType
ALU = mybir.AluOpType

_DT_SIZE = {F32: 4, BF16: 2}

D = 2048
KC = 16
TT = 512
NSUB = 4
NH = 8
DFF = 5632
FC = 44
MEM = 256
EPS = 1e-6
SB_BASE = 17408
SB_LIMIT = 224 * 1024


class Reg:
    __slots__ = ("space", "lo", "hi", "ap", "w", "rs", "ov", "name")

    def __init__(self, space, lo, hi, ap, name):
        self.space, self.lo, self.hi, self.ap, self.name = space, lo, hi, ap, name
        self.w = None
        self.rs = {}
        self.ov = [self]


class Prog:
    ENG = ("pe", "act", "dve", "pool", "sp")

    def __init__(self, nc):
        self.nc = nc
        self.stream = {e: [] for e in self.ENG}
        self.cnt = {e: 0 for e in self.ENG}
        self.waited = {e: {} for e in self.ENG}
        self.regs = {"sb": [], "ps": []}
        self.dma_cnt = {}
        self.sb_off = SB_BASE
        self.n_t = 0
        self.phase = ""
        self.pe_log = []

    def tensor_at(self, name, free_shape, dtype, lo, parts=128):
        self.n_t += 1
        n = int(np.prod(free_shape)) * _DT_SIZE[dtype]
        assert lo % 32 == 0 and lo + n <= SB_LIMIT, (name, lo, n)
        t = self.nc.alloc_sbuf_tensor_at("%s_%d" % (name, self.n_t), [parts] + list(free_shape), dtype, offset=lo)
        return t.ap()

    def alloc(self, nbytes):
        lo = (self.sb_off + 63) // 64 * 64
        self.sb_off = lo + nbytes
        assert self.sb_off <= SB_LIMIT, self.sb_off
        return lo

    def reg(self, space, lo, hi, ap, name=""):
        r = Reg(space, lo, hi, ap, name)
        if space in self.regs:
            for o in self.regs[space]:
                if o.lo < hi and lo < o.hi:
                    o.ov.append(r)
                    r.ov.append(o)
            self.regs[space].append(r)
        return r

    def buf(self, name, free_shape, dtype, lo=None):
        n = int(np.prod(free_shape)) * _DT_SIZE[dtype]
        if lo is None:
            lo = self.alloc(n)
        ap = self.tensor_at(name, free_shape, dtype, lo)
        return self.reg("sb", lo, lo + n, ap, name)

    def chunks(self, name, nch, celems, dtype, lo=None):
        cb = celems * _DT_SIZE[dtype]
        if lo is None:
            lo = self.alloc(nch * cb)
        ap = self.tensor_at(name, [nch, celems], dtype, lo)
        regs = [self.reg("sb", lo + i * cb, lo + (i + 1) * cb, ap[:, i, :], "%s%d" % (name, i)) for i in range(nch)]
        return ap, regs

    def _collect(self, eng, reads, writes):
        waits = {}
        wd = self.waited[eng]

        def need(k, v, raw):
            if k == eng and not raw and eng == "pe":
                return
            if wd.get(k, 0) >= v:
                return
            if waits.get(k, 0) < v:
                waits[k] = v

        for r in reads:
            for x in r.ov:
                if x.w is not None:
                    need(x.w[0], x.w[1], True)
        for w in writes:
            for x in w.ov:
                if x.w is not None:
                    need(x.w[0], x.w[1], False)
                for k, v in x.rs.items():
                    need(k, v, False)
        for k, v in waits.items():
            wd[k] = v
        return list(waits.items())

    def op(self, eng, reads, writes, fn):
        waits = self._collect(eng, reads, writes)
        self.cnt[eng] += 1
        c = self.cnt[eng]
        self.stream[eng].append((waits, fn, (eng, 1), self.phase))
        for r in reads:
            if r.rs.get(eng, 0) < c:
                r.rs[eng] = c
        for w in writes:
            w.w = (eng, c)
            w.rs = {}
        return (eng, c)

    def dma(self, q, key, reads, writes, fn):
        waits = self._collect(q, reads, writes)
        prev = self.dma_cnt.get(key, 0)
        if prev > 0 and self.waited[q].get(key, 0) < prev:
            waits = [w for w in waits if w[0] != key] + [(key, prev)]
            self.waited[q][key] = prev
        self.dma_cnt[key] = prev + 16
        c = self.dma_cnt[key]
        self.stream[q].append((waits, fn, (key, 16), self.phase))
        for r in reads:
            if r.rs.get(key, 0) < c:
                r.rs[key] = c
        for w in writes:
            w.w = (key, c)
            w.rs = {}
        return (key, c)

    def wait_all(self, eng, toks):
        waits = []
        for k, v in toks:
            if self.waited[eng].get(k, 0) < v:
                waits.append((k, v))
                self.waited[eng][k] = v
        self.stream[eng].append((waits, None, None, "end"))

    def emit(self):
        nc = self.nc
        keys = set()
        for e in self.ENG:
            for waits, fn, inc, _ph in self.stream[e]:
                for k, _ in waits:
                    keys.add(k)
                if inc is not None:
                    keys.add(inc[0])
        sems = {k: nc.alloc_semaphore("s_" + str(k)) for k in sorted(keys, key=str)}
        engobj = {"pe": "tensor", "act": "scalar", "dve": "vector", "pool": "gpsimd", "sp": "sync"}
        with nc.Block() as block:
            def mk(e):
                def body(eng):
                    cnt = [0]
                    if e == "pe":
                        class _C:
                            def __getattr__(s_, name):
                                f = getattr(eng, name)
                                if name in ("matmul", "transpose"):
                                    def g(*a, **k):
                                        cnt[0] += 1
                                        return f(*a, **k)
                                    return g
                                return f
                        engw = _C()
                    else:
                        engw = eng
                    for waits, fn, inc, ph in self.stream[e]:
                        for k, v in waits:
                            eng.wait_ge(sems[k], v)
                        if fn is not None:
                            c0 = cnt[0]
                            ins = fn(engw)
                            ins.then_inc(sems[inc[0]], inc[1])
                            if e == "pe":
                                self.pe_log.append((ph, cnt[0] - c0))
                return body
            for e in self.ENG:
                if self.stream[e]:
                    getattr(block, engobj[e])(mk(e))


CP_NW = 0
CP_LB = 80
CP_HNW = 96
CP_SCW = 97
CP_FCW = 121
CP_FCB = 253
CP_FLAG = 297
CP_EPS = 298
NCP = 300


def _fm(v, nch):
    return np.ascontiguousarray(np.asarray(v, np.float32).reshape(nch, 128).T)


def pack_consts(inp, flag):
    cp = np.zeros((128, NCP), np.float32)
    for i, n in enumerate(["norm1_w", "norm2_w", "norm3_w", "final_norm_w", "mem_norm_w"]):
        cp[:, CP_NW + 16 * i:CP_NW + 16 * (i + 1)] = _fm(np.asarray(inp[n]).reshape(-1), 16)
    lb = np.asarray(inp["hgrn_lb"], np.float32)
    for r in range(2):
        cp[:, CP_LB + 8 * r:CP_LB + 8 * (r + 1)] = _fm(lb[r], 8)
    cp[:, CP_HNW] = np.asarray(inp["hgrn_norm_w"], np.float32).reshape(-1)
    sc = np.asarray(inp["sconv_w"], np.float32).reshape(3, 1024)
    for j in range(3):
        cp[:, CP_SCW + 8 * j:CP_SCW + 8 * (j + 1)] = _fm(sc[j], 8)
    fw_ = np.asarray(inp["ffn_conv_w"], np.float32).reshape(3, DFF)
    for j in range(3):
        cp[:, CP_FCW + 44 * j:CP_FCW + 44 * (j + 1)] = _fm(fw_[j], 44)
    cp[:, CP_FCB:CP_FCB + 44] = _fm(np.asarray(inp["ffn_conv_b"], np.float32).reshape(-1), 44)
    cp[:, CP_FLAG] = flag
    cp[:, CP_EPS] = EPS
    return cp


def const_mats():
    cm = np.zeros((128, 2, 128), np.float32)
    cm[:, 0, :] = np.eye(128, dtype=np.float32)
    cm[:, 1, :] = np.triu(np.ones((128, 128), np.float32))
    return cm


W_SHAPES = {"w_in": (D, 7168), "w_out": (D, D), "wq": (D, D), "wk": (D, D), "wv": (D, D), "wo": (D, D),
            "w_gate": (D, DFF), "w_up": (D, DFF), "w_down": (DFF, D)}

STATE, WARM, FULL = 0, 1, 2


def tile_specs(mode, dbg=None):
    s = []
    if dbg == 0 and mode == FULL:
        return s
    if mode == STATE:
        s += [("w_in", 1024), ("w_in", 1536), ("w_in", 2048), ("w_in", 2560)]
        return s
    s += [("w_in", 0), ("w_in", 512)]
    s += [("w_in", 3072), ("w_in", 3584)]
    s += [("w_in", 1024), ("w_in", 1536)]
    s += [("w_in", 2048), ("w_in", 2560)]
    for b in range(2):
        s += [("w_in", 5120 + b * 512), ("w_in", 6144 + b * 512), ("w_in", 4096 + b * 512)]
    s += [("w_out", c * 512) for c in range(4)]
    if dbg == 1 and mode == FULL:
        return s
    s += [("wq", c * 512) for c in range(4)]
    s += [("wo", c * 512) for c in range(4)]
    if dbg == 2 and mode == FULL:
        return s
    for b in range(11):
        s.append(("w_gate", b * 512))
        if mode == FULL:
            s.append(("w_up", b * 512))
    if mode == FULL:
        for cg in range(4):
            for r in range(4):
                s.append(("w_down", r, cg))
    return s


def build_program(NPRE, NOWN, DBG=None):
    nc = bass.Bass("TRN2", target_bir_lowering=False)
    p = Prog(nc)

    xo_d = nc.dram_tensor("xo", [NOWN * TT, D], F32, kind="ExternalInput").ap()
    xp_d = nc.dram_tensor("xp", [max(NPRE, 1) * TT, D], F32, kind="ExternalInput").ap()
    mem_d = nc.dram_tensor("mem", [MEM, D], F32, kind="ExternalInput").ap()
    cp_d = nc.dram_tensor("cpack", [128, NCP], F32, kind="ExternalInput").ap()
    cm_d = nc.dram_tensor("cmats", [128, 2, 128], F32, kind="ExternalInput").ap()
    y_d = nc.dram_tensor("y", [NOWN * TT, D], F32, kind="ExternalOutput").ap()
    wf = {n: nc.dram_tensor(n, list(s), F32, kind="ExternalInput").ap() for n, s in W_SHAPES.items()}
    def _nblk(n_, s_):
        return 16 if n_ == "w_down" else s_[1] // 512
    wb = {n: nc.dram_tensor(n + "_bf", [_nblk(n, s), 128, (11 if n == "w_down" else 16) * 512], BF16, kind="Internal").ap()
          for n, s in W_SHAPES.items()}

    xT_ap, xT = p.chunks("xT", KC, TT, F32)
    hT_ap, hT = p.chunks("hT", KC, TT, BF16)
    cpk = p.buf("cpk", [NCP], F32)
    cmt = p.buf("cmt", [2, 128], F32)
    identb = p.buf("identb", [128], BF16)
    maskb = p.buf("maskb", [128], BF16)
    onesb = p.buf("onesb", [128], BF16)
    onesf = p.buf("onesf", [128], F32)
    lbv = p.buf("lbv", [16], F32)
    kT_ap, kT = p.chunks("kT", KC, MEM, BF16)
    vtm_ap, vtm = p.chunks("vtm", 2, D, BF16)
    S = p.buf("S", [NH, 128], F32)
    ucar_ap, ucar = p.chunks("ucar", 8, 2, F32)
    gcar_ap, gcar = p.chunks("gcar", FC, 2, F32)
    NSLOT = 3
    slots = [p.buf("wslot%d" % i, [8192], BF16) for i in range(NSLOT)]
    identf = cmt.ap[:, 0, :]

    def cpc(off, n=1):
        return cpk.ap[:, off:off + n]

    A = p.alloc(0)
    ARENA = SB_LIMIT - A

    def at(off):
        assert off % 64 == 0
        return A + off
    K1 = 1024
    SCR = []
    for i_, base_ in enumerate((0, 77 * K1)):
        SCR.append({n_: p.buf("%s%d" % (n_, i_), [TT], F32, at(base_ + k_ * 2 * K1))
                    for k_, n_ in enumerate(("qs", "omf", "lgf", "Bc", "bp"))})
    QtT_ap, QtT = p.chunks("QtT", NH, TT, BF16, at(10 * K1))
    KtT_ap, KtT = p.chunks("KtT", NH, TT, BF16, at(18 * K1))
    V_ap, V = p.chunks("V", NSUB, 1024, BF16, at(26 * K1))
    dsc = p.buf("dsc", [3, NSUB, NH], F32, at(34 * K1))
    Ktm = p.buf("Ktm", [NH, 128], BF16, at(34 * K1 + 512))
    scm = p.buf("scm", [NH, 128], BF16, at(36 * K1 + 512))
    smid = p.buf("smid", [NH, 128], BF16, at(38 * K1 + 512))
    sqo = p.buf("sqo", [NH * 128], BF16, at(40 * K1 + 512))
    rto = p.buf("rto", [NH * 128], F32, at(42 * K1 + 512))
    sgc = [p.buf("sgc%d" % i, [TT], BF16, at(46 * K1 + 512 + i * K1)) for i in range(2)]
    ccs = [p.buf("ccs%d" % i, [TT], F32, at(77 * K1 + i * 2 * K1)) for i in range(4)]
    t0s4 = [p.buf("t0s%d" % i, [TT], F32, at(i * 2 * K1)) for i in range(4)]
    ubuf = [p.buf("ubuf%d" % i, [TT + 2], F32, at(50 * K1 + 512 + i * 2112)) for i in range(2)]
    off = 57 * K1
    sqn = [p.buf("sqn%d" % i, [TT], BF16, at(off + i * K1)) for i in range(2)]
    rstdn = p.buf("rstdn", [TT], F32, at(off + 2 * K1))
    off = 61 * K1
    mix_ap, mix = p.chunks("mix", KC, TT, BF16, at(off))
    assert off + 16 * K1 <= ARENA, (off, ARENA)
    xin = [p.buf("xin%d" % i, [1024], F32, at(i * 4 * K1)) for i in range(4)]
    ost = [p.buf("ost%d" % i, [1024], F32, at(16 * K1 + i * 4 * K1)) for i in range(2)]
    oaT_ap, oaT = p.chunks("oaT", KC, TT, BF16, at(24 * K1))
    pT = [[p.buf("pT%d_%d" % (i, m), [TT], BF16, at(40 * K1 + (2 * i + m) * K1)) for m in range(2)] for i in range(2)]
    rs_ = [p.buf("rs%d" % i, [TT], F32, at(44 * K1 + i * 2 * K1)) for i in range(2)]
    hid_ap, hid = p.chunks("hid", FC, TT, BF16, at(12 * K1))
    gbuf = [p.buf("gbuf%d" % i, [TT + 2], F32, at(77 * K1 + i * 2112)) for i in range(2)]
    tb = [p.buf("tb%d" % i, [TT], F32, at(77 * K1 + 2 * 2112 + i * 2 * K1)) for i in range(2)]
    assert 77 * K1 + 2 * 2112 + 4 * K1 <= ARENA, ARENA

    ps_t = nc.alloc_psum_tensor("ps", [128, 4096], F32).ap()
    ps_tb = ps_t.bitcast(BF16)
    psr = [p.reg("ps", i * 2048, (i + 1) * 2048, ps_t[:, i * 512:(i + 1) * 512], "ps%d" % i) for i in range(8)]
    ps_ptr = [0]

    def psa(n=1):
        b = ps_ptr[0]
        if b % n:
            b += n - b % n
        if b + n > 6:
            b = 0
        ps_ptr[0] = (b + n) % 6
        return b, psr[b:b + n]

    def psf(b, n=1):
        return ps_t[:, b * 512:(b + n) * 512]

    wregs = {}

    def spec_aps(spec):
        if spec[0] == "w_down":
            _, r, cg = spec
            sl = (slice(r * 1408, (r + 1) * 1408), slice(cg * 512, (cg + 1) * 512))
            kk = 11
        else:
            n, c0 = spec
            sl = (slice(None), slice(c0, c0 + 512))
            kk = 16
        name = spec[0]
        blk = (spec[2] * 4 + spec[1]) if name == "w_down" else spec[1] // 512
        return wf[name][sl], wb[name][blk], kk

    modes = [STATE] * max(NPRE - 1, 0) + ([WARM] if NPRE > 0 else []) + [FULL] * NOWN
    NSTATE = max(NPRE - 1, 0)
    full_seq = []
    for ti_, m in enumerate(modes):
        if ti_ == NSTATE:
            full_seq += [("wk", c * 512) for c in range(4)] + [("wv", c * 512) for c in range(4)]
        full_seq += tile_specs(m, DBG)
    ci = 0
    for spec in full_seq:
        if spec in wregs:
            continue
        r = p.reg("dram", 0, 0, None, str(spec))
        wregs[spec] = r
        src, dst, kk_ = spec_aps(spec)
        srcv_ = src.rearrange("(k p) n -> p k n", p=128)
        dstv_ = dst.rearrange("p (k n) -> p k n", k=kk_)
        p.dma("pool", "cast%d" % (ci % 8), [], [r], lambda e, srcv_=srcv_, dstv_=dstv_: e.dma_start(out=dstv_, in_=srcv_))
        ci += 1

    ws = {"next_dma": 0, "next_acq": 0}

    def ws_issue():
        i = ws["next_dma"]
        if i >= len(full_seq):
            return
        spec = full_seq[i]
        _, src, kk = spec_aps(spec)
        slot = slots[i % NSLOT]
        dst = slot.ap[:, 0:kk * 512]
        p.dma("sp", "wslot%d" % (i % NSLOT), [wregs[spec]], [slot], lambda e, dst=dst, srcv=src: e.dma_start(out=dst, in_=srcv))
        ws["next_dma"] = i + 1

    def ws_acquire(spec):
        i = ws["next_acq"]
        if ws["next_dma"] == 0:
            for _ in range(NSLOT):
                ws_issue()
        assert full_seq[i] == spec, (i, full_seq[i], spec)
        ws["next_acq"] = i + 1
        assert ws["next_dma"] > i
        slot = slots[i % NSLOT]
        kk = 11 if spec[0] == "w_down" else 16
        return slot, slot.ap[:, 0:kk * 512].rearrange("p (k n) -> p k n", k=kk)

    def ws_release():
        ws_issue()

    p.dma("act", "ld_cp", [], [cpk], lambda e: e.dma_start(out=cpk.ap, in_=cp_d))
    p.dma("act", "ld_cm", [], [cmt], lambda e: e.dma_start(out=cmt.ap, in_=cm_d))
    p.op("dve", [cmt], [identb], lambda e: e.tensor_copy(identb.ap, cmt.ap[:, 0, :]))
    p.op("dve", [cmt], [maskb], lambda e: e.tensor_copy(maskb.ap, cmt.ap[:, 1, :]))
    p.op("dve", [], [onesb], lambda e: e.memset(onesb.ap, 1.0))
    p.op("dve", [], [onesf], lambda e: e.memset(onesf.ap, 1.0))
    p.op("dve", [], [S], lambda e: e.memset(S.ap, 0.0))
    p.op("dve", [], ucar, lambda e: e.memset(ucar_ap, 0.0))
    p.op("dve", [], gcar, lambda e: e.memset(gcar_ap, 0.0))
    p.op("dve", [cpk], [lbv], lambda e: e.tensor_tensor(lbv.ap[:, 0:8], cpc(CP_LB, 8), cpc(CP_LB + 8, 8), ALU.subtract))
    p.op("act", [lbv], [lbv], lambda e: e.activation(lbv.ap[:, 0:8], lbv.ap[:, 0:8], AF.Sigmoid))
    p.op("dve", [lbv], [lbv], lambda e: e.tensor_scalar(lbv.ap[:, 8:16], lbv.ap[:, 0:8], -1.0, 1.0, ALU.mult, ALU.add))

    eps_ap = cpc(CP_EPS)
    flip = {"a": 0}

    def evac_eng():
        flip["a"] ^= 1
        return "act" if flip["a"] else "dve"

    def copy_op(eng, reads, writes, out_ap, in_ap):
        if eng == "act":
            p.op("act", reads, writes, lambda e: e.copy(out_ap, in_ap))
        else:
            p.op(eng, reads, writes, lambda e: e.tensor_copy(out_ap, in_ap))

    pref = {"n": 0}

    def issue_x(src_d, row0, i):
        sub, half = divmod(i, 2)
        sl = xin[i % 4]
        srcv = src_d[row0 + sub * 128:row0 + (sub + 1) * 128, half * 1024:(half + 1) * 1024]
        p.dma("sp", "xin%d" % (i % 4), [], [sl], lambda e, sl=sl, srcv=srcv: e.dma_start(out=sl.ap, in_=srcv))

    def prefetch_x(nxt):
        if nxt is None:
            return
        for i in range(3):
            issue_x(nxt[0], nxt[1], i)
        pref["n"] = 3

    def load_x(src_d, row0, nsub):
        issued = pref["n"]
        pref["n"] = 0
        n = nsub * 2
        for i in range(n):
            while issued < n and issued < i + 4:
                issue_x(src_d, row0, issued)
                issued += 1
            sub, half = divmod(i, 2)
            sl = xin[i % 4]
            b, pr = psa(2)
            def tr(e, sl=sl, b=b):
                last = None
                for j in range(8):
                    last = e.transpose(ps_t[:, b * 512 + j * 128:b * 512 + (j + 1) * 128], sl.ap[:, j * 128:(j + 1) * 128], identf)
                return last
            p.op("pe", [sl, cmt], pr, tr)
            outv = xT_ap[:, half * 8:(half + 1) * 8, sub * 128:(sub + 1) * 128]
            inv = psf(b, 2).rearrange("p (j t) -> p j t", j=8)
            copy_op(evac_eng(), pr, xT[half * 8:(half + 1) * 8], outv, inv)

    xnb = [p.buf("xnb%d" % i, [1024], BF16, at(61 * K1 + i * 2 * K1)) for i in range(2)]
    sjunk = p.buf("sjunk", [1024], BF16, at(65 * K1))
    acc2 = p.buf("acc2", [4], F32, at(67 * K1))

    def load_norm_state(src_d, row0, want_xT=False):
        issued = pref["n"]
        pref["n"] = 0
        n_ = NSUB * 2
        for sub in range(NSUB):
            while issued < n_ and issued < 2 * sub + 4:
                issue_x(src_d, row0, issued)
                issued += 1
            s0, s1 = xin[(sub * 2) % 4], xin[(sub * 2 + 1) % 4]
            for half, sl in enumerate((s0, s1)):
                p.op("act", [sl], [sjunk, acc2], lambda e, sl=sl, half=half: e.activation(sjunk.ap, sl.ap, AF.Square, accum_out=acc2.ap[:, half:half + 1]))
            p.op("dve", [acc2], [acc2], lambda e: e.tensor_tensor(acc2.ap[:, 2:3], acc2.ap[:, 0:1], acc2.ap[:, 1:2], ALU.add))
            p.op("act", [acc2, cpk], [acc2], lambda e: e.activation(acc2.ap[:, 3:4], acc2.ap[:, 2:3], AF.Ln, bias=eps_ap, scale=1.0 / D))
            p.op("act", [acc2], [acc2], lambda e: e.activation(acc2.ap[:, 3:4], acc2.ap[:, 3:4], AF.Exp, scale=-0.5))
            for half, sl in enumerate((s0, s1)):
                xb = xnb[half]
                p.op("dve", [sl, acc2], [xb], lambda e, sl=sl, xb=xb: e.tensor_scalar(xb.ap, sl.ap, acc2.ap[:, 3:4], None, ALU.mult))
                bt, prt = psa(1)
                def tr(e, xb=xb, bt=bt):
                    last = None
                    for j in range(8):
                        last = e.transpose(ps_tb[:, bt * 1024 + j * 128:bt * 1024 + (j + 1) * 128], xb.ap[:, j * 128:(j + 1) * 128], identb.ap)
                    return last
                p.op("pe", [xb, identb], prt, tr)
                outv = hT_ap[:, half * 8:(half + 1) * 8, sub * 128:(sub + 1) * 128]
                inv = ps_tb[:, bt * 1024:(bt + 1) * 1024].rearrange("p (j t) -> p j t", j=8)
                wbc = cpc(CP_NW + half * 8, 8).unsqueeze(2).broadcast_to([128, 8, 128])
                p.op("dve", prt + [cpk], hT[half * 8:(half + 1) * 8], lambda e, outv=outv, inv=inv, wbc=wbc: e.tensor_tensor(outv, inv, wbc, ALU.mult))
                if want_xT:
                    b2, pr2 = psa(2)
                    def tr32(e, sl=sl, b2=b2):
                        last = None
                        for j in range(8):
                            last = e.transpose(ps_t[:, b2 * 512 + j * 128:b2 * 512 + (j + 1) * 128], sl.ap[:, j * 128:(j + 1) * 128], identf)
                        return last
                    p.op("pe", [sl, cmt], pr2, tr32)
                    outx = xT_ap[:, half * 8:(half + 1) * 8, sub * 128:(sub + 1) * 128]
                    inx = psf(b2, 2).rearrange("p (j t) -> p j t", j=8)
                    copy_op(evac_eng(), pr2, xT[half * 8:(half + 1) * 8], outx, inx)

    stat_ptr = [0]

    def norm_stats_begin():
        b = 6 + stat_ptr[0] % 2
        stat_ptr[0] += 1
        return {"b": b, "pr": [psr[b]], "n": 0}

    def norm_stats_add(st, kc, ntok, defer=False, c0=0):
        sq = sqn[st["n"] % 2]
        p.op("act", [xT[kc]], [sq], lambda e: e.activation(sq.ap[:, 0:ntok], xT_ap[:, kc, c0:c0 + ntok], AF.Square))
        first = st["n"] == 0
        last = st["n"] == KC - 1
        b = st["b"]
        def mm():
            p.op("pe", [sq, onesb], st["pr"], lambda e: e.matmul(ps_t[:, b * 512:b * 512 + ntok], onesb.ap, sq.ap[:, 0:ntok], start=first, stop=last))
        st["n"] += 1
        if not defer:
            mm()
            return
        prev = st.get("pend")
        st["pend"] = mm
        if prev is not None:
            prev()

    def norm_stats_flush(st):
        prev = st.get("pend")
        if prev is not None:
            prev()
            st["pend"] = None

    def norm_apply(st, widx, ntok, out_ap, out_regs, c0=0):
        norm_stats_flush(st)
        b = st["b"]
        p.op("act", st["pr"] + [cpk], [rstdn], lambda e: e.activation(rstdn.ap[:, 0:ntok], ps_t[:, b * 512:b * 512 + ntok], AF.Ln, bias=eps_ap, scale=1.0 / D))
        p.op("act", [rstdn], [rstdn], lambda e: e.activation(rstdn.ap[:, 0:ntok], rstdn.ap[:, 0:ntok], AF.Exp, scale=-0.5))
        for kc in range(KC):
            p.op("dve", [xT[kc], rstdn, cpk], [out_regs[kc]],
                 lambda e, kc=kc: e.scalar_tensor_tensor(out_ap[:, kc, c0:c0 + ntok], xT_ap[:, kc, c0:c0 + ntok], cpc(CP_NW + 16 * widx + kc), rstdn.ap[:, 0:ntok], ALU.mult, ALU.mult))

    def norm_full(widx, ntok, out_ap, out_regs):
        st = norm_stats_begin()
        for kc in range(KC):
            norm_stats_add(st, kc, ntok)
        norm_apply(st, widx, ntok, out_ap, out_regs)

    def proj_fm(slot, sv, j, rhs_ap, rhs_regs, ntok, nk=KC, fine=False, c0=0):
        b, pr = psa(1)
        if fine:
            for kc in range(nk):
                p.op("pe", [slot, rhs_regs[kc]], pr,
                     lambda e, kc=kc: e.matmul(ps_t[:, b * 512:b * 512 + ntok], sv[:, kc, j * 128:(j + 1) * 128], rhs_ap[:, kc, c0:c0 + ntok], start=(kc == 0), stop=(kc == nk - 1)))
            return b, pr
        def mm(e):
            last = None
            for kc in range(nk):
                last = e.matmul(ps_t[:, b * 512:b * 512 + ntok], sv[:, kc, j * 128:(j + 1) * 128], rhs_ap[:, kc, c0:c0 + ntok], start=(kc == 0), stop=(kc == nk - 1))
            return last
        p.op("pe", [slot] + list(rhs_regs), pr, mm)
        return b, pr

    def hgrn_prep_head(h, slot_f, sv_f, j, full, sc, fine=False, qcs=slice(0, TT)):
        omf, lgf, Bc, bp = sc["omf"], sc["lgf"], sc["Bc"], sc["bp"]
        bf_, prf = proj_fm(slot_f, sv_f, j, hT_ap, hT, TT, fine=fine)
        p.op("act", prf, [omf], lambda e: e.activation(omf.ap, psf(bf_), AF.Exp))
        yield
        p.op("act", [omf], [omf], lambda e: e.activation(omf.ap, omf.ap, AF.Ln, bias=1.0))
        yield
        p.op("act", [omf], [omf], lambda e: e.activation(omf.ap, omf.ap, AF.Exp, scale=-1.0))
        yield
        p.op("dve", [omf, lbv], [omf], lambda e: e.tensor_scalar(omf.ap, omf.ap, lbv.ap[:, 8 + h:9 + h], None, ALU.mult))
        yield
        p.op("act", [omf], [lgf], lambda e: e.activation(lgf.ap, omf.ap, AF.Ln, bias=1.0, scale=-1.0))
        yield
        def scan(e):
            last = None
            for s_ in range(NSUB):
                sl = slice(s_ * 128, (s_ + 1) * 128)
                last = e.tensor_tensor_scan(Bc.ap[:, sl], onesf.ap, lgf.ap[:, sl], 0.0, ALU.mult, ALU.add)
            return last
        p.op("dve", [lgf, onesf], [Bc], scan)
        yield
        B3 = Bc.ap.rearrange("p (s t) -> p s t", s=NSUB)
        bp3 = bp.ap.rearrange("p (s t) -> p s t", s=NSUB)
        p.op("dve", [Bc], [bp], lambda e: e.tensor_tensor(bp3, B3, B3[:, :, 63:64].broadcast_to([128, NSUB, 128]), ALU.subtract))
        yield
        p.op("act", [Bc], [dsc], lambda e: e.activation(dsc.ap[:, 0, :, h], B3[:, :, 127], AF.Exp))
        p.op("act", [Bc], [dsc], lambda e: e.activation(dsc.ap[:, 2, :, h], B3[:, :, 63], AF.Exp))
        p.op("act", [bp], [dsc], lambda e: e.activation(dsc.ap[:, 1, :, h], bp3[:, :, 127], AF.Exp))
        yield
        if full:
            p.op("act", [bp], [lgf], lambda e: e.activation(lgf.ap, bp.ap, AF.Exp))
        p.op("act", [bp], [Bc], lambda e: e.activation(Bc.ap, bp.ap, AF.Exp, scale=-1.0))
        yield
        if full:
            p.op("dve", [QtT[h], lgf], [QtT[h]], lambda e: e.tensor_tensor(QtT_ap[:, h, qcs], QtT_ap[:, h, qcs], lgf.ap[:, qcs], ALU.mult))
        p.op("dve", [omf, Bc], [KtT[h]], lambda e: e.tensor_tensor(KtT_ap[:, h, :], omf.ap, Bc.ap, ALU.mult))
        yield

    def run_interleaved(gens):
        act_ = list(gens)
        while act_:
            for g_ in list(act_):
                try:
                    next(g_)
                except StopIteration:
                    act_.remove(g_)

    def v_proj(slot, sv, blk):
        for sub in range(NSUB):
            b, pr = psa(1)
            def mm(e, b=b, sub=sub):
                last = None
                for kc in range(KC):
                    last = e.matmul(psf(b), hT_ap[:, kc, sub * 128:(sub + 1) * 128], sv[:, kc, :], start=(kc == 0), stop=(kc == KC - 1))
                return last
            p.op("pe", [slot] + hT, pr, mm)
            copy_op(evac_eng(), pr, [V[sub]], V_ap[:, sub, blk * 512:(blk + 1) * 512], psf(b))

    def hgrn_state_tr(sub, kbuf):
        cols = slice(sub * 128, (sub + 1) * 128)
        bt, prt = psa(1)
        def trk(e):
            last = None
            for h in range(NH):
                last = e.transpose(ps_tb[:, bt * 1024 + h * 128:bt * 1024 + (h + 1) * 128], KtT_ap[:, h, cols], identb.ap)
            return last
        p.op("pe", KtT + [identb], prt, trk)
        copy_op(evac_eng(), prt, [kbuf], kbuf.ap.rearrange("p h k -> p (h k)"), ps_tb[:, bt * 1024:(bt + 1) * 1024])

    def hgrn_state_upd(sub, kbuf):
        bp_, prp = psa(2)
        def pm(e):
            last = None
            for h in range(NH):
                last = e.matmul(ps_t[:, bp_ * 512 + h * 128:bp_ * 512 + (h + 1) * 128], kbuf.ap[:, h, :], V_ap[:, sub, h * 128:(h + 1) * 128], start=True, stop=True)
            return last
        p.op("pe", [kbuf, V[sub]], prp, pm)
        p.op("dve", [S, dsc], [S], lambda e: e.tensor_tensor(S.ap, S.ap, dsc.ap[:, 0, sub, :].unsqueeze(2).broadcast_to([128, NH, 128]), ALU.mult))
        def su(e):
            last = None
            for h in range(NH):
                last = e.scalar_tensor_tensor(S.ap[:, h, :], ps_t[:, bp_ * 512 + h * 128:bp_ * 512 + (h + 1) * 128], dsc.ap[:, 1, sub, h:h + 1], S.ap[:, h, :], ALU.mult, ALU.add)
            return last
        p.op("dve", prp + [S, dsc], [S], su)

    def hgrn_subtile(sub, full):
        cols = slice(sub * 128, (sub + 1) * 128)
        bt, prt = psa(1)
        def trk(e):
            last = None
            for h in range(NH):
                last = e.transpose(ps_tb[:, bt * 1024 + h * 128:bt * 1024 + (h + 1) * 128], KtT_ap[:, h, cols], identb.ap)
            return last
        p.op("pe", KtT + [identb], prt, trk)
        copy_op(evac_eng(), prt, [Ktm], Ktm.ap.rearrange("p h k -> p (h k)"), ps_tb[:, bt * 1024:(bt + 1) * 1024])
        if full:
            p.op("dve", [S, dsc], [smid], lambda e: e.tensor_tensor(smid.ap, S.ap, dsc.ap[:, 2, sub, :].unsqueeze(2).broadcast_to([128, NH, 128]), ALU.mult))
            bs, prs = psa(2)
            def sc(e):
                last = None
                for h in range(NH):
                    last = e.matmul(ps_t[:, bs * 512 + h * 128:bs * 512 + (h + 1) * 128], KtT_ap[:, h, cols], QtT_ap[:, h, cols], start=True, stop=True)
                return last
            p.op("pe", KtT + QtT, prs, sc)
            p.op("dve", prs + [maskb], [scm], lambda e: e.tensor_tensor(scm.ap, psf(bs, 2).rearrange("p (h t) -> p h t", h=NH), maskb.ap.unsqueeze(1).broadcast_to([128, NH, 128]), ALU.mult))
            bo, pro = psa(2)
            def om(e):
                last = None
                for h in range(NH):
                    o_ = ps_t[:, bo * 512 + h * 128:bo * 512 + (h + 1) * 128]
                    e.matmul(o_, V_ap[:, sub, h * 128:(h + 1) * 128], scm.ap[:, h, :], start=True, stop=False)
                    last = e.matmul(o_, smid.ap[:, h, :], QtT_ap[:, h, cols], start=False, stop=True)
                return last
            p.op("pe", [V[sub], scm, smid] + QtT, pro, om)
        bp_, prp = psa(2)
        def pm(e):
            last = None
            for h in range(NH):
                last = e.matmul(ps_t[:, bp_ * 512 + h * 128:bp_ * 512 + (h + 1) * 128], Ktm.ap[:, h, :], V_ap[:, sub, h * 128:(h + 1) * 128], start=True, stop=True)
            return last
        p.op("pe", [Ktm, V[sub]], prp, pm)
        p.op("dve", [S, dsc], [S], lambda e: e.tensor_tensor(S.ap, S.ap, dsc.ap[:, 0, sub, :].unsqueeze(2).broadcast_to([128, NH, 128]), ALU.mult))
        def su(e):
            last = None
            for h in range(NH):
                last = e.scalar_tensor_tensor(S.ap[:, h, :], ps_t[:, bp_ * 512 + h * 128:bp_ * 512 + (h + 1) * 128], dsc.ap[:, 1, sub, h:h + 1], S.ap[:, h, :], ALU.mult, ALU.add)
            return last
        p.op("dve", prp + [S, dsc], [S], su)
        if full:
            p.op("act", pro, [sqo], lambda e: e.activation(sqo.ap, psf(bo, 2), AF.Square))
            bq, prq = psa(2)
            def ssm(e):
                e.matmul(psf(bq), onesb.ap, sqo.ap[:, 0:512], start=True, stop=True)
                return e.matmul(psf(bq + 1), onesb.ap, sqo.ap[:, 512:1024], start=True, stop=True)
            p.op("pe", [sqo, onesb], prq, ssm)
            p.op("act", prq + [cpk], [rto], lambda e: e.activation(rto.ap, psf(bq, 2), AF.Ln, bias=eps_ap, scale=1.0 / 128))
            p.op("act", [rto], [rto], lambda e: e.activation(rto.ap, rto.ap, AF.Exp, scale=-0.5))
            p.op("dve", pro + [rto], [sqo], lambda e: e.tensor_tensor(sqo.ap, psf(bo, 2), rto.ap, ALU.mult))
            p.op("dve", [sqo, cpk] + mix[0:NH], mix[0:NH],
                 lambda e: e.scalar_tensor_tensor(mix_ap[:, 0:NH, cols], sqo.ap.rearrange("p (h t) -> p h t", h=NH), cpc(CP_HNW), mix_ap[:, 0:NH, cols], ALU.mult, ALU.mult))

    out_toks = []
    ost_ptr = [0]

    tile_no = [0]

    def emit_output(out_row0):
        for sub in range(NSUB):
            for half in range(2):
                b, pr = psa(2)
                def tro(e, b=b, sub=sub, half=half):
                    last = None
                    for j in range(8):
                        last = e.transpose(ps_t[:, b * 512 + j * 128:b * 512 + (j + 1) * 128], xT_ap[:, half * 8 + j, sub * 128:(sub + 1) * 128], identf)
                    return last
                p.op("pe", xT[half * 8:(half + 1) * 8] + [cmt], pr, tro)
                os_ = ost[ost_ptr[0] % 2]
                ost_ptr[0] += 1
                copy_op(evac_eng(), pr, [os_], os_.ap, psf(b, 2))
                dstv = y_d[out_row0 + sub * 128:out_row0 + (sub + 1) * 128, half * 1024:(half + 1) * 1024]
                out_toks.append(p.dma("act", "ost%d" % ((ost_ptr[0] - 1) % 2), [os_], [], lambda e, os_=os_, dstv=dstv: e.dma_start(out=dstv, in_=os_.ap)))

    def do_tile(src_d, row0, mode, out_row0, nxt=None):
        full = mode != STATE
        c0, n = (0, TT) if mode != WARM else (TT - 128, 128)
        cs = slice(c0, c0 + n)
        tn = tile_no[0]
        tile_no[0] += 1
        p.phase = "t%d.load" % tn
        load_norm_state(src_d, row0, want_xT=(mode != STATE))
        if DBG == 0 and mode == FULL:
            emit_output(out_row0)
            return
        p.phase = "t%d.prep" % tn
        if full:
            for hg in range(2):
                slot_q, sv_q = ws_acquire(("w_in", hg * 512))
                for j in range(4):
                    h = hg * 4 + j
                    b, pr = proj_fm(slot_q, sv_q, j, hT_ap, hT, n, fine=(h == 0), c0=c0)
                    p.op("act", pr, [QtT[h]], lambda e, b=b, h=h: e.activation(QtT_ap[:, h, cs], ps_t[:, b * 512:b * 512 + n], AF.Silu))
                ws_release()
            for hg in range(2):
                slot_g, sv_g = ws_acquire(("w_in", 3072 + hg * 512))
                for j in range(4):
                    h = hg * 4 + j
                    b, pr = proj_fm(slot_g, sv_g, j, hT_ap, hT, n, c0=c0)
                    p.op("act", pr, [mix[h]], lambda e, b=b, h=h: e.activation(mix_ap[:, h, cs], ps_t[:, b * 512:b * 512 + n], AF.Silu))
                ws_release()
        for hg in range(2):
            slot_f, sv_f = ws_acquire(("w_in", 1024 + hg * 512))
            for jj in range(0, 4, 2):
                run_interleaved([hgrn_prep_head(hg * 4 + j, slot_f, sv_f, j, full, SCR[j % 2], fine=(not full and hg == 0 and j == 0), qcs=cs)
                                 for j in (jj, jj + 1)])
            ws_release()
        p.phase = "t%d.vproj" % tn
        for blk in range(2):
            slot, sv = ws_acquire(("w_in", 2048 + blk * 512))
            v_proj(slot, sv, blk)
            ws_release()
        p.phase = "t%d.subtiles" % tn
        if mode == STATE:
            kb = [Ktm, scm]
            hgrn_state_tr(0, kb[0])
            for sub in range(NSUB):
                if sub + 1 < NSUB:
                    hgrn_state_tr(sub + 1, kb[(sub + 1) % 2])
                hgrn_state_upd(sub, kb[sub % 2])
        else:
            for sub in range(NSUB):
                hgrn_subtile(sub, mode == FULL or (mode == WARM and sub == NSUB - 1))
        if not full:
            prefetch_x(nxt)
            return
        p.phase = "t%d.sconv" % tn
        for blk in range(2):
            slot_c, sv_c = ws_acquire(("w_in", 5120 + blk * 512))
            for j in range(4):
                bc_, prc = proj_fm(slot_c, sv_c, j, hT_ap, hT, n, c0=c0)
                p.op("act", prc, [ccs[j]], lambda e, bc_=bc_, j=j: e.copy(ccs[j].ap[:, 0:n], ps_t[:, bc_ * 512:bc_ * 512 + n]))
            ws_release()
            slot_h, sv_h = ws_acquire(("w_in", 6144 + blk * 512))
            for j in range(4):
                c = blk * 4 + j
                bh, prh = proj_fm(slot_h, sv_h, j, hT_ap, hT, n, c0=c0)
                ub = ubuf[c % 2]
                t0s = t0s4[j]
                p.op("dve", [ucar[c]], [ub], lambda e, ub=ub, c=c: e.tensor_copy(ub.ap[:, 0:2], ucar_ap[:, c, :]))
                p.op("dve", prh + [ccs[j]], [ub], lambda e, ub=ub, bh=bh, j=j: e.tensor_tensor(ub.ap[:, 2:n + 2], ps_t[:, bh * 512:bh * 512 + n], ccs[j].ap[:, 0:n], ALU.mult))
                p.op("dve", [ub], [ucar[c]], lambda e, ub=ub, c=c: e.tensor_copy(ucar_ap[:, c, :], ub.ap[:, n:n + 2]))
                p.op("act", [ub, cpk], [t0s], lambda e, ub=ub, c=c, t0s=t0s: e.activation(t0s.ap[:, 0:n], ub.ap[:, 2:n + 2], AF.Identity, scale=cpc(CP_SCW + 16 + c)))
                p.op("dve", [ub, t0s, cpk], [t0s], lambda e, ub=ub, c=c, t0s=t0s: e.scalar_tensor_tensor(t0s.ap[:, 0:n], ub.ap[:, 1:n + 1], cpc(CP_SCW + 8 + c), t0s.ap[:, 0:n], ALU.mult, ALU.add))
                p.op("dve", [ub, t0s, cpk], [t0s], lambda e, ub=ub, c=c, t0s=t0s: e.scalar_tensor_tensor(t0s.ap[:, 0:n], ub.ap[:, 0:n], cpc(CP_SCW + c), t0s.ap[:, 0:n], ALU.mult, ALU.add))
            ws_release()
            slot_b, sv_b = ws_acquire(("w_in", 4096 + blk * 512))
            for j in range(4):
                c = blk * 4 + j
                bb, prb = proj_fm(slot_b, sv_b, j, hT_ap, hT, n, c0=c0)
                p.op("dve", prb + [t0s4[j]], [mix[8 + c]], lambda e, bb=bb, c=c, j=j: e.tensor_tensor(mix_ap[:, 8 + c, cs], ps_t[:, bb * 512:bb * 512 + n], t0s4[j].ap[:, 0:n], ALU.mult))
            ws_release()

        def resid_proj(wname, act_ap, act_regs, st):
            for c in range(4):
                slot, sv = ws_acquire((wname, c * 512))
                for j in range(4):
                    ch = c * 4 + j
                    b, pr = proj_fm(slot, sv, j, act_ap, act_regs, n, c0=c0)
                    p.op("dve", pr + [xT[ch]], [xT[ch]], lambda e, b=b, ch=ch: e.tensor_tensor(xT_ap[:, ch, cs], ps_t[:, b * 512:b * 512 + n], xT_ap[:, ch, cs], ALU.add))
                    norm_stats_add(st, ch, n, defer=True, c0=c0)
                ws_release()

        p.phase = "t%d.wout" % tn
        st = norm_stats_begin()
        resid_proj("w_out", mix_ap, mix, st)
        if DBG == 1 and mode == FULL:
            emit_output(out_row0)
            return
        norm_apply(st, 1, n, hT_ap, hT, c0=c0)
        p.phase = "t%d.wq" % tn
        qT_ap, qT = mix_ap, mix
        for c in range(4):
            slot, sv = ws_acquire(("wq", c * 512))
            for j in range(4):
                ch = c * 4 + j
                b, pr = proj_fm(slot, sv, j, hT_ap, hT, n, fine=(ch == 0), c0=c0)
                p.op("act", pr, [qT[ch]], lambda e, b=b, ch=ch: e.activation(qT_ap[:, ch, cs], ps_t[:, b * 512:b * 512 + n], AF.Identity, scale=float(512 ** -0.5)))
            ws_release()
        def att_scores(hd):
            pts = pT[hd % 2]
            for mc in range(2):
                b, pr = psa(1)
                def smm(e, b=b, mc=mc, hd=hd):
                    last = None
                    for dc in range(4):
                        kc = hd * 4 + dc
                        last = e.matmul(ps_t[:, b * 512:b * 512 + n], kT_ap[:, kc, mc * 128:(mc + 1) * 128], qT_ap[:, kc, cs], start=(dc == 0), stop=(dc == 3))
                    return last
                p.op("pe", kT[hd * 4:hd * 4 + 4] + qT[hd * 4:hd * 4 + 4], pr, smm)
                p.op("act", pr, [pts[mc]], lambda e, b=b, mc=mc, pts=pts: e.activation(pts[mc].ap[:, 0:n], ps_t[:, b * 512:b * 512 + n], AF.Exp))

        def att_rest(hd):
            pts = pT[hd % 2]
            b, pr = psa(1)
            def summ(e, b=b, pts=pts):
                e.matmul(ps_t[:, b * 512:b * 512 + n], onesb.ap, pts[0].ap[:, 0:n], start=True, stop=False)
                return e.matmul(ps_t[:, b * 512:b * 512 + n], onesb.ap, pts[1].ap[:, 0:n], start=False, stop=True)
            p.op("pe", [pts[0], pts[1], onesb], pr, summ)
            rsb = rs_[hd % 2]
            p.op("act", pr, [rsb], lambda e, b=b, rsb=rsb: e.activation(rsb.ap[:, 0:n], ps_t[:, b * 512:b * 512 + n], AF.Ln))
            p.op("act", [rsb], [rsb], lambda e, rsb=rsb: e.activation(rsb.ap[:, 0:n], rsb.ap[:, 0:n], AF.Exp, scale=-1.0))
            for dc in range(4):
                kc = hd * 4 + dc
                b, pr = psa(1)
                def pv(e, b=b, kc=kc, pts=pts):
                    e.matmul(ps_t[:, b * 512:b * 512 + n], vtm_ap[:, 0, kc * 128:(kc + 1) * 128], pts[0].ap[:, 0:n], start=True, stop=False)
                    return e.matmul(ps_t[:, b * 512:b * 512 + n], vtm_ap[:, 1, kc * 128:(kc + 1) * 128], pts[1].ap[:, 0:n], start=False, stop=True)
                p.op("pe", [vtm[0], vtm[1], pts[0], pts[1]], pr, pv)
                p.op("dve", pr + [rsb], [oaT[kc]], lambda e, b=b, kc=kc, rsb=rsb: e.tensor_tensor(oaT_ap[:, kc, cs], ps_t[:, b * 512:b * 512 + n], rsb.ap[:, 0:n], ALU.mult))

        p.phase = "t%d.att" % tn
        att_scores(0)
        for hd in range(4):
            if hd + 1 < 4:
                att_scores(hd + 1)
            att_rest(hd)
        p.phase = "t%d.wo" % tn
        st = norm_stats_begin()
        resid_proj("wo", oaT_ap, oaT, st)
        if DBG == 2 and mode == FULL:
            emit_output(out_row0)
            return
        norm_apply(st, 2, n, hT_ap, hT, c0=c0)
        p.phase = "t%d.ffn" % tn
        prefetch_x(nxt)
        for blk in range(11):
            slot_g, sv_g = ws_acquire(("w_gate", blk * 512))
            if mode == FULL:
                slot_u, sv_u = ws_acquire(("w_up", blk * 512))
            for j in range(4):
                c = blk * 4 + j
                bg, prg = proj_fm(slot_g, sv_g, j, hT_ap, hT, n, fine=(c == 0), c0=c0)
                gb = gbuf[c % 2]
                t = tb[c % 2]
                p.op("dve", [gcar[c]], [gb], lambda e, gb=gb, c=c: e.tensor_copy(gb.ap[:, 0:2], gcar_ap[:, c, :]))
                p.op("act", prg, [gb], lambda e, gb=gb, bg=bg: e.copy(gb.ap[:, 2:n + 2], ps_t[:, bg * 512:bg * 512 + n]))
                p.op("dve", [gb], [gcar[c]], lambda e, gb=gb, c=c: e.tensor_copy(gcar_ap[:, c, :], gb.ap[:, n:n + 2]))
                if mode != FULL:
                    continue
                p.op("act", prg + [cpk], [t], lambda e, t=t, bg=bg, c=c: e.activation(t.ap, psf(bg), AF.Identity, bias=cpc(CP_FCB + c), scale=cpc(CP_FCW + 88 + c)))
                bu, pru = proj_fm(slot_u, sv_u, j, hT_ap, hT, TT)
                p.op("dve", [gb, t, cpk], [t], lambda e, gb=gb, t=t, c=c: e.scalar_tensor_tensor(t.ap, gb.ap[:, 1:TT + 1], cpc(CP_FCW + 44 + c), t.ap, ALU.mult, ALU.add))
                p.op("dve", [gb, t, cpk], [t], lambda e, gb=gb, t=t, c=c: e.scalar_tensor_tensor(t.ap, gb.ap[:, 0:TT], cpc(CP_FCW + c), t.ap, ALU.mult, ALU.add))
                p.op("act", [t], [t], lambda e, t=t: e.activation(t.ap, t.ap, AF.Silu))
                p.op("dve", pru + [t], [hid[c]], lambda e, t=t, bu=bu, c=c: e.tensor_tensor(hid_ap[:, c, :], psf(bu), t.ap, ALU.mult))
            ws_release()
            if mode == FULL:
                ws_release()
        if mode != FULL:
            p.op("dve", gcar + [cpk], gcar, lambda e: e.tensor_scalar(gcar_ap, gcar_ap, cpc(CP_FLAG), None, ALU.mult))
            p.op("dve", ucar + [cpk], ucar, lambda e: e.tensor_scalar(ucar_ap, ucar_ap, cpc(CP_FLAG), None, ALU.mult))
            p.op("dve", [S, cpk], [S], lambda e: e.tensor_scalar(S.ap, S.ap, cpc(CP_FLAG), None, ALU.mult))
            return
        p.phase = "t%d.down" % tn
        st = norm_stats_begin()
        for cg in range(4):
            b4, pr4 = psa(4)
            for r in range(4):
                slot, sv = ws_acquire(("w_down", r, cg))
                for j in range(4):
                    def dm(e, b4=b4, r=r, sv=sv, j=j):
                        last = None
                        for kk in range(11):
                            last = e.matmul(psf(b4 + j), sv[:, kk, j * 128:(j + 1) * 128], hid_ap[:, r * 11 + kk, :], start=(r == 0 and kk == 0), stop=(r == 3 and kk == 10))
                        return last
                    p.op("pe", [slot] + hid[r * 11:(r + 1) * 11], [pr4[j]], dm)
                ws_release()
            for j in range(4):
                ch = cg * 4 + j
                p.op("dve", [pr4[j], xT[ch]], [xT[ch]], lambda e, b4=b4, j=j, ch=ch: e.tensor_tensor(xT_ap[:, ch, :], psf(b4 + j), xT_ap[:, ch, :], ALU.add))
                norm_stats_add(st, ch, TT, defer=True)
        p.phase = "t%d.out" % tn
        if DBG != 3:
            norm_apply(st, 3, TT, xT_ap, xT)
        emit_output(out_row0)

    def mem_kv():
        p.phase = "mem"
        load_x(mem_d, 0, 2)
        norm_full(4, MEM, hT_ap, hT)
        for c in range(4):
            slot, sv = ws_acquire(("wk", c * 512))
            for j in range(4):
                b, pr = proj_fm(slot, sv, j, hT_ap, hT, MEM)
                ch = c * 4 + j
                copy_op(evac_eng(), pr, [kT[ch]], kT_ap[:, ch, :], ps_t[:, b * 512:b * 512 + MEM])
            ws_release()
        for c in range(4):
            slot, sv = ws_acquire(("wv", c * 512))
            for ms in range(2):
                b, pr = psa(1)
                def mm(e, b=b, ms=ms, sv=sv):
                    last = None
                    for kc in range(KC):
                        last = e.matmul(psf(b), hT_ap[:, kc, ms * 128:(ms + 1) * 128], sv[:, kc, :], start=(kc == 0), stop=(kc == KC - 1))
                    return last
                p.op("pe", [slot] + hT, pr, mm)
                copy_op(evac_eng(), pr, [vtm[ms]], vtm_ap[:, ms, c * 512:(c + 1) * 512], psf(b))
            ws_release()


    tiles = [(xp_d, t * TT, modes[t], None) for t in range(NPRE)] + [(xo_d, t * TT, FULL, t * TT) for t in range(NOWN)]
    for ti, (src_, r0_, m_, o_) in enumerate(tiles):
        if ti == NSTATE:
            mem_kv()
        nxt = tiles[ti + 1][0:2] if ti + 1 < len(tiles) else None
        if DBG is not None or ti + 1 == NSTATE:
            nxt = None
        do_tile(src_, r0_, m_, o_, nxt)
    assert ws["next_acq"] == len(full_seq)
    last = {}
    for k, v in out_toks:
        last[k] = max(last.get(k, 0), v)
    p.wait_all("act", list(last.items()))
    p.emit()
    global _LAST_PROG
    _LAST_PROG = p
    return nc


_CACHE = {}
_LAST_PROG = None


def run_cores(inp, x, mem, NPRE, NOWN, n_split, DBG=None):
    B, S_, _ = x.shape
    key = (NPRE, NOWN, DBG)
    if key not in _CACHE:
        _CACHE[key] = build_program(NPRE, NOWN, DBG)
    nc = _CACHE[key]
    cm = const_mats()
    wts = {n: np.ascontiguousarray(np.asarray(inp[n], np.float32).reshape(W_SHAPES[n])) for n in W_SHAPES}
    in_maps = []
    L = NOWN * TT
    P_ = max(NPRE, 1) * TT
    for b in range(B):
        for h in range(n_split):
            xo = np.ascontiguousarray(x[b, h * L:(h + 1) * L])
            if h == 0:
                xp = np.zeros((P_, D), np.float32)
                flag = 0.0
            else:
                xp = np.ascontiguousarray(x[b, h * L - P_:h * L])
                flag = 1.0
            m = {"xo": xo, "xp": xp, "mem": np.ascontiguousarray(mem[b]), "cpack": pack_consts(inp, flag), "cmats": cm}
            m.update(wts)
            in_maps.append(m)
    res = run_bass_kernel_spmd(nc, in_maps, core_ids=list(range(len(in_maps))))
    out = np.empty((B, S_, D), np.float32)
    i = 0
    for b in range(B):
        for h in range(n_split):
            out[b, h * L:(h + 1) * L] = res.results[i]["y"]
            i += 1
    return out


def kernel(**inputs):
    x = np.asarray(inputs["x"], np.float32)
    mem = np.asarray(inputs["mem"], np.float32)
    return run_cores(inputs, x, mem, NPRE=8, NOWN=8, n_split=2)
```
